# Optimizing a Trainium2 kernel written in Bass

```python
import math
import jax, jax.numpy as jnp
from jax import lax
import numpy as np

D_MODEL = 4096
BATCH = 8
SEQ = 2048
DEPTH = 1

MEM_LEN = 256
DIFF_WIDTH = D_MODEL // 2
DN_WIDTH = D_MODEL - DIFF_WIDTH
DIFF_VDIM = 128
DIFF_HEADS = DIFF_WIDTH // DIFF_VDIM
DIFF_QKDIM = DIFF_VDIM // 2
DN_KDIM = 128
DN_VDIM = 128
DN_HEADS = DN_WIDTH // DN_VDIM
DN_CONV = 5
DN_CHUNK = 64
ROPE_THETA = 500000.0
ROPE_DIM = DIFF_QKDIM // 4
ATTN_BLOCK = 128
D_FF = ((8 * D_MODEL // 3 + 255) // 256) * 256
MEM_HEADS = 4
MEM_HDIM = 128
NORM_EPS = 1e-6
SUBLN_EPS = 1e-5
IN_SPLITS = [DIFF_WIDTH, 2 * DIFF_WIDTH, 3 * DIFF_WIDTH,
             3 * DIFF_WIDTH + 3 * DN_WIDTH, 3 * DIFF_WIDTH + 4 * DN_WIDTH]
IN_COLS = 3 * DIFF_WIDTH + 4 * DN_WIDTH + 4 * DN_HEADS

kernel_name = "hybrid_diffattn_gdn_macaron_encoder"

F32 = jnp.float32


def rms_norm(x, w, eps=NORM_EPS):
    xf = x.astype(F32)
    y = xf * lax.rsqrt(jnp.mean(xf * xf, axis=-1, keepdims=True) + eps)
    return (y * w.astype(F32)).astype(x.dtype)


def l2_norm(x, eps=1e-6):
    return x * lax.rsqrt(jnp.sum(x * x, axis=-1, keepdims=True) + eps)


def swiglu(h, w_gu, w_down):
    gate, up = jnp.split(h @ w_gu, 2, axis=-1)
    return (jax.nn.silu(gate) * up) @ w_down


def rope_tables(positions):
    inv_freq = ROPE_THETA ** (-jnp.arange(0, ROPE_DIM, 2, dtype=F32) / ROPE_DIM)
    ang = positions.astype(F32)[..., None] * inv_freq
    return jnp.cos(ang)[:, :, None, :], jnp.sin(ang)[:, :, None, :]


def apply_partial_rope(x, cos, sin):
    half = ROPE_DIM // 2
    xf = x.astype(F32)
    x1, x2, rest = xf[..., :half], xf[..., half:ROPE_DIM], xf[..., ROPE_DIM:]
    out = jnp.concatenate([x1 * cos - x2 * sin, x2 * cos + x1 * sin, rest], axis=-1)
    return out.astype(x.dtype)


def diff_attention(q, k, v, cos, sin, lam_params, subln_w, lambda_init):
    B, S = q.shape[0], q.shape[1]
    H, dq, dv = DIFF_HEADS, DIFF_QKDIM, DIFF_VDIM
    q = apply_partial_rope(q, cos, sin) * (dq ** -0.5)
    k = apply_partial_rope(k, cos, sin)
    lp = lam_params.astype(F32)
    lam = jnp.exp(jnp.sum(lp[0] * lp[1])) - jnp.exp(jnp.sum(lp[2] * lp[3])) + lambda_init
    nb = S // ATTN_BLOCK
    qb = q.reshape(B, nb, ATTN_BLOCK, 2 * H, dq).transpose(1, 0, 3, 2, 4)
    kt = k.transpose(0, 2, 1, 3)
    vt = v.transpose(0, 2, 1, 3)

    def block(qblk):
        s = jnp.einsum('bhqd,bhkd->bhqk', qblk, kt).astype(F32)
        p = jax.nn.softmax(s, axis=-1).reshape(B, H, 2, ATTN_BLOCK, S)
        p = p[:, :, 0] - lam * p[:, :, 1]
        return jnp.einsum('bhqk,bhkd->bhqd', p.astype(vt.dtype), vt)

    o = lax.map(block, qb)
    o = o.transpose(1, 0, 3, 2, 4).reshape(B, S, H, dv)
    o = rms_norm(o, subln_w, SUBLN_EPS) * (1.0 - lambda_init)
    return o.reshape(B, S, H * dv)


def short_conv(x, w):
    pad = (DN_CONV - 1) // 2
    y = lax.conv_general_dilated(x, w.astype(x.dtype)[:, None, :], window_strides=(1,),
                                 padding=[(pad, pad)], dimension_numbers=('NWC', 'WIO', 'NWC'),
                                 feature_group_count=x.shape[-1])
    return jax.nn.silu(y)


def gated_delta_chunked(q, k, v, g, beta):
    B, H, S, dk = q.shape
    dv = v.shape[-1]
    C = DN_CHUNK
    N = S // C
    q = q * (dk ** -0.5)
    qc = q.reshape(B, H, N, C, dk)
    kc = k.reshape(B, H, N, C, dk)
    vc = v.reshape(B, H, N, C, dv)
    bc = beta.reshape(B, H, N, C)
    gc = jnp.cumsum(g.reshape(B, H, N, C), axis=-1)
    incl = jnp.tril(jnp.ones((C, C), dtype=bool))
    strict = jnp.tril(jnp.ones((C, C), dtype=bool), -1)
    gdiff = gc[..., :, None] - gc[..., None, :]
    decay = jnp.where(incl, jnp.exp(jnp.where(incl, gdiff, 0.0)), 0.0)
    kb = kc * bc[..., None]
    a = jnp.where(strict, jnp.einsum('bhnid,bhnjd->bhnij', kb, kc) * decay, 0.0)
    rhs = jnp.concatenate([vc * bc[..., None], kb * jnp.exp(gc)[..., None]], axis=-1)
    sol = lax.linalg.triangular_solve(a + jnp.eye(C, dtype=F32), rhs, left_side=True, lower=True)
    u, w = sol[..., :dv], sol[..., dv:]
    qk = jnp.einsum('bhnid,bhnjd->bhnij', qc, kc) * decay
    g_last = gc[..., -1]
    k_tail = kc * jnp.exp(g_last[..., None] - gc)[..., None]

    def step(state, xs):
        q_n, u_n, w_n, qk_n, g_n, kt_n, gl_n = xs
        v_new = u_n - jnp.einsum('bhck,bhkv->bhcv', w_n, state)
        o = (jnp.einsum('bhck,bhkv->bhcv', q_n * jnp.exp(g_n)[..., None], state)
             + jnp.einsum('bhij,bhjv->bhiv', qk_n, v_new))
        state = state * jnp.exp(gl_n)[..., None, None] + jnp.einsum('bhck,bhcv->bhkv', kt_n, v_new)
        return state, o

    xs = tuple(jnp.moveaxis(t, 2, 0) for t in (qc, u, w, qk, gc, k_tail, g_last))
    state0 = jnp.zeros((B, H, dk, dv), F32)
    _, o = lax.scan(step, state0, xs)
    return jnp.moveaxis(o, 0, 2).reshape(B, H, S, dv)


def gated_deltanet_bidir(qkv, z, gates, conv_w, a_log, dt_bias, norm_w):
    B, S = qkv.shape[0], qkv.shape[1]
    H = DN_HEADS
    qkv = short_conv(qkv, conv_w)
    q, k, v = jnp.split(qkv, 3, axis=-1)
    to_heads = lambda t, d: t.reshape(B, S, H, d).transpose(0, 2, 1, 3).astype(F32)
    q = l2_norm(to_heads(q, DN_KDIM))
    k = l2_norm(to_heads(k, DN_KDIM))
    v = to_heads(v, DN_VDIM)
    gt = gates.astype(F32).reshape(B, S, 4, H).transpose(2, 0, 3, 1)
    A = jnp.exp(a_log.astype(F32))[:, None, :, None]
    dtb = dt_bias.astype(F32)[:, None, :, None]
    g_f = -A[0] * jax.nn.softplus(gt[0] + dtb[0])
    g_b = -A[1] * jax.nn.softplus(gt[2] + dtb[1])
    beta_f = jax.nn.sigmoid(gt[1])
    beta_b = jax.nn.sigmoid(gt[3])
    o_f = gated_delta_chunked(q, k, v, g_f, beta_f)
    flip = lambda t: jnp.flip(t, axis=2)
    o_b = flip(gated_delta_chunked(flip(q), flip(k), flip(v), flip(g_b), flip(beta_b)))
    o = (o_f + o_b).transpose(0, 2, 1, 3)
    zf = z.astype(F32).reshape(B, S, H, DN_VDIM)
    o = rms_norm(o, norm_w) * jax.nn.silu(zf)
    return o.reshape(B, S, H * DN_VDIM).astype(z.dtype)


def memory_cross_attention(xn, memn, w_q, w_kv, w_o):
    B, S = xn.shape[0], xn.shape[1]
    M = memn.shape[1]
    q = (xn @ w_q).reshape(B, S, MEM_HEADS, MEM_HDIM)
    kv = (memn @ w_kv).reshape(B, M, 2, MEM_HEADS, MEM_HDIM)
    k, v = kv[:, :, 0], kv[:, :, 1]
    s = jnp.einsum('bshd,bmhd->bhsm', q, k).astype(F32) * (MEM_HDIM ** -0.5)
    p = jax.nn.softmax(s, axis=-1)
    o = jnp.einsum('bhsm,bmhd->bshd', p.astype(v.dtype), v).reshape(B, S, MEM_HEADS * MEM_HDIM)
    return o @ w_o


def setup_inputs(seed: int = 0) -> dict:
    key = jax.random.key(seed)
    ks = jax.random.split(key, 24)
    nrm = lambda k, shape, scale: jax.random.normal(k, shape, F32) * scale
    gain = lambda k, shape: 1.0 + 0.05 * jax.random.normal(k, shape, F32)
    L, D = DEPTH, D_MODEL
    x = jax.random.normal(ks[0], (BATCH, SEQ, D), F32)
    mem = jax.random.normal(ks[1], (BATCH, MEM_LEN, D), F32)
    positions = (jnp.arange(SEQ, dtype=jnp.int32)[None, :]
                 + jax.random.randint(ks[2], (BATCH, 1), 0, 4096, dtype=jnp.int32))
    A = jax.random.uniform(ks[10], (L, 2, DN_HEADS), F32, 1.0, 16.0)
    dt = jnp.exp(jax.random.uniform(ks[11], (L, 2, DN_HEADS), F32, math.log(1e-3), math.log(1e-1)))
    return {
        "x": x,
        "mem": mem,
        "positions": positions,
        "ffn1_norms": gain(ks[3], (L, 2, D)),
        "ffn1_w_gu": nrm(ks[4], (L, D, 2 * D_FF), D ** -0.5),
        "ffn1_w_down": nrm(ks[5], (L, D_FF, D), D_FF ** -0.5),
        "mix_norms": gain(ks[6], (L, 2, D)),
        "mix_w_in": nrm(ks[7], (L, D, IN_COLS), D ** -0.5),
        "dn_conv_w": nrm(ks[8], (L, DN_CONV, 3 * DN_WIDTH), DN_CONV ** -0.5),
        "dn_a_log": jnp.log(A),
        "dn_dt_bias": dt + jnp.log(-jnp.expm1(-dt)),
        "dn_norm_w": gain(ks[12], (L, DN_VDIM)),
        "diff_lambda": nrm(ks[13], (L, 4, DIFF_QKDIM), 0.1),
        "diff_subln_w": gain(ks[14], (L, DIFF_VDIM)),
        "mix_w_out": nrm(ks[15], (L, D, D), D ** -0.5),
        "mem_norms": gain(ks[16], (L, 3, D)),
        "mem_w_q": nrm(ks[17], (L, D, MEM_HEADS * MEM_HDIM), D ** -0.5),
        "mem_w_kv": nrm(ks[18], (L, D, 2 * MEM_HEADS * MEM_HDIM), D ** -0.5),
        "mem_w_o": nrm(ks[19], (L, MEM_HEADS * MEM_HDIM, D), (MEM_HEADS * MEM_HDIM) ** -0.5),
        "ffn2_norms": gain(ks[20], (L, 2, D)),
        "ffn2_w_gu": nrm(ks[21], (L, D, 2 * D_FF), D ** -0.5),
        "ffn2_w_down": nrm(ks[22], (L, D_FF, D), D_FF ** -0.5),
    }


def reference(x, mem, positions, ffn1_norms, ffn1_w_gu, ffn1_w_down, mix_norms, mix_w_in,
              dn_conv_w, dn_a_log, dn_dt_bias, dn_norm_w, diff_lambda, diff_subln_w, mix_w_out,
              mem_norms, mem_w_q, mem_w_kv, mem_w_o, ffn2_norms, ffn2_w_gu, ffn2_w_down):
    B, S = x.shape[0], x.shape[1]
    cos, sin = rope_tables(positions)
    for l in range(DEPTH):
        lambda_init = 0.8 - 0.6 * math.exp(-0.3 * l)
        h = rms_norm(x, ffn1_norms[l, 0])
        x = x + 0.5 * rms_norm(swiglu(h, ffn1_w_gu[l], ffn1_w_down[l]), ffn1_norms[l, 1])
        h = rms_norm(x, mix_norms[l, 0])
        proj = h @ mix_w_in[l]
        d_q, d_k, d_v, dn_qkv, dn_z, dn_gates = jnp.split(proj, IN_SPLITS, axis=-1)
        o_diff = diff_attention(d_q.reshape(B, S, 2 * DIFF_HEADS, DIFF_QKDIM),
                                d_k.reshape(B, S, 2 * DIFF_HEADS, DIFF_QKDIM),
                                d_v.reshape(B, S, DIFF_HEADS, DIFF_VDIM),
                                cos, sin, diff_lambda[l], diff_subln_w[l], lambda_init)
        o_dn = gated_deltanet_bidir(dn_qkv, dn_z, dn_gates, dn_conv_w[l], dn_a_log[l],
                                    dn_dt_bias[l], dn_norm_w[l])
        mixed = jnp.concatenate([o_diff.astype(x.dtype), o_dn.astype(x.dtype)], axis=-1) @ mix_w_out[l]
        x = x + rms_norm(mixed, mix_norms[l, 1])
        h = rms_norm(x, mem_norms[l, 0])
        memn = rms_norm(mem, mem_norms[l, 1])
        c = memory_cross_attention(h, memn, mem_w_q[l], mem_w_kv[l], mem_w_o[l])
        x = x + rms_norm(c, mem_norms[l, 2])
        h = rms_norm(x, ffn2_norms[l, 0])
        x = x + 0.5 * rms_norm(swiglu(h, ffn2_w_gu[l], ffn2_w_down[l]), ffn2_norms[l, 1])
    return x
```

```python
import numpy as np
from contextlib import ExitStack
import concourse.bass as bass
import concourse.mybir as mybir
from concourse.bass_utils import run_bass_kernel_spmd

F32 = mybir.dt.float32
BF16 = mybir.dt.bfloat16
I32 = mybir.dt.int32
AF = mybir.ActivationFunctionType
ALU = mybir.AluOpType
AX = mybir.AxisListType


class Cfg:
    def __init__(self, D=4096, S=2048, DFF=11008, MEM=256, ncores=8):
        self.D, self.S, self.DFF, self.MEM, self.ncores = D, S, DFF, MEM, ncores
        self.KC = D // 128
        self.NT = S // 128
        self.T = min(512, S)
        self.NP = S // self.T
        self.TT = self.T // 128
        self.FC = DFF // 128
        self.NKG = (self.FC + 15) // 16
        self.HW = D // 2
        self.NH = self.HW // 128
        self.INCOLS = 3 * self.HW + 4 * self.HW + 4 * self.NH
        self.MH = 4
        self.lambda_init = 0.2
        self.rope_theta = 500000.0


class Buf:
    __slots__ = ("w", "r", "name")

    def __init__(self, name=""):
        self.w = None
        self.r = []
        self.name = name


class DSem:
    def __init__(self, sem):
        self.sem = sem
        self.count = 0


ENGS = ("pe", "act", "dve", "pool", "sp")


class Prog:
    def __init__(self, nc, es):
        self.nc = nc
        self.es = es
        self.ops = {e: [] for e in ENGS}
        self.dsems = []
        self.all_dma_events = []

    def dsem(self, name):
        d = DSem(self.es.enter_context(self.nc.semaphore("d_" + name)))
        self.dsems.append(d)
        return d

    def emit(self, eng, fn, reads=(), writes=(), dsem=None, acc=False):
        deps = []
        for b in reads:
            if b.w is not None:
                deps.append(b.w)
        for b in writes:
            if b.w is not None:
                if not (acc and b.w[0] == "c" and b.w[1] == "pe"):
                    deps.append(b.w)
            deps.extend(b.r)
        seq = len(self.ops[eng])
        if dsem is None:
            ev = ("c", eng, seq)
        else:
            dsem.count += 16
            ev = ("d", dsem, dsem.count)
            self.all_dma_events.append(ev)
        deps2 = []
        for d in deps:
            if d[0] == "c" and d[1] == eng and eng == "pe":
                continue
            deps2.append(d)
        self.ops[eng].append([deps2, fn, ev, dsem])
        for b in reads:
            b.r.append(ev)
        for b in writes:
            b.w = ev
            b.r = []
        return ev

    def barrier(self):
        lasts = []
        for e in ENGS:
            if self.ops[e]:
                for op in reversed(self.ops[e]):
                    if op[2][0] == "c" and op[1] is not None:
                        lasts.append(op[2])
                        break
        dm = list(self.all_dma_events)
        for e in ENGS:
            self.ops[e].append([lasts + dm, None, ("c", e, len(self.ops[e])), None])
        self.all_dma_events = []

    def finalize(self, block):
        nc = self.nc
        csem = {e: self.es.enter_context(nc.semaphore("c_" + e)) for e in ENGS}
        waited = {e: set() for e in ENGS}
        for e in ENGS:
            for deps, fn, ev, ds in self.ops[e]:
                for d in deps:
                    if d[0] == "c":
                        waited[d[1]].add(d[2])
        rank = {}
        for e in ENGS:
            r = 0
            for seq in sorted(waited[e]):
                assert self.ops[e][seq][1] is not None, "wait on a barrier pseudo-op"
                r += 1
                rank[(e, seq)] = r
        ops = self.ops

        def run(e, eng):
            have = {}
            for seq, (deps, fn, ev, ds) in enumerate(ops[e]):
                need = {}
                for d in deps:
                    if d[0] == "c":
                        key = ("c", d[1])
                        val = rank[(d[1], d[2])]
                        sem = csem[d[1]]
                    else:
                        key = ("d", id(d[1]))
                        val = d[2]
                        sem = d[1].sem
                    if have.get(key, 0) >= val:
                        continue
                    if key not in need or need[key][1] < val:
                        need[key] = (sem, val)
                for key, (sem, val) in need.items():
                    eng.wait_ge(sem, val)
                    have[key] = val
                if fn is None:
                    continue
                ins = fn(eng)
                if ds is not None:
                    ins.then_inc(ds.sem, 16)
                elif (e, seq) in rank:
                    ins.then_inc(csem[e], 1)

        block.tensor(lambda eng: run("pe", eng))
        block.scalar(lambda eng: run("act", eng))
        block.vector(lambda eng: run("dve", eng))
        block.gpsimd(lambda eng: run("pool", eng))
        block.sync(lambda eng: run("sp", eng))


class K:
    def __init__(self, cfg):
        self.cfg = cfg
        self.nc = bass.Bass("TRN2", target_bir_lowering=False)
        self.es = ExitStack()
        self.p = Prog(self.nc, self.es)
        self.dram = {}

    def din(self, name, shape, dt=F32):
        t = self.nc.dram_tensor(name, list(shape), dt, kind="ExternalInput").ap()
        self.dram[name] = t
        return t

    def dscr(self, name, shape, dt=F32):
        t = self.nc.dram_tensor(name, list(shape), dt,
                                kind="ExternalOutput" if getattr(self.cfg, "debug", False) else "Internal").ap()
        self.dram[name] = t
        return t


def build(cfg):
    k = K(cfg)
    nc, es, p = k.nc, k.es, k.p
    D, S, DFF, KC, NT, T, NP, TT, FC, NKG = (cfg.D, cfg.S, cfg.DFF, cfg.KC, cfg.NT, cfg.T, cfg.NP,
                                             cfg.TT, cfg.FC, cfg.NKG)
    HW, NH, MEM = cfg.HW, cfg.NH, cfg.MEM
    NB256 = D // 256
    NFB = D // 512
    EPS = 1e-6

    x_in = k.din("x", [S, D])
    mem_in = k.din("mem", [MEM, D])
    pos_in = k.din("pos", [1, S], I32)
    out = nc.dram_tensor("out", [S, D], F32, kind="ExternalOutput").ap()
    wgu = [k.din(f"wgu{i}", [FC, 128, KC, 256]) for i in (1, 2)]
    wdn = [k.din(f"wdn{i}", [NFB, NKG, 128, 16, 512]) for i in (1, 2)]
    NBIN = (3 * HW + 4 * HW) // 256
    win = k.din("win", [NBIN, 128, KC, 256])
    wgate = k.din("wgate", [128, KC, 4 * NH])
    wout = k.din("wout", [NB256, 128, KC, 256])
    wmq = k.din("wmq", [2, 128, KC, 256])
    wmkv = k.din("wmkv", [4, 128, KC, 256])
    wmo = k.din("wmo", [NB256, 128, 4, 256])
    norms = k.din("norms", [9, D])
    convw = k.din("convw", [128, 3 * NH, 5])
    dn_small = k.din("dn_small", [1, 4 * NH])
    dn_normw = k.din("dn_normw", [1, 128])
    subln = k.din("subln", [1, 128])
    lam_in = k.din("lam", [1, 256])
    consts = k.din("consts", [128, 8, 128])
    invf = k.din("invf", [128, 1])
    Y = k.dscr("Y", [S, D])
    QT = k.dscr("QT", [NH, 128, S], BF16)
    KT = k.dscr("KT", [NH, 128, S], BF16)
    VV = k.dscr("VV", [S, HW], BF16)
    DNT = k.dscr("DNT", [3 * NH, 128, S])
    ZZ = k.dscr("ZZ", [S, HW])
    GG = k.dscr("GG", [S, 4 * NH])
    OT = k.dscr("OT", [KC, 128, S], BF16)
    MQT = k.dscr("MQT", [4, 128, S], BF16)

    ARENA_F32 = 46000
    arena = es.enter_context(nc.sbuf_tensor("arena", [128, ARENA_F32], F32))
    psum = es.enter_context(nc.psum_tensor("psum", [128, 8, 512], F32))
    cursor = [0]

    def alloc(nbytes):
        off = cursor[0]
        cursor[0] += (nbytes + 31) // 32 * 32
        assert cursor[0] <= ARENA_F32 * 4, f"arena overflow {cursor[0]}"
        return off

    def view(off, n_elems, dt):
        nb = n_elems * mybir.dt.size(dt)
        assert off % 4 == 0 and nb % 4 == 0
        v = arena[:, off // 4:(off + nb) // 4]
        return v if dt == F32 else v.bitcast(dt)

    o_const = alloc(8 * 128 * 4)
    cst = view(o_const, 8 * 128, F32).rearrange("p (a b) -> p a b", a=8)
    ident, ones = cst[:, 0, :], cst[:, 6, :]
    o_identb = alloc(128 * 2)
    identb = view(o_identb, 128, BF16)
    o_ss = alloc(NT * 16 * 4)
    ssparts = view(o_ss, NT * 16, F32).rearrange("p (a b) -> p a b", a=NT)
    o_small = alloc(64 * 4)
    small = view(o_small, 64, F32)
    B_const, B_ss, B_small, B_identb = Buf("const"), Buf("ss"), Buf("small"), Buf("identb")
    PB = [Buf(f"ps{i}") for i in range(8)]
    o_hT = alloc(KC * T * 2)
    hT = view(o_hT, KC * T, BF16).rearrange("p (a b) -> p a b", a=KC)
    B_hT = Buf("hT")
    NSLOT = 2
    o_ws = [alloc(8192 * 2) for _ in range(NSLOT)]
    wslot = [view(o, 8192, BF16) for o in o_ws]
    B_ws = [Buf(f"ws{i}") for i in range(NSLOT)]
    ws_sem = [p.dsem(f"ws{i}") for i in range(NSLOT)]
    ws_ctr = [0]
    o_big = cursor[0]
    big_cursor = [o_big]

    def balloc(nbytes):
        off = big_cursor[0]
        big_cursor[0] += (nbytes + 31) // 32 * 32
        assert big_cursor[0] <= ARENA_F32 * 4, f"big arena overflow {big_cursor[0]} {ARENA_F32*4}"
        return off

    o_big_holder = [o_big]

    def breset():
        big_cursor[0] = o_big_holder[0]

    misc_sem = p.dsem("misc")

    def dma(eng, out_ap, in_ap, dsem, reads=(), writes=()):
        return p.emit(eng, lambda e: e.dma_start(out=out_ap, in_=in_ap), reads, writes, dsem=dsem)

    o_epsv = alloc(8 * 4)
    epsv = view(o_epsv, 8, F32)
    B_eps = Buf("eps")
    EPSV = {1e-6: 0, 1e-5: 1, 1.0: 2, 0.0: 3}
    for val_, col_ in EPSV.items():
        p.emit("dve", lambda e, val_=val_, col_=col_: e.memset(epsv[:, col_:col_ + 1], float(val_)), [], [B_eps])

    def epsc(v):
        c = EPSV[v]
        return epsv[:, c:c + 1]

    dma("sp", cst, consts, misc_sem, writes=[B_const])
    p.emit("dve", lambda e: e.tensor_copy(out=identb, in_=ident), [B_const], [B_identb])

    def wload(src_ap, shape_str=None, **kw):
        i = ws_ctr[0] % NSLOT
        ws_ctr[0] += 1
        n = 1
        for s_ in src_ap.shape[1:]:
            n *= s_
        assert n <= 8192
        dst = wslot[i][:, 0:n]
        if len(src_ap.shape) == 3:
            dst = dst.rearrange("p (a b) -> p a b", a=src_ap.shape[1])
        p.emit("pool", lambda e: e.dma_start(out=dst, in_=src_ap, max_dma_last_dim=8192), [], [B_ws[i]],
               dsem=ws_sem[i])
        return dst, B_ws[i]

    def mm(out_ap, lhsT, rhs, start, stop, reads, wbuf):
        p.emit("pe", lambda e: e.matmul(out_ap, lhsT, rhs, start=start, stop=stop), reads, [wbuf],
               acc=not start)

    def rstd_from_ss(dst, src, n, eps, rb, wb, eng="dve"):
        p.emit("dve", lambda e: e.tensor_scalar(out=dst, in0=src, scalar1=1.0 / n, scalar2=float(eps), op0=ALU.mult,
                                                op1=ALU.add), rb, wb)
        p.emit("act", lambda e: e.activation(out=dst, in_=dst, func=AF.Ln), wb, wb)
        p.emit("act", lambda e: e.activation(out=dst, in_=dst, func=AF.Exp, scale=-0.5), wb, wb)

    def boundary(x_src, y_info, pre_row, tok0, ntok, want_h=True, write_x=True):
        breset()
        o_xt, o_yt = balloc(D * 4), balloc(D * 4)
        o_wpost, o_wpre, o_xn = balloc(D * 4), balloc(D * 4), balloc(D * 2)
        o_junk = balloc(D * 4)
        xt, yt = view(o_xt, D, F32), view(o_yt, D, F32)
        wpost, wpre, xn = view(o_wpost, D, F32), view(o_wpre, D, F32), view(o_xn, D, BF16)
        junk = view(o_junk, D, F32)
        Bx, By, Bwpo, Bwpr, Bxn, Bj = Buf("xt"), Buf("yt"), Buf("wpost"), Buf("wpre"), Buf("xn"), Buf("junk")
        Bst = Buf("stat")
        st = small
        sx, sy, sw, sxs = (boundary.sems[i] for i in range(4))
        if y_info is not None:
            coef, post_row, nparts = y_info
            dma("sp", wpost, norms[post_row:post_row + 1, :].partition_broadcast(128), sw, writes=[Bwpo])
        if want_h:
            dma("sp", wpre, norms[pre_row:pre_row + 1, :].partition_broadcast(128), sw, writes=[Bwpr])
        for tt in range(ntok // 128):
            r0 = tok0 + tt * 128
            gt = r0 // 128
            dma("sp", xt, x_src[r0:r0 + 128, :], sx, writes=[Bx])
            if y_info is not None:
                dma("sp", yt, Y[r0:r0 + 128, :], sy, writes=[By])
                p.emit("dve", lambda e, gt=gt: e.reduce_sum(out=st[:, 2:3], in_=ssparts[:, gt, 0:nparts], axis=AX.X),
                       [B_ss], [Bst])
                rstd_from_ss(st[:, 3:4], st[:, 2:3], D, EPS, [Bst], [Bst])
                if coef != 1.0:
                    p.emit("dve", lambda e: e.tensor_scalar(out=st[:, 3:4], in0=st[:, 3:4], scalar1=float(coef),
                                                            scalar2=None, op0=ALU.mult), [Bst], [Bst])
                p.emit("pool", lambda e: e.tensor_tensor(out=yt, in0=yt, in1=wpost, op=ALU.mult), [By, Bwpo], [By])
                p.emit("dve", lambda e: e.scalar_tensor_tensor(out=xt, in0=yt, scalar=st[:, 3:4], in1=xt,
                                                               op0=ALU.mult, op1=ALU.add), [By, Bx, Bst], [Bx])
            if write_x and (y_info is not None or x_src is not out):
                dma("sp", out[r0:r0 + 128, :], xt, sxs, reads=[Bx])
            if not want_h:
                continue
            p.emit("act", lambda e: e.activation(out=junk, in_=xt, func=AF.Square), [Bx], [Bj])
            p.emit("dve", lambda e: e.reduce_sum(out=st[:, 0:1], in_=junk, axis=AX.X), [Bj], [Bst])
            rstd_from_ss(st[:, 1:2], st[:, 0:1], D, EPS, [Bst], [Bst])
            p.emit("dve", lambda e: e.scalar_tensor_tensor(out=xn, in0=xt, scalar=st[:, 1:2], in1=wpre,
                                                           op0=ALU.mult, op1=ALU.mult), [Bx, Bst, Bwpr], [Bxn])
            for g in range(0, KC, 8):
                ng = min(8, KC - g)
                bank = (g // 8) % 2
                pt = psum[:, bank, :].bitcast(BF16)[:, 0:ng * 128].rearrange("p (a b) -> p a b", a=ng)
                for j in range(ng):
                    p.emit("pe", lambda e, j=j, g=g, pt=pt: e.transpose(pt[:, j, :], xn[:, (g + j) * 128:(g + j + 1) * 128],
                                                                         identb),
                           [Bxn, B_identb], [PB[bank]], acc=(j > 0))
                eng = "act" if (g // 8) % 2 == 0 else "dve"
                dst = hT[:, g:g + ng, tt * 128:(tt + 1) * 128]
                if eng == "act":
                    p.emit("act", lambda e, dst=dst, pt=pt: e.activation(out=dst, in_=pt, func=AF.Copy),
                           [PB[bank]], [B_hT])
                else:
                    p.emit("dve", lambda e, dst=dst, pt=pt: e.tensor_copy(out=dst, in_=pt), [PB[bank]], [B_hT])

    boundary.sems = [p.dsem(n) for n in ("bx", "by", "bw", "bxs")]

    ystage_sem = [p.dsem("ys0"), p.dsem("ys1")]

    def evac_y(ps_ap, bank, r0, c0, ncols, col_idx, stg):
        i = evac_y.ctr % 2
        evac_y.ctr += 1
        ys, Bys, jk, Bjk = stg[i]
        p.emit("dve", lambda e: e.tensor_copy(out=ys[:, 0:ncols], in_=ps_ap), [PB[bank]], [Bys])
        p.emit("act", lambda e: e.activation(out=jk[:, 0:ncols], in_=ys[:, 0:ncols], func=AF.Square), [Bys], [Bjk])
        p.emit("dve", lambda e: e.reduce_sum(out=ssparts[:, r0 // 128, col_idx:col_idx + 1], in_=jk[:, 0:ncols],
                                             axis=AX.X), [Bjk], [B_ss])
        dma("sp", Y[r0:r0 + 128, c0:c0 + ncols], ys[:, 0:ncols], ystage_sem[i], reads=[Bys])

    evac_y.ctr = 0

    def make_ystage():
        stg = []
        for i in range(2):
            o1, o2 = balloc(512 * 4), balloc(512 * 4)
            stg.append((view(o1, 512, F32), Buf(f"ys{i}"), view(o2, 512, F32), Buf(f"jk{i}")))
        return stg

    def ffn(idx, x_src, y_prev, pre_row):
        for ps_ in range(NP):
            tok0 = ps_ * T
            boundary(x_src, y_prev, pre_row, tok0, T)
            p.barrier()
            breset()
            o_act = balloc(FC * T * 2)
            actT = view(o_act, FC * T, BF16).rearrange("p (a b) -> p a b", a=FC)
            B_act = Buf("actT")
            o_sg = [balloc(T * 4) for _ in range(2)]
            sg = [view(o, T, F32) for o in o_sg]
            B_sg = [Buf("sg0"), Buf("sg1")]
            stg = make_ystage()
            for j in range(FC):
                wv, wb = wload(wgu[idx][j])
                bg, bu = (2 * j) % 4, (2 * j) % 4 + 1
                for kc in range(KC):
                    mm(psum[:, bg, 0:T], wv[:, kc, 0:128], hT[:, kc, :], kc == 0, kc == KC - 1, [wb, B_hT], PB[bg])
                for kc in range(KC):
                    mm(psum[:, bu, 0:T], wv[:, kc, 128:256], hT[:, kc, :], kc == 0, kc == KC - 1, [wb, B_hT], PB[bu])
                si = j % 2
                p.emit("act", lambda e, si=si, bg=bg: e.activation(out=sg[si], in_=psum[:, bg, 0:T], func=AF.Silu),
                       [PB[bg]], [B_sg[si]])
                p.emit("dve", lambda e, si=si, bu=bu, j=j: e.tensor_tensor(out=actT[:, j, :], in0=sg[si],
                                                                          in1=psum[:, bu, 0:T], op=ALU.mult),
                       [B_sg[si], PB[bu]], [B_act])
            for fb in range(NFB):
                for kg in range(NKG):
                    nk = min(16, FC - kg * 16)
                    wv, wb = wload(wdn[idx][fb, kg][:, 0:nk, :])
                    for kk in range(nk):
                        kc = kg * 16 + kk
                        for tt in range(TT):
                            bank = 4 * (fb % 2) + tt
                            mm(psum[:, bank, :], actT[:, kc, tt * 128:(tt + 1) * 128], wv[:, kk, :], kc == 0,
                               kc == FC - 1, [wb, B_act], PB[bank])
                for tt in range(TT):
                    bank = 4 * (fb % 2) + tt
                    evac_y(psum[:, bank, :], bank, tok0 + tt * 128, fb * 512, 512, fb, stg)
            p.barrier()


    def proj_fm(wblk_ap, nchunk, consume, ntok=T, hsrc=None, Bh=None, kcn=KC, banks=(2, 3)):
        hsrc = hT if hsrc is None else hsrc
        Bh = B_hT if Bh is None else Bh
        wv, wb = wload(wblk_ap)
        for j in range(nchunk):
            bank = banks[proj_fm.ctr % len(banks)]
            proj_fm.ctr += 1
            for kc in range(kcn):
                mm(psum[:, bank, 0:ntok], wv[:, kc, j * 128:(j + 1) * 128], hsrc[:, kc, 0:ntok], kc == 0, kc == kcn - 1,
                   [wb, Bh], PB[bank])
            consume(j, psum[:, bank, 0:ntok], bank)

    proj_fm.ctr = 0

    def proj_tm(wblk_ap, ncols, consume, ntok=T, hsrc=None, Bh=None, kcn=KC, banks=(4, 5, 6, 7), wv_wb=None):
        hsrc = hT if hsrc is None else hsrc
        Bh = B_hT if Bh is None else Bh
        wv, wb = wload(wblk_ap) if wv_wb is None else wv_wb
        for tt in range(ntok // 128):
            bank = banks[proj_tm.ctr % len(banks)]
            proj_tm.ctr += 1
            for kc in range(kcn):
                mm(psum[:, bank, 0:ncols], hsrc[:, kc, tt * 128:(tt + 1) * 128], wv[:, kc, 0:ncols], kc == 0,
                   kc == kcn - 1, [wb, Bh], PB[bank])
            consume(tt, psum[:, bank, 0:ncols], bank)

    proj_tm.ctr = 0

    class Ring:
        def __init__(self, name, n, nelem, dt):
            self.items = []
            for i in range(n):
                o = balloc(nelem * mybir.dt.size(dt))
                self.items.append((view(o, nelem, dt), Buf(f"{name}{i}"), Ring.sems(name, i)))
            self.i = 0

        def next(self):
            it = self.items[self.i % len(self.items)]
            self.i += 1
            return it

        _sems = {}

        @staticmethod
        def sems(name, i):
            key = (name, i)
            if key not in Ring._sems:
                Ring._sems[key] = p.dsem(f"{name}{i}")
            return Ring._sems[key]

    Ring._sems = {}

    def copy_evac(i, dst, src, reads, writes):
        if i % 2 == 0:
            p.emit("act", lambda e: e.activation(out=dst, in_=src, func=AF.Copy), reads, writes)
        else:
            p.emit("dve", lambda e: e.tensor_copy(out=dst, in_=src), reads, writes)

    def mem_attn(x_src, y_prev, pre_row, memn_row):
        MH = 4
        boundary(mem_in, None, memn_row, 0, MEM, write_x=False)
        p.barrier()
        breset()
        o_mk, o_mv = balloc(MH * MEM * 2), balloc((MEM // 128) * MH * 128 * 2)
        memK = view(o_mk, MH * MEM, BF16).rearrange("p (a b) -> p a b", a=MH)
        memV = view(o_mv, (MEM // 128) * MH * 128, BF16).rearrange("p (a b c) -> p a b c", a=MEM // 128, b=MH)
        B_mk, B_mv = Buf("memK"), Buf("memV")
        o_ones = balloc(128 * 2)
        onesb = view(o_ones, 128, BF16)
        B_onesb = Buf("onesb")
        p.emit("dve", lambda e: e.tensor_copy(out=onesb, in_=ones), [B_const], [B_onesb])
        for blk in range(2):
            def consK(j, ps, bank, blk=blk):
                h = blk * 2 + j
                copy_evac(h, memK[:, h, :], ps, [PB[bank]], [B_mk])
            proj_fm(wmkv[blk], 2, consK, ntok=MEM)
        for blk in range(2):
            def consV(tt, ps, bank, blk=blk):
                copy_evac(tt, memV[:, tt, 2 * blk:2 * blk + 2, :], ps.rearrange("p (a b) -> p a b", a=2),
                          [PB[bank]], [B_mv])
            proj_tm(wmkv[2 + blk], 256, consV, ntok=MEM)
        p.barrier()
        keep = big_cursor[0]
        scale = 128 ** -0.5
        for ps_ in range(NP):
            tok0 = ps_ * T
            big_cursor[0] = keep
            boundary_keep(x_src, y_prev, pre_row, tok0, T, keep)
            p.barrier()
            big_cursor[0] = keep
            o_q, o_oc = balloc(MH * T * 2), balloc(MH * T * 2)
            qT = view(o_q, MH * T, BF16).rearrange("p (a b) -> p a b", a=MH)
            ocT = view(o_oc, MH * T, BF16).rearrange("p (a b) -> p a b", a=MH)
            B_q, B_oc = Buf("mqT"), Buf("ocT")
            o_E = [balloc(T * 2) for _ in range(2)]
            E = [view(o, T, BF16) for o in o_E]
            B_E = [Buf("E0"), Buf("E1")]
            o_r = balloc(T * 4)
            rr = view(o_r, T, F32)
            B_r = Buf("rr")
            stg = make_ystage()
            for blk in range(2):
                def consQ(j, ps, bank, blk=blk):
                    h = blk * 2 + j
                    copy_evac(h, qT[:, h, :], ps, [PB[bank]], [B_q])
                proj_fm(wmq[blk], 2, consQ)
            for h in range(MH):
                for mt in range(MEM // 128):
                    sb = mt % 2
                    mm(psum[:, sb, 0:T], memK[:, h, mt * 128:(mt + 1) * 128], qT[:, h, :], True, True, [B_mk, B_q], PB[sb])
                    p.emit("act", lambda e, sb=sb: e.activation(out=E[sb], in_=psum[:, sb, 0:T], func=AF.Exp, scale=scale),
                           [PB[sb]], [B_E[sb]])
                    mm(psum[:, 2, 0:T], memV[:, mt, h, :], E[sb], mt == 0, mt == MEM // 128 - 1, [B_mv, B_E[sb]], PB[2])
                    mm(psum[:, 3, 0:T], onesb, E[sb], mt == 0, mt == MEM // 128 - 1, [B_onesb, B_E[sb]], PB[3])
                p.emit("dve", lambda e: e.reciprocal(out=rr, in_=psum[:, 3, 0:T]), [PB[3]], [B_r])
                p.emit("dve", lambda e, h=h: e.tensor_tensor(out=ocT[:, h, :], in0=psum[:, 2, 0:T], in1=rr, op=ALU.mult),
                       [PB[2], B_r], [B_oc])
            for blk in range(NB256):
                def consO(tt, ps, bank, blk=blk, tok0=tok0):
                    evac_y(ps, bank, tok0 + tt * 128, blk * 256, 256, blk, stg)
                proj_tm(wmo[blk], 256, consO, hsrc=ocT, Bh=B_oc, kcn=4)
            p.barrier()

    def boundary_keep(x_src, y_prev, pre_row, tok0, ntok, keep):
        saved = o_big_holder[0]
        o_big_holder[0] = keep
        try:
            boundary(x_src, y_prev, pre_row, tok0, ntok)
        finally:
            o_big_holder[0] = saved


    def mixer_inproj(x_src, y_prev, pre_row):
        import math
        NQ = HW // 256
        breset()
        o_cos, o_sin = balloc(S * 4), balloc(S * 4)
        COS, SIN = view(o_cos, S, F32), view(o_sin, S, F32)
        B_cs = Buf("cossin")
        o_wg = balloc(KC * 4 * NH * 2)
        wg = view(o_wg, KC * 4 * NH, BF16).rearrange("p (a b) -> p a b", a=KC)
        B_wg = Buf("wg")
        o_invf = balloc(4)
        invf_t = view(o_invf, 1, F32)
        keep = big_cursor[0]
        o_pi, o_pf = balloc(S * 4), balloc(S * 4)
        posi, posf = view(o_pi, S, I32), view(o_pf, S, F32)
        B_pi, B_pf = Buf("posi"), Buf("posf")
        dma("sp", posi, pos_in.partition_broadcast(128), misc_sem, writes=[B_pi])
        dma("sp", invf_t, invf, misc_sem, writes=[B_cs])
        p.emit("pool", lambda e: e.dma_start(out=wg, in_=wgate), [], [B_wg], dsem=misc_sem)
        p.emit("dve", lambda e: e.tensor_copy(out=posf, in_=posi), [B_pi], [B_pf])
        p.emit("dve", lambda e: e.tensor_scalar(out=posf, in0=posf, scalar1=invf_t[:, 0:1], scalar2=None, op0=ALU.mult),
               [B_pf, B_cs], [B_pf])
        TWO_PI = 2.0 * math.pi
        o_kf = balloc(S * 4)
        kf = view(o_kf, S, F32)
        B_kf = Buf("kf")
        TS = lambda **kw: (lambda e: e.tensor_scalar(**kw))
        for tab, shift in ((SIN, 0.0), (COS, 0.25)):
            p.emit("dve", TS(out=tab, in0=posf, scalar1=1.0 / TWO_PI, scalar2=shift, op0=ALU.mult, op1=ALU.add), [B_pf], [B_cs])
            p.emit("dve", lambda e, tab=tab: e.tensor_copy(out=posi, in_=tab), [B_cs], [B_pi])
            p.emit("dve", lambda e: e.tensor_copy(out=kf, in_=posi), [B_pi], [B_kf])
            p.emit("dve", lambda e, tab=tab: e.tensor_tensor(out=tab, in0=tab, in1=kf, op=ALU.subtract), [B_cs, B_kf], [B_cs])
            p.emit("dve", TS(out=kf, in0=tab, scalar1=0.5, scalar2=None, op0=ALU.is_gt), [B_cs], [B_kf])
            p.emit("dve", lambda e, tab=tab: e.tensor_tensor(out=tab, in0=tab, in1=kf, op=ALU.subtract), [B_cs, B_kf], [B_cs])
            p.emit("dve", TS(out=kf, in0=tab, scalar1=-0.5, scalar2=None, op0=ALU.is_lt), [B_cs], [B_kf])
            p.emit("dve", lambda e, tab=tab: e.tensor_tensor(out=tab, in0=tab, in1=kf, op=ALU.add), [B_cs, B_kf], [B_cs])
            p.emit("dve", TS(out=tab, in0=tab, scalar1=0.49999, scalar2=-0.49999, op0=ALU.min, op1=ALU.max), [B_cs], [B_cs])
            p.emit("act", lambda e, tab=tab: e.activation(out=tab, in_=tab, func=AF.Sin, scale=TWO_PI), [B_cs], [B_cs])
        p.barrier()
        RT = cst[:, 5, :]
        for ps_ in range(NP):
            tok0 = ps_ * T
            boundary_keep(x_src, y_prev, pre_row, tok0, T, keep)
            p.barrier()
            big_cursor[0] = keep
            rq = Ring("rq", 2, T, F32)
            rt1 = Ring("rt1", 2, T, F32)
            rt2 = Ring("rt2", 2, T, F32)
            rqo = Ring("rqo", 2, T, BF16)
            rdn = Ring("rdn", 2, T, F32)
            rv = Ring("rv", 2, 256, BF16)
            rz = Ring("rz", 2, 256, F32)
            rg = Ring("rg", 2, 4 * NH, F32)
            for blk in range(2 * NQ):
                dst_t = QT if blk < NQ else KT

                def consQK(j, ps, bank, blk=blk, dst_t=dst_t, tok0=tok0):
                    c = (blk % NQ) * 2 + j
                    qs, Bq, _ = rq.next()
                    t1, Bt1, _ = rt1.next()
                    t2, Bt2, _ = rt2.next()
                    qo, Bqo, sqo = rqo.next()
                    p.emit("act", lambda e: e.activation(out=qs, in_=ps, func=AF.Copy), [PB[bank]], [Bq])
                    mm(psum[:, 0, 0:T], RT, qs, True, True, [B_const, Bq], PB[0])
                    p.emit("pool", lambda e: e.tensor_tensor(out=t1, in0=qs, in1=COS[:, tok0:tok0 + T], op=ALU.mult),
                           [Bq, B_cs], [Bt1])
                    p.emit("dve", lambda e: e.tensor_tensor(out=t2, in0=psum[:, 0, 0:T], in1=SIN[:, tok0:tok0 + T],
                                                            op=ALU.mult), [PB[0], B_cs], [Bt2])
                    p.emit("pool", lambda e: e.tensor_tensor(out=qo, in0=t1, in1=t2, op=ALU.add), [Bt1, Bt2], [Bqo])
                    dma("sp", dst_t[c, :, tok0:tok0 + T], qo, sqo, reads=[Bqo])
                proj_fm(win[blk], 2, consQK)
            for blk in range(3 * NQ):
                def consDN(j, ps, bank, blk=blk, tok0=tok0):
                    c = blk * 2 + j
                    d, Bd, sd = rdn.next()
                    copy_evac(c, d, ps, [PB[bank]], [Bd])
                    dma("sp", DNT[c, :, tok0:tok0 + T], d, sd, reads=[Bd])
                proj_fm(win[2 * NQ + blk], 2, consDN)
            for blk in range(NQ):
                def consV(tt, ps, bank, blk=blk, tok0=tok0):
                    v, Bv, sv = rv.next()
                    copy_evac(tt, v, ps, [PB[bank]], [Bv])
                    dma("sp", VV[tok0 + tt * 128:tok0 + (tt + 1) * 128, blk * 256:(blk + 1) * 256], v, sv, reads=[Bv])
                proj_tm(win[5 * NQ + blk], 256, consV)
            for blk in range(NQ):
                def consZ(tt, ps, bank, blk=blk, tok0=tok0):
                    z, Bz, sz = rz.next()
                    copy_evac(tt, z, ps, [PB[bank]], [Bz])
                    dma("sp", ZZ[tok0 + tt * 128:tok0 + (tt + 1) * 128, blk * 256:(blk + 1) * 256], z, sz, reads=[Bz])
                proj_tm(win[6 * NQ + blk], 256, consZ)

            def consG(tt, ps, bank, tok0=tok0):
                g, Bg, sg_ = rg.next()
                copy_evac(tt, g, ps, [PB[bank]], [Bg])
                dma("sp", GG[tok0 + tt * 128:tok0 + (tt + 1) * 128, :], g, sg_, reads=[Bg])
            proj_tm(None, 4 * NH, consG, wv_wb=(wg, B_wg))
            p.barrier()

    def diff_attn():
        breset()
        lam0 = cfg.lambda_init
        o_lp = balloc(256 * 4)
        lp = view(o_lp, 256, F32)
        o_lm = balloc(16 * 4)
        lm = view(o_lm, 16, F32)
        B_lp, B_lm = Buf("lp"), Buf("lm")
        o_sw = balloc(8)
        sw = view(o_sw, 2, F32)
        B_sw = Buf("sw")
        o_ones = balloc(128 * 2)
        onesb = view(o_ones, 128, BF16)
        B_onesb = Buf("onesb")
        p.emit("dve", lambda e: e.tensor_copy(out=onesb, in_=ones), [B_const], [B_onesb])
        dma("sp", lp, lam_in.partition_broadcast(128), misc_sem, writes=[B_lp])
        dma("sp", sw[:, 0:1], subln.rearrange("o d -> d o"), misc_sem, writes=[B_sw])
        p.emit("dve", lambda e: e.tensor_scalar(out=sw[:, 1:2], in0=sw[:, 0:1], scalar1=float(1.0 - lam0), scalar2=None,
                                                op0=ALU.mult), [B_sw], [B_sw])
        p.emit("dve", lambda e: e.tensor_tensor(out=lp[:, 0:64], in0=lp[:, 0:64], in1=lp[:, 64:128], op=ALU.mult), [B_lp], [B_lp])
        p.emit("dve", lambda e: e.tensor_tensor(out=lp[:, 128:192], in0=lp[:, 128:192], in1=lp[:, 192:256], op=ALU.mult), [B_lp], [B_lp])
        p.emit("dve", lambda e: e.reduce_sum(out=lm[:, 0:1], in_=lp[:, 0:64], axis=AX.X), [B_lp], [B_lm])
        p.emit("dve", lambda e: e.reduce_sum(out=lm[:, 1:2], in_=lp[:, 128:192], axis=AX.X), [B_lp], [B_lm])
        p.emit("act", lambda e: e.activation(out=lm[:, 2:4], in_=lm[:, 0:2], func=AF.Exp), [B_lm], [B_lm])
        p.emit("dve", lambda e: e.tensor_tensor(out=lm[:, 4:5], in0=lm[:, 3:4], in1=lm[:, 2:3], op=ALU.subtract), [B_lm], [B_lm])
        p.emit("dve", lambda e: e.tensor_scalar(out=lm[:, 4:5], in0=lm[:, 4:5], scalar1=-float(lam0), scalar2=None, op0=ALU.add),
               [B_lm], [B_lm])
        o_q, o_k, o_v = balloc(S * 2), balloc(S * 2), balloc(NT * 128 * 2)
        qt, kt = view(o_q, S, BF16), view(o_k, S, BF16)
        vt = view(o_v, NT * 128, BF16).rearrange("p (a b) -> p a b", a=NT)
        B_q, B_k, B_v = Buf("dq"), Buf("dk"), Buf("dv")
        sq_, sk_, sv_ = p.dsem("dq"), p.dsem("dk"), p.dsem("dv")
        o_E = [balloc(T * 2) for _ in range(2)]
        E = [view(o, T, BF16) for o in o_E]
        B_E = [Buf("E0"), Buf("E1")]
        f32t = lambda: view(balloc(T * 4), T, F32)
        r1, o1, o2, sqv, rs = f32t(), f32t(), f32t(), f32t(), f32t()
        B_r1, B_o1, B_o2, B_sqv, B_rs = Buf("r1"), Buf("o1"), Buf("o2"), Buf("sqv"), Buf("rs")
        ron = Ring("ron", 2, T, BF16)
        for c in range(NH):
            dma("sp", qt, QT[c], sq_, writes=[B_q])
            dma("sp", kt, KT[c], sk_, writes=[B_k])
            dma("sp", vt, VV[:, c * 128:(c + 1) * 128].rearrange("(a p) d -> p a d", p=128), sv_, writes=[B_v])
            for qg in range(S // T):
                q0 = qg * T
                for a in range(2):
                    pa = slice(a * 64, (a + 1) * 64)
                    bo, bs = 2 + 2 * a, 3 + 2 * a
                    for kb in range(NT):
                        sb = kb % 2
                        mm(psum[:, sb, 0:T], kt[pa, kb * 128:(kb + 1) * 128], qt[pa, q0:q0 + T], True, True, [B_k, B_q], PB[sb])
                        p.emit("act", lambda e, sb=sb: e.activation(out=E[sb], in_=psum[:, sb, 0:T], func=AF.Exp, scale=0.125),
                               [PB[sb]], [B_E[sb]])
                        mm(psum[:, bo, 0:T], vt[:, kb, :], E[sb], kb == 0, kb == NT - 1, [B_v, B_E[sb]], PB[bo])
                        mm(psum[:, bs, 0:T], onesb, E[sb], kb == 0, kb == NT - 1, [B_onesb, B_E[sb]], PB[bs])
                p.emit("dve", lambda e: e.reciprocal(out=r1, in_=psum[:, 3, 0:T]), [PB[3]], [B_r1])
                p.emit("dve", lambda e: e.tensor_tensor(out=o1, in0=psum[:, 2, 0:T], in1=r1, op=ALU.mult), [PB[2], B_r1], [B_o1])
                p.emit("dve", lambda e: e.reciprocal(out=r1, in_=psum[:, 5, 0:T]), [PB[5]], [B_r1])
                p.emit("dve", lambda e: e.tensor_tensor(out=o2, in0=psum[:, 4, 0:T], in1=r1, op=ALU.mult), [PB[4], B_r1], [B_o2])
                p.emit("dve", lambda e: e.scalar_tensor_tensor(out=o1, in0=o2, scalar=lm[:, 4:5], in1=o1, op0=ALU.mult,
                                                               op1=ALU.add), [B_o2, B_o1, B_lm], [B_o1])
                p.emit("act", lambda e: e.activation(out=sqv, in_=o1, func=AF.Square), [B_o1], [B_sqv])
                mm(psum[:, 6, 0:T], ones, sqv, True, True, [B_const, B_sqv], PB[6])
                p.emit("dve", lambda e: e.tensor_scalar(out=rs, in0=psum[:, 6, 0:T], scalar1=1.0 / 128, scalar2=1e-5,
                                                        op0=ALU.mult, op1=ALU.add), [PB[6]], [B_rs])
                p.emit("act", lambda e: e.activation(out=rs, in_=rs, func=AF.Ln), [B_rs], [B_rs])
                p.emit("act", lambda e: e.activation(out=rs, in_=rs, func=AF.Exp, scale=-0.5), [B_rs], [B_rs])
                on, Bon, son = ron.next()
                p.emit("dve", lambda e, on=on: e.scalar_tensor_tensor(out=on, in0=o1, scalar=sw[:, 1:2], in1=rs, op0=ALU.mult,
                                                                      op1=ALU.mult), [B_o1, B_sw, B_rs], [Bon])
                dma("sp", OT[c, :, q0:q0 + T], on, son, reads=[Bon])
        p.barrier()

    op_sem = p.dsem("opl")

    def out_proj():
        for ps_ in range(NP):
            tok0 = ps_ * T
            breset()
            stg = make_ystage()
            for c_ in range(KC):
                dma("sp", hT[:, c_, :], OT[c_, :, tok0:tok0 + T], op_sem, writes=[B_hT])
            for blk in range(NB256):
                def consO(tt, ps, bank, blk=blk, tok0=tok0):
                    evac_y(ps, bank, tok0 + tt * 128, blk * 256, 256, blk, stg)
                proj_tm(wout[blk], 256, consO)
            p.barrier()


    def gdn():
        o_big_holder[0] = o_hT
        breset()
        G4 = 4 * NH
        f32v = lambda n: view(balloc(n * 4), n, F32)
        gts = f32v(NT * G4).rearrange("p (a b) -> p a b", a=NT)
        dsm = f32v(G4)
        gd = f32v(NT * 2 * NH).rearrange("p (a d h) -> p a d h", a=NT, d=2)
        bd = f32v(NT * 2 * NH).rearrange("p (a d h) -> p a d h", a=NT, d=2)
        NN = NT * NH
        egc_f, negc_f, etail_f, egtot_f = f32v(2 * NN), f32v(2 * NN), f32v(2 * NN), f32v(2 * NN)
        v4 = lambda f: f.rearrange("p (d a h) -> p d a h", d=2, a=NT)
        egc, negc, etail, egtot = v4(egc_f), v4(negc_f), v4(etail_f), v4(egtot_f)
        tmpg_f, tmpg2_f = f32v(NN), f32v(NN)
        tmpg = tmpg_f.rearrange("p (a h) -> p a h", a=NT)
        tmpg2 = tmpg2_f.rearrange("p (a h) -> p a h", a=NT)
        cw = f32v(3 * NH * 5).rearrange("p (c t) -> p c t", c=3 * NH)
        nwb = f32v(128)
        Bg = Buf("gates")
        dma("sp", gts, GG.rearrange("(a p) g -> p a g", p=128), misc_sem, writes=[Bg])
        dma("sp", dsm, dn_small.partition_broadcast(128), misc_sem, writes=[Bg])
        dma("sp", cw, convw, misc_sem, writes=[Bg])
        dma("sp", nwb, dn_normw.partition_broadcast(128), misc_sem, writes=[Bg])
        p.barrier()
        E1 = lambda eng, fn: p.emit(eng, fn, [Bg], [Bg])
        E1("act", lambda e: e.activation(out=dsm[:, 0:2 * NH], in_=dsm[:, 0:2 * NH], func=AF.Exp))
        E1("dve", lambda e: e.tensor_scalar(out=dsm[:, 0:2 * NH], in0=dsm[:, 0:2 * NH], scalar1=-1.0, scalar2=None, op0=ALU.mult))
        for d in range(2):
            a_sl = slice(d * 2 * NH, d * 2 * NH + NH)
            b_sl = slice(d * 2 * NH + NH, (d + 1) * 2 * NH)
            for t in range(NT):
                E1("dve", lambda e, t=t, a_sl=a_sl, d=d: e.tensor_tensor(out=tmpg[:, t, :], in0=gts[:, t, a_sl],
                                                                      in1=dsm[:, 2 * NH + d * NH:2 * NH + (d + 1) * NH], op=ALU.add))
            E1("dve", lambda e: e.tensor_scalar(out=tmpg2, in0=tmpg, scalar1=-1.0, scalar2=None, op0=ALU.mult))
            E1("dve", lambda e: e.tensor_tensor(out=tmpg2, in0=tmpg2, in1=tmpg, op=ALU.max))
            E1("act", lambda e: e.activation(out=tmpg2, in_=tmpg2, func=AF.Exp, scale=-1.0))
            E1("dve", lambda e: e.tensor_scalar(out=tmpg2, in0=tmpg2, scalar1=1.0, scalar2=None, op0=ALU.add))
            E1("act", lambda e: e.activation(out=tmpg2, in_=tmpg2, func=AF.Ln))
            E1("dve", lambda e: e.scalar_tensor_tensor(out=tmpg, in0=tmpg, scalar=0.0, in1=tmpg2, op0=ALU.max, op1=ALU.add))
            for t in range(NT):
                E1("dve", lambda e, t=t, d=d: e.tensor_tensor(out=gd[:, t, d, :], in0=tmpg[:, t, :],
                                                             in1=dsm[:, d * NH:(d + 1) * NH], op=ALU.mult))
            E1("act", lambda e, d=d, b_sl=b_sl: e.activation(out=bd[:, :, d, :], in_=gts[:, :, b_sl], func=AF.Sigmoid))
            mincl = cst[:, 1 if d == 0 else 3, :]
            dsl = slice(d * NN, (d + 1) * NN)
            E1("dve", lambda e, d=d: e.tensor_copy(out=tmpg2, in_=gd[:, :, d, :]))
            mm(psum[:, 0, 0:NN], mincl, tmpg2_f, True, True, [Bg, B_const], PB[0])
            mm(psum[:, 1, 0:NN], ones, tmpg2_f, True, True, [Bg, B_const], PB[1])
            CL = lambda o_, i_: (lambda e: e.tensor_scalar(out=o_, in0=i_, scalar1=-80.0, scalar2=None, op0=ALU.max))
            p.emit("dve", CL(egc_f[:, dsl], psum[:, 0, 0:NN]), [PB[0]], [Bg])
            p.emit("dve", CL(egtot_f[:, dsl], psum[:, 1, 0:NN]), [PB[1]], [Bg])
            p.emit("dve", lambda e: e.tensor_copy(out=tmpg_f, in_=psum[:, 0, 0:NN]), [PB[0]], [Bg])
            p.emit("dve", lambda e, dsl=dsl: e.tensor_tensor(out=etail_f[:, dsl], in0=psum[:, 1, 0:NN], in1=tmpg_f, op=ALU.subtract),
                   [PB[1], Bg], [Bg])
            E1("dve", CL(etail_f[:, dsl], etail_f[:, dsl]))
            E1("act", lambda e, dsl=dsl: e.activation(out=egc_f[:, dsl], in_=egc_f[:, dsl], func=AF.Exp))
            E1("act", lambda e, dsl=dsl: e.activation(out=egtot_f[:, dsl], in_=egtot_f[:, dsl], func=AF.Exp))
            E1("act", lambda e, dsl=dsl: e.activation(out=etail_f[:, dsl], in_=etail_f[:, dsl], func=AF.Exp))
            E1("dve", lambda e, dsl=dsl: e.tensor_scalar(out=negc_f[:, dsl], in0=egc_f[:, dsl], scalar1=-1.0, scalar2=None, op0=ALU.mult))
        p.barrier()
        xp = [f32v(S + 4) for _ in range(2)]
        B_xp = [Buf("xp0"), Buf("xp1")]
        xp_sem = [p.dsem("xp0"), p.dsem("xp1")]
        acc, sqb = f32v(S), f32v(S)
        B_acc, B_sqb = Buf("acc"), Buf("sqb")
        qn, kn, vs = f32v(S), f32v(S), f32v(S)
        B_qkv = [Buf("qn"), Buf("kn"), Buf("vs")]
        rsb = f32v(512)
        B_rsb = Buf("rsb")
        oacc = [f32v(NT * 128).rearrange("p (a b) -> p a b", a=NT) for _ in range(2)]
        B_oacc = [Buf("of"), Buf("ob")]
        zt = f32v(NT * 128).rearrange("p (a b) -> p a b", a=NT)
        B_z = Buf("zt")
        z_sem = p.dsem("zt")
        st2 = f32v(2 * NT)
        B_st2 = Buf("st2")
        Sst = [f32v(128), f32v(128)]
        B_S = [Buf("S0"), Buf("S1")]
        W = []
        for d in range(2):
            w_ = {n: f32v(128) for n in ("Gm", "DT", "DTi", "DTs", "X", "XT", "QKD", "Ra", "Rb", "P", "PT", "P2", "P2T", "kt",
                                         "vt", "r", "vn", "t1")}
            w_["B"] = {n: Buf(n + str(d)) for n in w_}
            W.append(w_)
        rog = Ring("rog", 2, 128, BF16)
        rot = Ring("rot", 2, 128, BF16)
        for i_ in range(2):
            p.emit("pool", lambda e, i_=i_: e.memset(xp[i_][:, 0:2], 0.0), [], [B_xp[i_]])
            p.emit("pool", lambda e, i_=i_: e.memset(xp[i_][:, S + 2:S + 4], 0.0), [], [B_xp[i_]])
        pb = [0]

        def nb():
            b = pb[0] % 8
            pb[0] += 1
            return b

        def mmf(lhsT, rhs, reads):
            b = nb()
            o_ = psum[:, b, 0:128]
            mm(o_, lhsT, rhs, True, True, reads, PB[b])
            return o_, PB[b]

        xpc = [0]
        for h in range(NH):
            for qi, (dst, Bd) in enumerate(zip((qn, kn, vs), B_qkv)):
                ch = qi * NH + h
                i_ = xpc[0] % 2
                xpc[0] += 1
                x_, Bx_ = xp[i_], B_xp[i_]
                dma("sp", x_[:, 2:S + 2], DNT[ch], xp_sem[i_], writes=[Bx_])
                eng = "dve" if qi != 1 else "pool"
                p.emit(eng, lambda e, x_=x_, ch=ch: e.tensor_scalar(out=acc, in0=x_[:, 0:S], scalar1=cw[:, ch, 0:1], scalar2=None,
                                                                   op0=ALU.mult), [Bx_, Bg], [B_acc])
                for j in range(1, 5):
                    p.emit("dve", lambda e, x_=x_, ch=ch, j=j: e.scalar_tensor_tensor(out=acc, in0=x_[:, j:j + S], scalar=cw[:, ch, j:j + 1],
                                                                                   in1=acc, op0=ALU.mult, op1=ALU.add), [Bx_, Bg, B_acc], [B_acc])
                if qi == 2:
                    p.emit("act", lambda e, dst=dst: e.activation(out=dst, in_=acc, func=AF.Silu), [B_acc], [Bd])
                    continue
                p.emit("act", lambda e: e.activation(out=acc, in_=acc, func=AF.Silu), [B_acc], [B_acc])
                p.emit("act", lambda e: e.activation(out=sqb, in_=acc, func=AF.Square), [B_acc], [B_sqb])
                for b0 in range(0, S, 512):
                    bk = nb()
                    mm(psum[:, bk, :], ones, sqb[:, b0:b0 + 512], True, True, [B_const, B_sqb], PB[bk])
                    p.emit("dve", lambda e, bk=bk: e.tensor_scalar(out=rsb, in0=psum[:, bk, :], scalar1=1e-6, scalar2=None, op0=ALU.add),
                           [PB[bk]], [B_rsb])
                    p.emit("act", lambda e: e.activation(out=rsb, in_=rsb, func=AF.Ln), [B_rsb], [B_rsb])
                    p.emit("act", lambda e: e.activation(out=rsb, in_=rsb, func=AF.Exp, scale=-0.5), [B_rsb], [B_rsb])
                    sc = float(128 ** -0.5) if qi == 0 else 1.0
                    p.emit("dve", lambda e, dst=dst, b0=b0, sc=sc: e.scalar_tensor_tensor(out=dst[:, b0:b0 + 512], in0=acc[:, b0:b0 + 512],
                                                                                       scalar=sc, in1=rsb, op0=ALU.mult, op1=ALU.mult),
                           [B_acc, B_rsb], [Bd])
            for d in range(2):
                p.emit("pool", lambda e, d=d: e.memset(Sst[d], 0.0), [], [B_S[d]])
            for n_ in range(NT):
                for d in range(2):
                    t = n_ if d == 0 else NT - 1 - n_
                    w_ = W[d]
                    B_ = w_["B"]
                    cs = slice(t * 128, (t + 1) * 128)
                    MinclG = cst[:, 1 if d == 0 else 3, :]
                    Mstr = cst[:, 2 if d == 0 else 4, :]
                    maskI = cst[:, 1 if d == 0 else 3, :]
                    maskS = cst[:, 4 if d == 0 else 2, :]
                    gcol = gd[:, t, d, h:h + 1]
                    bcol = bd[:, t, d, h:h + 1]
                    Bq, Bk, Bv = B_qkv
                    p.emit("pool", lambda e, w_=w_, MinclG=MinclG, gcol=gcol: e.tensor_scalar(out=w_["Gm"], in0=MinclG, scalar1=gcol,
                                                                                           scalar2=None, op0=ALU.mult), [B_const, Bg], [B_["Gm"]])
                    ps, Bp = mmf(Mstr, w_["Gm"], [B_const, B_["Gm"]])
                    p.emit("dve", lambda e, w_=w_, ps=ps: e.tensor_scalar(out=w_["DT"], in0=ps, scalar1=-80.0, scalar2=None, op0=ALU.max),
                           [Bp], [B_["DT"]])
                    p.emit("act", lambda e, w_=w_: e.activation(out=w_["DT"], in_=w_["DT"], func=AF.Exp), [B_["DT"]], [B_["DT"]])
                    p.emit("pool", lambda e, w_=w_, maskI=maskI: e.tensor_tensor(out=w_["DTi"], in0=w_["DT"], in1=maskI, op=ALU.mult),
                           [B_["DT"], B_const], [B_["DTi"]])
                    p.emit("pool", lambda e, w_=w_, maskS=maskS: e.tensor_tensor(out=w_["DTs"], in0=w_["DT"], in1=maskS, op=ALU.mult),
                           [B_["DT"], B_const], [B_["DTs"]])
                    ps, Bp = mmf(kn[:, cs], kn[:, cs], [Bk])
                    p.emit("dve", lambda e, w_=w_, ps=ps, bcol=bcol: e.scalar_tensor_tensor(out=w_["X"], in0=ps, scalar=bcol, in1=w_["DTs"],
                                                                                        op0=ALU.mult, op1=ALU.mult), [Bp, Bg, B_["DTs"]], [B_["X"]])
                    ps, Bp = mmf(kn[:, cs], qn[:, cs], [Bk, Bq])
                    p.emit("dve", lambda e, w_=w_, ps=ps: e.tensor_tensor(out=w_["QKD"], in0=ps, in1=w_["DTi"], op=ALU.mult),
                           [Bp, B_["DTi"]], [B_["QKD"]])
                    ps, Bp = mmf(w_["X"], ident, [B_["X"], B_const])
                    p.emit("act", lambda e, w_=w_, ps=ps: e.activation(out=w_["XT"], in_=ps, func=AF.Copy), [Bp], [B_["XT"]])
                    p.emit("dve", lambda e, w_=w_: e.tensor_tensor(out=w_["Ra"], in0=ident, in1=w_["X"], op=ALU.subtract),
                           [B_const, B_["X"]], [B_["Ra"]])
                    P_, PT_, R_ = ("X", "XT", "Ra")
                    for lvl in range(1, 8):
                        if (1 << lvl) >= 128:
                            break
                        last = (1 << (lvl + 1)) >= 128
                        nP, nPT = ("P2", "P2T") if P_ in ("X", "P") else ("P", "PT")
                        ps, Bp = mmf(w_[P_], w_[PT_], [B_[P_], B_[PT_]])
                        p.emit("act", lambda e, w_=w_, ps=ps, nPT=nPT: e.activation(out=w_[nPT], in_=ps, func=AF.Copy), [Bp], [B_[nPT]])
                        if not last:
                            ps, Bp = mmf(w_[PT_], w_[P_], [B_[P_], B_[PT_]])
                            p.emit("dve", lambda e, w_=w_, ps=ps, nP=nP: e.tensor_copy(out=w_[nP], in_=ps), [Bp], [B_[nP]])
                        nR = "Rb" if R_ == "Ra" else "Ra"
                        ps, Bp = mmf(w_[nPT], w_[R_], [B_[nPT], B_[R_]])
                        p.emit("dve", lambda e, w_=w_, ps=ps, R_=R_, nR=nR: e.tensor_tensor(out=w_[nR], in0=ps, in1=w_[R_], op=ALU.add),
                               [Bp, B_[R_]], [B_[nR]])
                        P_, PT_, R_ = nP, nPT, nR
                    ps, Bp = mmf(kn[:, cs], ident, [Bk, B_const])
                    p.emit("dve", lambda e, w_=w_, ps=ps, t=t, d=d, h=h: e.tensor_scalar(out=w_["kt"], in0=ps, scalar1=etail[:, d, t, h:h + 1],
                                                                                  scalar2=None, op0=ALU.mult), [Bp, Bg], [B_["kt"]])
                    ps, Bp = mmf(vs[:, cs], ident, [Bv, B_const])
                    p.emit("act", lambda e, w_=w_, ps=ps: e.activation(out=w_["vt"], in_=ps, func=AF.Copy), [Bp], [B_["vt"]])
                    ps, Bp = mmf(kn[:, cs], Sst[d], [Bk, B_S[d]])
                    p.emit("dve", lambda e, w_=w_, ps=ps, t=t, d=d, h=h: e.scalar_tensor_tensor(out=w_["r"], in0=ps, scalar=negc[:, d, t, h:h + 1],
                                                                                         in1=w_["vt"], op0=ALU.mult, op1=ALU.add),
                           [Bp, Bg, B_["vt"]], [B_["r"]])
                    ps, Bp = mmf(w_[R_], w_["r"], [B_[R_], B_["r"]])
                    p.emit("dve", lambda e, w_=w_, ps=ps, bcol=bcol: e.tensor_scalar(out=w_["vn"], in0=ps, scalar1=bcol, scalar2=None,
                                                                                   op0=ALU.mult), [Bp, Bg], [B_["vn"]])
                    ps, Bp = mmf(qn[:, cs], Sst[d], [Bq, B_S[d]])
                    p.emit("dve", lambda e, w_=w_, ps=ps, t=t, d=d, h=h: e.tensor_scalar(out=w_["t1"], in0=ps, scalar1=egc[:, d, t, h:h + 1],
                                                                                  scalar2=None, op0=ALU.mult), [Bp, Bg], [B_["t1"]])
                    ps, Bp = mmf(w_["QKD"], w_["vn"], [B_["QKD"], B_["vn"]])
                    p.emit("dve", lambda e, w_=w_, ps=ps, t=t, d=d, h=h: e.tensor_tensor(out=oacc[d][:, t, :], in0=ps, in1=w_["t1"], op=ALU.add),
                           [Bp, B_["t1"]], [B_oacc[d]])
                    ps, Bp = mmf(w_["kt"], w_["vn"], [B_["kt"], B_["vn"]])
                    p.emit("dve", lambda e, ps=ps, t=t, d=d, h=h: e.scalar_tensor_tensor(out=Sst[d], in0=Sst[d], scalar=egtot[:, d, t, h:h + 1],
                                                                                  in1=ps, op0=ALU.mult, op1=ALU.add), [Bp, Bg, B_S[d]], [B_S[d]])
            dma("sp", zt, ZZ[:, h * 128:(h + 1) * 128].rearrange("(a p) d -> p a d", p=128), z_sem, writes=[B_z])
            of_, ob_ = oacc
            p.emit("dve", lambda e: e.tensor_tensor(out=of_, in0=of_, in1=ob_, op=ALU.add), [B_oacc[0], B_oacc[1]], [B_oacc[0]])
            p.emit("act", lambda e: e.activation(out=ob_, in_=of_, func=AF.Square), [B_oacc[0]], [B_oacc[1]])
            p.emit("dve", lambda e: e.reduce_sum(out=st2[:, 0:NT], in_=ob_, axis=AX.X), [B_oacc[1]], [B_st2])
            rstd_from_ss(st2[:, 0:NT], st2[:, 0:NT], 128, 1e-6, [B_st2], [B_st2])
            p.emit("act", lambda e: e.activation(out=zt, in_=zt, func=AF.Silu), [B_z], [B_z])
            for t in range(NT):
                og, Bog, _ = rog.next()
                ot, Bot, sot = rot.next()
                p.emit("dve", lambda e, t=t: e.scalar_tensor_tensor(out=ob_[:, t, :], in0=of_[:, t, :], scalar=st2[:, t:t + 1], in1=nwb,
                                                                   op0=ALU.mult, op1=ALU.mult), [B_oacc[0], B_st2, Bg], [B_oacc[1]])
                p.emit("dve", lambda e, t=t, og=og: e.tensor_tensor(out=og, in0=ob_[:, t, :], in1=zt[:, t, :], op=ALU.mult),
                       [B_oacc[1], B_z], [Bog])
                bk = nb()
                pt = psum[:, bk, :].bitcast(BF16)[:, 0:128]
                p.emit("pe", lambda e, pt=pt, og=og: e.transpose(pt, og, identb), [Bog, B_identb], [PB[bk]])
                p.emit("act", lambda e, pt=pt, ot=ot: e.activation(out=ot, in_=pt, func=AF.Copy), [PB[bk]], [Bot])
                dma("sp", OT[NH + h, :, t * 128:(t + 1) * 128], ot, sot, reads=[Bot])
        p.barrier()
        o_big_holder[0] = o_big

    dbg_sem = p.dsem("dbg")

    def dbg(name, v, bufs):
        t = nc.dram_tensor(name, list(v.shape), v.dtype, kind="ExternalOutput").ap()
        dma("sp", t, v, dbg_sem, reads=bufs)

    k.phase = dict(dbg=dbg, gdn=gdn, mixer_inproj=mixer_inproj, diff_attn=diff_attn, out_proj=out_proj, mem_attn=mem_attn, boundary=boundary, ffn=ffn, evac_y=evac_y, make_ystage=make_ystage, wload=wload, mm=mm,
                   dma=dma, view=view, balloc=balloc, breset=breset, rstd_from_ss=rstd_from_ss)
    k.syms = dict(locals())
    return k


def finish(k, final_events_wait=True):
    p, nc, es = k.p, k.nc, k.es
    p.barrier()
    with nc.Block() as block:
        p.finalize(block)
    es.close()
    return nc


def blk256(w, KC):
    K, N = w.shape
    return np.ascontiguousarray(w.reshape(KC, 128, N // 256, 256).transpose(2, 1, 0, 3))


def lay_wgu(w, cfg):
    K, N2 = w.shape
    FC, KC = cfg.FC, cfg.KC
    g = w[:, :cfg.DFF].reshape(KC, 128, FC, 128)
    u = w[:, cfg.DFF:].reshape(KC, 128, FC, 128)
    gu = np.concatenate([g, u], axis=3)
    return np.ascontiguousarray(gu.transpose(2, 1, 0, 3))


def lay_wdn(w, cfg):
    FC, NKG, D = cfg.FC, cfg.NKG, cfg.D
    wp = np.zeros((NKG * 16 * 128, D), np.float32)
    wp[:cfg.DFF] = w
    a = wp.reshape(NKG, 16, 128, D // 512, 512)
    return np.ascontiguousarray(a.transpose(3, 0, 2, 1, 4))


def make_consts():
    c = np.zeros((128, 8, 128), np.float32)
    c[:, 0, :] = np.eye(128)
    i = np.arange(128)
    c[:, 1, :] = (i[:, None] <= i[None, :])
    c[:, 2, :] = (i[:, None] > i[None, :])
    c[:, 3, :] = (i[:, None] >= i[None, :])
    c[:, 4, :] = (i[:, None] < i[None, :])
    RT = np.zeros((128, 128), np.float32)
    for p_ in range(128):
        d = p_ % 64
        if d < 8:
            RT[p_ + 8, p_] = -1.0
        elif d < 16:
            RT[p_ - 8, p_] = 1.0
    c[:, 5, :] = RT
    c[:, 6, :] = 1.0
    return c


def host_mixer_inputs(cfg, w_in, w_out, lam, subln, pos):
    HW, NH, KC = cfg.HW, cfg.NH, cfg.KC
    cols = np.concatenate([np.arange(0, 2 * HW), np.arange(3 * HW, 6 * HW), np.arange(2 * HW, 3 * HW),
                           np.arange(6 * HW, 7 * HW)])
    d = {}
    d["win"] = blk256(w_in[:, cols], KC)
    wg = w_in[:, 7 * HW:]
    d["wgate"] = np.ascontiguousarray(wg.reshape(KC, 128, 4 * NH).transpose(1, 0, 2))
    d["wout"] = blk256(w_out, KC)
    d["lam"] = np.ascontiguousarray(lam.reshape(1, 256))
    d["subln"] = np.ascontiguousarray(subln.reshape(1, 128))
    d["pos"] = np.ascontiguousarray(pos.reshape(1, -1).astype(np.int32))
    invf = np.zeros((128, 1), np.float32)
    base = (np.float32(cfg.rope_theta) ** (-np.arange(0, 16, 2, dtype=np.float32) / np.float32(16))).astype(np.float32)
    for p_ in range(128):
        dd = p_ % 64
        if dd < 16:
            invf[p_, 0] = base[dd % 8]
    d["invf"] = invf
    return d


def host_gdn_inputs(cfg, conv_w, a_log, dt_bias, normw):
    NH = cfg.NH
    d = {}
    d["convw"] = np.ascontiguousarray(conv_w.T.reshape(3 * NH, 128, 5).transpose(1, 0, 2))
    d["dn_small"] = np.ascontiguousarray(np.concatenate([a_log.reshape(-1), dt_bias.reshape(-1)]).reshape(1, 4 * NH))
    d["dn_normw"] = np.ascontiguousarray(normw.reshape(1, 128))
    return d


_CACHE = {}


def build_full(cfg):
    k = build(cfg)
    ph = k.phase
    out = k.syms["out"]
    NFB, NB256 = cfg.D // 512, cfg.D // 256
    ph["ffn"](0, k.dram["x"], None, 0)
    ph["mixer_inproj"](out, (0.5, 1, NFB), 2)
    ph["diff_attn"]()
    ph["gdn"]()
    ph["out_proj"]()
    ph["mem_attn"](out, (1.0, 3, NB256), 4, 5)
    ph["ffn"](1, out, (1.0, 6, NB256), 7)
    ph["boundary"](out, (0.5, 8, NFB), 0, 0, cfg.S, want_h=False)
    return k, finish(k)


def kernel(x, mem, positions, ffn1_norms, ffn1_w_gu, ffn1_w_down, mix_norms, mix_w_in, dn_conv_w, dn_a_log,
           dn_dt_bias, dn_norm_w, diff_lambda, diff_subln_w, mix_w_out, mem_norms, mem_w_q, mem_w_kv, mem_w_o,
           ffn2_norms, ffn2_w_gu, ffn2_w_down):
    f = lambda a: np.asarray(a, dtype=np.float32)
    x = f(x)
    B, S, D = x.shape
    cfg = Cfg(D=D, S=S, DFF=f(ffn1_w_down).shape[1], MEM=np.asarray(mem).shape[1], ncores=B)
    k, nc = build_full(cfg)
    shared = {}
    shared["wgu1"] = lay_wgu(f(ffn1_w_gu)[0], cfg)
    shared["wdn1"] = lay_wdn(f(ffn1_w_down)[0], cfg)
    shared["wgu2"] = lay_wgu(f(ffn2_w_gu)[0], cfg)
    shared["wdn2"] = lay_wdn(f(ffn2_w_down)[0], cfg)
    shared.update(host_gdn_inputs(cfg, f(dn_conv_w)[0], f(dn_a_log)[0], f(dn_dt_bias)[0], f(dn_norm_w)[0]))
    shared["wmq"] = blk256(f(mem_w_q)[0], cfg.KC)
    shared["wmkv"] = blk256(f(mem_w_kv)[0], cfg.KC)
    shared["wmo"] = blk256(f(mem_w_o)[0], 4)
    shared["norms"] = np.ascontiguousarray(np.concatenate([f(ffn1_norms)[0], f(mix_norms)[0], f(mem_norms)[0][[0, 1, 2]],
                                                           f(ffn2_norms)[0]], axis=0))
    shared["consts"] = make_consts()
    pos = np.asarray(positions)
    in_maps = []
    for b in range(B):
        m = dict(shared)
        m.update(host_mixer_inputs(cfg, f(mix_w_in)[0], f(mix_w_out)[0], f(diff_lambda)[0], f(diff_subln_w)[0], pos[b])
                 if b == 0 else {kk: in_maps[0][kk] for kk in ("win", "wgate", "wout", "lam", "subln", "invf")})
        m["pos"] = np.ascontiguousarray(pos[b].reshape(1, -1).astype(np.int32))
        m["x"] = np.ascontiguousarray(x[b])
        m["mem"] = np.ascontiguousarray(f(mem)[b])
        in_maps.append(m)
    res = run_bass_kernel_spmd(nc, in_maps, core_ids=list(range(B)))
    return np.stack([np.asarray(r["out"]) for r in res.results], axis=0).astype(np.float32)
```

```python
import numpy as np
from contextlib import ExitStack
import concourse.bass as bass
import concourse.mybir as mybir
from concourse.bass_utils import run_bass_kernel_spmd

F32 = mybir.dt.float32
BF16 = mybir.dt.bfloat16
I32 = mybir.dt.int32
AF = mybir.ActivationFunctionType
ALU = mybir.AluOpType
AX = mybir.AxisListType


class Cfg:
    def __init__(self, D=4096, S=2048, DFF=11008, MEM=256, ncores=8):
        self.D, self.S, self.DFF, self.MEM, self.ncores = D, S, DFF, MEM, ncores
        self.KC = D // 128
        self.NT = S // 128
        self.T = min(512, S)
        self.NP = S // self.T
        self.TT = self.T // 128
        self.FC = DFF // 128
        self.NKG = (self.FC + 15) // 16
        self.HW = D // 2
        self.NH = self.HW // 128
        self.INCOLS = 3 * self.HW + 4 * self.HW + 4 * self.NH
        self.MH = 4
        self.lambda_init = 0.2
        self.rope_theta = 500000.0


class Buf:
    __slots__ = ("w", "r", "name")

    def __init__(self, name=""):
        self.w = None
        self.r = []
        self.name = name


class DSem:
    def __init__(self, sem):
        self.sem = sem
        self.count = 0


ENGS = ("pe", "act", "dve", "pool", "sp")


class Prog:
    def __init__(self, nc, es):
        self.nc = nc
        self.es = es
        self.ops = {e: [] for e in ENGS}
        self.dsems = []
        self.all_dma_events = []

    def dsem(self, name):
        d = DSem(self.es.enter_context(self.nc.semaphore("d_" + name)))
        self.dsems.append(d)
        return d

    def emit(self, eng, fn, reads=(), writes=(), dsem=None, acc=False):
        deps = []
        for b in reads:
            if b.w is not None:
                deps.append(b.w)
        for b in writes:
            if b.w is not None:
                if not (acc and b.w[0] == "c" and b.w[1] == "pe"):
                    deps.append(b.w)
            deps.extend(b.r)
        seq = len(self.ops[eng])
        if dsem is None:
            ev = ("c", eng, seq)
        else:
            dsem.count += 16
            ev = ("d", dsem, dsem.count)
            self.all_dma_events.append(ev)
        deps2 = []
        for d in deps:
            if d[0] == "c" and d[1] == eng and eng == "pe":
                continue
            deps2.append(d)
        self.ops[eng].append([deps2, fn, ev, dsem])
        for b in reads:
            b.r.append(ev)
        for b in writes:
            b.w = ev
            b.r = []
        return ev

    def barrier(self):
        lasts = []
        for e in ENGS:
            if self.ops[e]:
                for op in reversed(self.ops[e]):
                    if op[2][0] == "c" and op[1] is not None:
                        lasts.append(op[2])
                        break
        dm = list(self.all_dma_events)
        for e in ENGS:
            self.ops[e].append([lasts + dm, None, ("c", e, len(self.ops[e])), None])
        self.all_dma_events = []

    def finalize(self, block):
        nc = self.nc
        csem = {e: self.es.enter_context(nc.semaphore("c_" + e)) for e in ENGS}
        waited = {e: set() for e in ENGS}
        for e in ENGS:
            for deps, fn, ev, ds in self.ops[e]:
                for d in deps:
                    if d[0] == "c":
                        waited[d[1]].add(d[2])
        rank = {}
        for e in ENGS:
            r = 0
            for seq in sorted(waited[e]):
                assert self.ops[e][seq][1] is not None, "wait on a barrier pseudo-op"
                r += 1
                rank[(e, seq)] = r
        ops = self.ops

        def run(e, eng):
            have = {}
            for seq, (deps, fn, ev, ds) in enumerate(ops[e]):
                need = {}
                for d in deps:
                    if d[0] == "c":
                        key = ("c", d[1])
                        val = rank[(d[1], d[2])]
                        sem = csem[d[1]]
                    else:
                        key = ("d", id(d[1]))
                        val = d[2]
                        sem = d[1].sem
                    if have.get(key, 0) >= val:
                        continue
                    if key not in need or need[key][1] < val:
                        need[key] = (sem, val)
                for key, (sem, val) in need.items():
                    eng.wait_ge(sem, val)
                    have[key] = val
                if fn is None:
                    continue
                ins = fn(eng)
                if ds is not None:
                    ins.then_inc(ds.sem, 16)
                elif (e, seq) in rank:
                    ins.then_inc(csem[e], 1)

        block.tensor(lambda eng: run("pe", eng))
        block.scalar(lambda eng: run("act", eng))
        block.vector(lambda eng: run("dve", eng))
        block.gpsimd(lambda eng: run("pool", eng))
        block.sync(lambda eng: run("sp", eng))


class K:
    def __init__(self, cfg):
        self.cfg = cfg
        self.nc = bass.Bass("TRN2", target_bir_lowering=False)
        self.es = ExitStack()
        self.p = Prog(self.nc, self.es)
        self.dram = {}

    def din(self, name, shape, dt=F32):
        t = self.nc.dram_tensor(name, list(shape), dt, kind="ExternalInput").ap()
        self.dram[name] = t
        return t

    def dscr(self, name, shape, dt=F32):
        t = self.nc.dram_tensor(name, list(shape), dt,
                                kind="ExternalOutput" if getattr(self.cfg, "debug", False) else "Internal").ap()
        self.dram[name] = t
        return t


def build(cfg):
    k = K(cfg)
    nc, es, p = k.nc, k.es, k.p
    D, S, DFF, KC, NT, T, NP, TT, FC, NKG = (cfg.D, cfg.S, cfg.DFF, cfg.KC, cfg.NT, cfg.T, cfg.NP,
                                             cfg.TT, cfg.FC, cfg.NKG)
    HW, NH, MEM = cfg.HW, cfg.NH, cfg.MEM
    NB256 = D // 256
    NFB = D // 512
    EPS = 1e-6

    x_in = k.din("x", [S, D])
    mem_in = k.din("mem", [MEM, D])
    pos_in = k.din("pos", [1, S], I32)
    out = nc.dram_tensor("out", [S, D], F32, kind="ExternalOutput").ap()
    wgu = [k.din(f"wgu{i}", [FC, 128, KC, 256]) for i in (1, 2)]
    wdn = [k.din(f"wdn{i}", [NFB, NKG, 128, 16, 512]) for i in (1, 2)]
    NBIN = (3 * HW + 4 * HW) // 256
    win = k.din("win", [NBIN, 128, KC, 256])
    wgate = k.din("wgate", [128, KC, 4 * NH])
    wout = k.din("wout", [NB256, 128, KC, 256])
    wmq = k.din("wmq", [2, 128, KC, 256])
    wmkv = k.din("wmkv", [4, 128, KC, 256])
    wmo = k.din("wmo", [NB256, 128, 4, 256])
    norms = k.din("norms", [9, D])
    convw = k.din("convw", [128, 3 * NH, 5])
    dn_small = k.din("dn_small", [1, 4 * NH])
    dn_normw = k.din("dn_normw", [1, 128])
    subln = k.din("subln", [1, 128])
    lam_in = k.din("lam", [1, 256])
    consts = k.din("consts", [128, 8, 128])
    invf = k.din("invf", [128, 1])
    Y = k.dscr("Y", [S, D])
    QT = k.dscr("QT", [NH, 128, S], BF16)
    KT = k.dscr("KT", [NH, 128, S], BF16)
    VV = k.dscr("VV", [S, HW], BF16)
    DNT = k.dscr("DNT", [3 * NH, 128, S])
    ZZ = k.dscr("ZZ", [S, HW])
    GG = k.dscr("GG", [S, 4 * NH])
    OT = k.dscr("OT", [KC, 128, S], BF16)
    MQT = k.dscr("MQT", [4, 128, S], BF16)

    ARENA_F32 = 46000
    arena = es.enter_context(nc.sbuf_tensor("arena", [128, ARENA_F32], F32))
    psum = es.enter_context(nc.psum_tensor("psum", [128, 8, 512], F32))
    cursor = [0]

    def alloc(nbytes):
        off = cursor[0]
        cursor[0] += (nbytes + 31) // 32 * 32
        assert cursor[0] <= ARENA_F32 * 4, f"arena overflow {cursor[0]}"
        return off

    def view(off, n_elems, dt):
        nb = n_elems * mybir.dt.size(dt)
        assert off % 4 == 0 and nb % 4 == 0
        v = arena[:, off // 4:(off + nb) // 4]
        return v if dt == F32 else v.bitcast(dt)

    o_const = alloc(8 * 128 * 4)
    cst = view(o_const, 8 * 128, F32).rearrange("p (a b) -> p a b", a=8)
    ident, ones = cst[:, 0, :], cst[:, 6, :]
    o_identb = alloc(128 * 2)
    identb = view(o_identb, 128, BF16)
    o_ss = alloc(NT * 16 * 4)
    ssparts = view(o_ss, NT * 16, F32).rearrange("p (a b) -> p a b", a=NT)
    o_small = alloc(64 * 4)
    small = view(o_small, 64, F32)
    B_const, B_ss, B_small, B_identb = Buf("const"), Buf("ss"), Buf("small"), Buf("identb")
    PB = [Buf(f"ps{i}") for i in range(8)]
    o_hT = alloc(KC * T * 2)
    hT = view(o_hT, KC * T, BF16).rearrange("p (a b) -> p a b", a=KC)
    B_hT = Buf("hT")
    NSLOT = 2
    o_ws = [alloc(8192 * 2) for _ in range(NSLOT)]
    wslot = [view(o, 8192, BF16) for o in o_ws]
    B_ws = [Buf(f"ws{i}") for i in range(NSLOT)]
    ws_sem = [p.dsem(f"ws{i}") for i in range(NSLOT)]
    ws_ctr = [0]
    o_big = cursor[0]
    big_cursor = [o_big]

    def balloc(nbytes):
        off = big_cursor[0]
        big_cursor[0] += (nbytes + 31) // 32 * 32
        assert big_cursor[0] <= ARENA_F32 * 4, f"big arena overflow {big_cursor[0]} {ARENA_F32*4}"
        return off

    o_big_holder = [o_big]

    def breset():
        big_cursor[0] = o_big_holder[0]
        del xslots[:]

    misc_sem = p.dsem("misc")

    def dma(eng, out_ap, in_ap, dsem, reads=(), writes=()):
        return p.emit(eng, lambda e: e.dma_start(out=out_ap, in_=in_ap), reads, writes, dsem=dsem)

    o_epsv = alloc(8 * 4)
    epsv = view(o_epsv, 8, F32)
    B_eps = Buf("eps")
    EPSV = {1e-6: 0, 1e-5: 1, 1.0: 2, 0.0: 3}
    for val_, col_ in EPSV.items():
        p.emit("dve", lambda e, val_=val_, col_=col_: e.memset(epsv[:, col_:col_ + 1], float(val_)), [], [B_eps])

    def epsc(v):
        c = EPSV[v]
        return epsv[:, c:c + 1]

    dma("sp", cst, consts, misc_sem, writes=[B_const])
    p.emit("dve", lambda e: e.tensor_copy(out=identb, in_=ident), [B_const], [B_identb])

    NXS = 4
    xs_sem = [p.dsem(f"xs{i}") for i in range(NXS)]
    xslots = []

    def set_xslots(n):
        del xslots[:]
        for i in range(n):
            xslots.append((view(balloc(8192 * 2), 8192, BF16), Buf(f"xs{i}"), xs_sem[i]))

    def wload(src_ap, shape_str=None, **kw):
        nsl = NSLOT + len(xslots)
        i = ws_ctr[0] % nsl
        ws_ctr[0] += 1
        n = 1
        for s_ in src_ap.shape[1:]:
            n *= s_
        assert n <= 8192
        if i >= NSLOT:
            xv, xb, xsm = xslots[i - NSLOT]
            dst = xv[:, 0:n]
            if len(src_ap.shape) == 3:
                dst = dst.rearrange("p (a b) -> p a b", a=src_ap.shape[1])
            p.emit("pool", lambda e: e.dma_start(out=dst, in_=src_ap, max_dma_last_dim=8192), [], [xb], dsem=xsm)
            return dst, xb
        dst = wslot[i][:, 0:n]
        if len(src_ap.shape) == 3:
            dst = dst.rearrange("p (a b) -> p a b", a=src_ap.shape[1])
        p.emit("pool", lambda e: e.dma_start(out=dst, in_=src_ap, max_dma_last_dim=8192), [], [B_ws[i]],
               dsem=ws_sem[i])
        return dst, B_ws[i]

    def mm(out_ap, lhsT, rhs, start, stop, reads, wbuf):
        p.emit("pe", lambda e: e.matmul(out_ap, lhsT, rhs, start=start, stop=stop), reads, [wbuf],
               acc=not start)

    def rstd_from_ss(dst, src, n, eps, rb, wb, eng="dve"):
        p.emit("dve", lambda e: e.tensor_scalar(out=dst, in0=src, scalar1=1.0 / n, scalar2=float(eps), op0=ALU.mult,
                                                op1=ALU.add), rb, wb)
        p.emit("act", lambda e: e.activation(out=dst, in_=dst, func=AF.Ln), wb, wb)
        p.emit("act", lambda e: e.activation(out=dst, in_=dst, func=AF.Exp, scale=-0.5), wb, wb)

    def boundary(x_src, y_info, pre_row, tok0, ntok, want_h=True, write_x=True):
        breset()
        o_xt, o_yt = balloc(D * 4), balloc(D * 4)
        o_wpost, o_wpre, o_xn = balloc(D * 4), balloc(D * 4), balloc(D * 2)
        o_junk = balloc(D * 4)
        xt, yt = view(o_xt, D, F32), view(o_yt, D, F32)
        wpost, wpre, xn = view(o_wpost, D, F32), view(o_wpre, D, F32), view(o_xn, D, BF16)
        junk = view(o_junk, D, F32)
        Bx, By, Bwpo, Bwpr, Bxn, Bj = Buf("xt"), Buf("yt"), Buf("wpost"), Buf("wpre"), Buf("xn"), Buf("junk")
        Bst = Buf("stat")
        st = small
        sx, sy, sw, sxs = (boundary.sems[i] for i in range(4))
        if y_info is not None:
            coef, post_row, nparts = y_info
            dma("sp", wpost, norms[post_row:post_row + 1, :].partition_broadcast(128), sw, writes=[Bwpo])
        if want_h:
            dma("sp", wpre, norms[pre_row:pre_row + 1, :].partition_broadcast(128), sw, writes=[Bwpr])
        for tt in range(ntok // 128):
            r0 = tok0 + tt * 128
            gt = r0 // 128
            dma("sp", xt, x_src[r0:r0 + 128, :], sx, writes=[Bx])
            if y_info is not None:
                dma("sp", yt, Y[r0:r0 + 128, :], sy, writes=[By])
                p.emit("dve", lambda e, gt=gt: e.reduce_sum(out=st[:, 2:3], in_=ssparts[:, gt, 0:nparts], axis=AX.X),
                       [B_ss], [Bst])
                rstd_from_ss(st[:, 3:4], st[:, 2:3], D, EPS, [Bst], [Bst])
                if coef != 1.0:
                    p.emit("dve", lambda e: e.tensor_scalar(out=st[:, 3:4], in0=st[:, 3:4], scalar1=float(coef),
                                                            scalar2=None, op0=ALU.mult), [Bst], [Bst])
                p.emit("pool", lambda e: e.tensor_tensor(out=yt, in0=yt, in1=wpost, op=ALU.mult), [By, Bwpo], [By])
                p.emit("dve", lambda e: e.scalar_tensor_tensor(out=xt, in0=yt, scalar=st[:, 3:4], in1=xt,
                                                               op0=ALU.mult, op1=ALU.add), [By, Bx, Bst], [Bx])
            if write_x and (y_info is not None or x_src is not out):
                dma("sp", out[r0:r0 + 128, :], xt, sxs, reads=[Bx])
            if not want_h:
                continue
            p.emit("act", lambda e: e.activation(out=junk, in_=xt, func=AF.Square), [Bx], [Bj])
            p.emit("dve", lambda e: e.reduce_sum(out=st[:, 0:1], in_=junk, axis=AX.X), [Bj], [Bst])
            rstd_from_ss(st[:, 1:2], st[:, 0:1], D, EPS, [Bst], [Bst])
            p.emit("dve", lambda e: e.scalar_tensor_tensor(out=xn, in0=xt, scalar=st[:, 1:2], in1=wpre,
                                                           op0=ALU.mult, op1=ALU.mult), [Bx, Bst, Bwpr], [Bxn])
            for g in range(0, KC, 8):
                ng = min(8, KC - g)
                bank = (g // 8) % 2
                pt = psum[:, bank, :].bitcast(BF16)[:, 0:ng * 128].rearrange("p (a b) -> p a b", a=ng)
                for j in range(ng):
                    p.emit("pe", lambda e, j=j, g=g, pt=pt: e.transpose(pt[:, j, :], xn[:, (g + j) * 128:(g + j + 1) * 128],
                                                                         identb),
                           [Bxn, B_identb], [PB[bank]], acc=(j > 0))
                eng = "act" if (g // 8) % 2 == 0 else "dve"
                dst = hT[:, g:g + ng, tt * 128:(tt + 1) * 128]
                if eng == "act":
                    p.emit("act", lambda e, dst=dst, pt=pt: e.activation(out=dst, in_=pt, func=AF.Copy),
                           [PB[bank]], [B_hT])
                else:
                    p.emit("dve", lambda e, dst=dst, pt=pt: e.tensor_copy(out=dst, in_=pt), [PB[bank]], [B_hT])

    boundary.sems = [p.dsem(n) for n in ("bx", "by", "bw", "bxs")]

    ystage_sem = [p.dsem("ys0"), p.dsem("ys1")]

    def evac_y(ps_ap, bank, r0, c0, ncols, col_idx, stg):
        i = evac_y.ctr % 2
        evac_y.ctr += 1
        ys, Bys, jk, Bjk = stg[i]
        p.emit("dve", lambda e: e.tensor_copy(out=ys[:, 0:ncols], in_=ps_ap), [PB[bank]], [Bys])
        p.emit("act", lambda e: e.activation(out=jk[:, 0:ncols], in_=ys[:, 0:ncols], func=AF.Square), [Bys], [Bjk])
        p.emit("dve", lambda e: e.reduce_sum(out=ssparts[:, r0 // 128, col_idx:col_idx + 1], in_=jk[:, 0:ncols],
                                             axis=AX.X), [Bjk], [B_ss])
        dma("sp", Y[r0:r0 + 128, c0:c0 + ncols], ys[:, 0:ncols], ystage_sem[i], reads=[Bys])

    evac_y.ctr = 0

    def make_ystage():
        stg = []
        for i in range(2):
            o1, o2 = balloc(512 * 4), balloc(512 * 4)
            stg.append((view(o1, 512, F32), Buf(f"ys{i}"), view(o2, 512, F32), Buf(f"jk{i}")))
        return stg

    def ffn(idx, x_src, y_prev, pre_row):
        for ps_ in range(NP):
            tok0 = ps_ * T
            boundary(x_src, y_prev, pre_row, tok0, T)
            p.barrier()
            breset()
            o_act = balloc(FC * T * 2)
            actT = view(o_act, FC * T, BF16).rearrange("p (a b) -> p a b", a=FC)
            B_act = Buf("actT")
            o_sg = [balloc(T * 4) for _ in range(2)]
            sg = [view(o, T, F32) for o in o_sg]
            B_sg = [Buf("sg0"), Buf("sg1")]
            stg = make_ystage()
            for j in range(FC):
                wv, wb = wload(wgu[idx][j])
                bg, bu = (2 * j) % 4, (2 * j) % 4 + 1
                for kc in range(KC):
                    mm(psum[:, bg, 0:T], wv[:, kc, 0:128], hT[:, kc, :], kc == 0, kc == KC - 1, [wb, B_hT], PB[bg])
                for kc in range(KC):
                    mm(psum[:, bu, 0:T], wv[:, kc, 128:256], hT[:, kc, :], kc == 0, kc == KC - 1, [wb, B_hT], PB[bu])
                si = j % 2
                p.emit("act", lambda e, si=si, bg=bg: e.activation(out=sg[si], in_=psum[:, bg, 0:T], func=AF.Silu),
                       [PB[bg]], [B_sg[si]])
                p.emit("dve", lambda e, si=si, bu=bu, j=j: e.tensor_tensor(out=actT[:, j, :], in0=sg[si],
                                                                          in1=psum[:, bu, 0:T], op=ALU.mult),
                       [B_sg[si], PB[bu]], [B_act])
            for fb in range(NFB):
                for kg in range(NKG):
                    nk = min(16, FC - kg * 16)
                    wv, wb = wload(wdn[idx][fb, kg][:, 0:nk, :])
                    for kk in range(nk):
                        kc = kg * 16 + kk
                        for tt in range(TT):
                            bank = 4 * (fb % 2) + tt
                            mm(psum[:, bank, :], actT[:, kc, tt * 128:(tt + 1) * 128], wv[:, kk, :], kc == 0,
                               kc == FC - 1, [wb, B_act], PB[bank])
                for tt in range(TT):
                    bank = 4 * (fb % 2) + tt
                    evac_y(psum[:, bank, :], bank, tok0 + tt * 128, fb * 512, 512, fb, stg)
            p.barrier()


    def proj_fm(wblk_ap, nchunk, consume, ntok=T, hsrc=None, Bh=None, kcn=KC, banks=(2, 3)):
        hsrc = hT if hsrc is None else hsrc
        Bh = B_hT if Bh is None else Bh
        wv, wb = wload(wblk_ap)
        for j in range(nchunk):
            bank = banks[proj_fm.ctr % len(banks)]
            proj_fm.ctr += 1
            for kc in range(kcn):
                mm(psum[:, bank, 0:ntok], wv[:, kc, j * 128:(j + 1) * 128], hsrc[:, kc, 0:ntok], kc == 0, kc == kcn - 1,
                   [wb, Bh], PB[bank])
            consume(j, psum[:, bank, 0:ntok], bank)

    proj_fm.ctr = 0

    def proj_tm(wblk_ap, ncols, consume, ntok=T, hsrc=None, Bh=None, kcn=KC, banks=(4, 5, 6, 7), wv_wb=None):
        hsrc = hT if hsrc is None else hsrc
        Bh = B_hT if Bh is None else Bh
        wv, wb = wload(wblk_ap) if wv_wb is None else wv_wb
        for tt in range(ntok // 128):
            bank = banks[proj_tm.ctr % len(banks)]
            proj_tm.ctr += 1
            for kc in range(kcn):
                mm(psum[:, bank, 0:ncols], hsrc[:, kc, tt * 128:(tt + 1) * 128], wv[:, kc, 0:ncols], kc == 0,
                   kc == kcn - 1, [wb, Bh], PB[bank])
            consume(tt, psum[:, bank, 0:ncols], bank)

    proj_tm.ctr = 0

    class Ring:
        def __init__(self, name, n, nelem, dt):
            self.items = []
            for i in range(n):
                o = balloc(nelem * mybir.dt.size(dt))
                self.items.append((view(o, nelem, dt), Buf(f"{name}{i}"), Ring.sems(name, i)))
            self.i = 0

        def next(self):
            it = self.items[self.i % len(self.items)]
            self.i += 1
            return it

        _sems = {}

        @staticmethod
        def sems(name, i):
            key = (name, i)
            if key not in Ring._sems:
                Ring._sems[key] = p.dsem(f"{name}{i}")
            return Ring._sems[key]

    Ring._sems = {}

    def copy_evac(i, dst, src, reads, writes):
        if i % 2 == 0:
            p.emit("act", lambda e: e.activation(out=dst, in_=src, func=AF.Copy), reads, writes)
        else:
            p.emit("dve", lambda e: e.tensor_copy(out=dst, in_=src), reads, writes)

    def mem_attn(x_src, y_prev, pre_row, memn_row):
        MH = 4
        boundary(mem_in, None, memn_row, 0, MEM, write_x=False)
        p.barrier()
        breset()
        o_mk, o_mv = balloc(MH * MEM * 2), balloc((MEM // 128) * MH * 128 * 2)
        memK = view(o_mk, MH * MEM, BF16).rearrange("p (a b) -> p a b", a=MH)
        memV = view(o_mv, (MEM // 128) * MH * 128, BF16).rearrange("p (a b c) -> p a b c", a=MEM // 128, b=MH)
        B_mk, B_mv = Buf("memK"), Buf("memV")
        o_ones = balloc(128 * 2)
        onesb = view(o_ones, 128, BF16)
        B_onesb = Buf("onesb")
        p.emit("dve", lambda e: e.tensor_copy(out=onesb, in_=ones), [B_const], [B_onesb])
        for blk in range(2):
            def consK(j, ps, bank, blk=blk):
                h = blk * 2 + j
                copy_evac(h, memK[:, h, :], ps, [PB[bank]], [B_mk])
            proj_fm(wmkv[blk], 2, consK, ntok=MEM)
        for blk in range(2):
            def consV(tt, ps, bank, blk=blk):
                copy_evac(tt, memV[:, tt, 2 * blk:2 * blk + 2, :], ps.rearrange("p (a b) -> p a b", a=2),
                          [PB[bank]], [B_mv])
            proj_tm(wmkv[2 + blk], 256, consV, ntok=MEM)
        p.barrier()
        keep = big_cursor[0]
        scale = 128 ** -0.5
        for ps_ in range(NP):
            tok0 = ps_ * T
            big_cursor[0] = keep
            boundary_keep(x_src, y_prev, pre_row, tok0, T, keep)
            p.barrier()
            big_cursor[0] = keep
            o_q, o_oc = balloc(MH * T * 2), balloc(MH * T * 2)
            qT = view(o_q, MH * T, BF16).rearrange("p (a b) -> p a b", a=MH)
            ocT = view(o_oc, MH * T, BF16).rearrange("p (a b) -> p a b", a=MH)
            B_q, B_oc = Buf("mqT"), Buf("ocT")
            o_E = [balloc(T * 2) for _ in range(2)]
            E = [view(o, T, BF16) for o in o_E]
            B_E = [Buf("E0"), Buf("E1")]
            o_r = balloc(T * 4)
            rr = view(o_r, T, F32)
            B_r = Buf("rr")
            stg = make_ystage()
            for blk in range(2):
                def consQ(j, ps, bank, blk=blk):
                    h = blk * 2 + j
                    copy_evac(h, qT[:, h, :], ps, [PB[bank]], [B_q])
                proj_fm(wmq[blk], 2, consQ)
            for h in range(MH):
                for mt in range(MEM // 128):
                    sb = mt % 2
                    mm(psum[:, sb, 0:T], memK[:, h, mt * 128:(mt + 1) * 128], qT[:, h, :], True, True, [B_mk, B_q], PB[sb])
                    p.emit("act", lambda e, sb=sb: e.activation(out=E[sb], in_=psum[:, sb, 0:T], func=AF.Exp, scale=scale),
                           [PB[sb]], [B_E[sb]])
                    mm(psum[:, 2, 0:T], memV[:, mt, h, :], E[sb], mt == 0, mt == MEM // 128 - 1, [B_mv, B_E[sb]], PB[2])
                    mm(psum[:, 3, 0:T], onesb, E[sb], mt == 0, mt == MEM // 128 - 1, [B_onesb, B_E[sb]], PB[3])
                p.emit("dve", lambda e: e.reciprocal(out=rr, in_=psum[:, 3, 0:T]), [PB[3]], [B_r])
                p.emit("dve", lambda e, h=h: e.tensor_tensor(out=ocT[:, h, :], in0=psum[:, 2, 0:T], in1=rr, op=ALU.mult),
                       [PB[2], B_r], [B_oc])
            for blk in range(NB256):
                def consO(tt, ps, bank, blk=blk, tok0=tok0):
                    evac_y(ps, bank, tok0 + tt * 128, blk * 256, 256, blk, stg)
                proj_tm(wmo[blk], 256, consO, hsrc=ocT, Bh=B_oc, kcn=4)
            p.barrier()

    def boundary_keep(x_src, y_prev, pre_row, tok0, ntok, keep):
        saved = o_big_holder[0]
        o_big_holder[0] = keep
        try:
            boundary(x_src, y_prev, pre_row, tok0, ntok)
        finally:
            o_big_holder[0] = saved


    def mixer_inproj(x_src, y_prev, pre_row):
        import math
        NQ = HW // 256
        breset()
        o_cos, o_sin = balloc(S * 4), balloc(S * 4)
        COS, SIN = view(o_cos, S, F32), view(o_sin, S, F32)
        B_cs = Buf("cossin")
        o_wg = balloc(KC * 4 * NH * 2)
        wg = view(o_wg, KC * 4 * NH, BF16).rearrange("p (a b) -> p a b", a=KC)
        B_wg = Buf("wg")
        o_invf = balloc(4)
        invf_t = view(o_invf, 1, F32)
        keep = big_cursor[0]
        o_pi, o_pf = balloc(S * 4), balloc(S * 4)
        posi, posf = view(o_pi, S, I32), view(o_pf, S, F32)
        B_pi, B_pf = Buf("posi"), Buf("posf")
        dma("sp", posi, pos_in.partition_broadcast(128), misc_sem, writes=[B_pi])
        dma("sp", invf_t, invf, misc_sem, writes=[B_cs])
        p.emit("pool", lambda e: e.dma_start(out=wg, in_=wgate), [], [B_wg], dsem=misc_sem)
        p.emit("dve", lambda e: e.tensor_copy(out=posf, in_=posi), [B_pi], [B_pf])
        p.emit("dve", lambda e: e.tensor_scalar(out=posf, in0=posf, scalar1=invf_t[:, 0:1], scalar2=None, op0=ALU.mult),
               [B_pf, B_cs], [B_pf])
        TWO_PI = 2.0 * math.pi
        o_kf = balloc(S * 4)
        kf = view(o_kf, S, F32)
        B_kf = Buf("kf")
        TS = lambda **kw: (lambda e: e.tensor_scalar(**kw))
        for tab, shift in ((SIN, 0.0), (COS, 0.25)):
            p.emit("dve", TS(out=tab, in0=posf, scalar1=1.0 / TWO_PI, scalar2=shift, op0=ALU.mult, op1=ALU.add), [B_pf], [B_cs])
            p.emit("dve", lambda e, tab=tab: e.tensor_copy(out=posi, in_=tab), [B_cs], [B_pi])
            p.emit("dve", lambda e: e.tensor_copy(out=kf, in_=posi), [B_pi], [B_kf])
            p.emit("dve", lambda e, tab=tab: e.tensor_tensor(out=tab, in0=tab, in1=kf, op=ALU.subtract), [B_cs, B_kf], [B_cs])
            p.emit("dve", TS(out=kf, in0=tab, scalar1=0.5, scalar2=None, op0=ALU.is_gt), [B_cs], [B_kf])
            p.emit("dve", lambda e, tab=tab: e.tensor_tensor(out=tab, in0=tab, in1=kf, op=ALU.subtract), [B_cs, B_kf], [B_cs])
            p.emit("dve", TS(out=kf, in0=tab, scalar1=-0.5, scalar2=None, op0=ALU.is_lt), [B_cs], [B_kf])
            p.emit("dve", lambda e, tab=tab: e.tensor_tensor(out=tab, in0=tab, in1=kf, op=ALU.add), [B_cs, B_kf], [B_cs])
            p.emit("dve", TS(out=tab, in0=tab, scalar1=0.49999, scalar2=-0.49999, op0=ALU.min, op1=ALU.max), [B_cs], [B_cs])
            p.emit("act", lambda e, tab=tab: e.activation(out=tab, in_=tab, func=AF.Sin, scale=TWO_PI), [B_cs], [B_cs])
        p.barrier()
        RT = cst[:, 5, :]
        for ps_ in range(NP):
            tok0 = ps_ * T
            boundary_keep(x_src, y_prev, pre_row, tok0, T, keep)
            p.barrier()
            big_cursor[0] = keep
            rq = Ring("rq", 2, T, F32)
            rt1 = Ring("rt1", 2, T, F32)
            rt2 = Ring("rt2", 2, T, F32)
            rqo = Ring("rqo", 2, T, BF16)
            rdn = Ring("rdn", 2, T, F32)
            rv = Ring("rv", 2, 256, BF16)
            rz = Ring("rz", 2, 256, F32)
            rg = Ring("rg", 2, 4 * NH, F32)
            del xslots[:]
            set_xslots(3)
            for blk in range(2 * NQ):
                dst_t = QT if blk < NQ else KT

                def consQK(j, ps, bank, blk=blk, dst_t=dst_t, tok0=tok0):
                    c = (blk % NQ) * 2 + j
                    qs, Bq, _ = rq.next()
                    t1, Bt1, _ = rt1.next()
                    t2, Bt2, _ = rt2.next()
                    qo, Bqo, sqo = rqo.next()
                    p.emit("act", lambda e: e.activation(out=qs, in_=ps, func=AF.Copy), [PB[bank]], [Bq])
                    mm(psum[:, 0, 0:T], RT, qs, True, True, [B_const, Bq], PB[0])
                    p.emit("pool", lambda e: e.tensor_tensor(out=t1, in0=qs, in1=COS[:, tok0:tok0 + T], op=ALU.mult),
                           [Bq, B_cs], [Bt1])
                    p.emit("dve", lambda e: e.tensor_tensor(out=t2, in0=psum[:, 0, 0:T], in1=SIN[:, tok0:tok0 + T],
                                                            op=ALU.mult), [PB[0], B_cs], [Bt2])
                    p.emit("pool", lambda e: e.tensor_tensor(out=qo, in0=t1, in1=t2, op=ALU.add), [Bt1, Bt2], [Bqo])
                    dma("sp", dst_t[c, :, tok0:tok0 + T], qo, sqo, reads=[Bqo])
                proj_fm(win[blk], 2, consQK)
            for blk in range(3 * NQ):
                def consDN(j, ps, bank, blk=blk, tok0=tok0):
                    c = blk * 2 + j
                    d, Bd, sd = rdn.next()
                    copy_evac(c, d, ps, [PB[bank]], [Bd])
                    dma("sp", DNT[c, :, tok0:tok0 + T], d, sd, reads=[Bd])
                proj_fm(win[2 * NQ + blk], 2, consDN)
            for blk in range(NQ):
                def consV(tt, ps, bank, blk=blk, tok0=tok0):
                    v, Bv, sv = rv.next()
                    copy_evac(tt, v, ps, [PB[bank]], [Bv])
                    dma("sp", VV[tok0 + tt * 128:tok0 + (tt + 1) * 128, blk * 256:(blk + 1) * 256], v, sv, reads=[Bv])
                proj_tm(win[5 * NQ + blk], 256, consV)
            for blk in range(NQ):
                def consZ(tt, ps, bank, blk=blk, tok0=tok0):
                    z, Bz, sz = rz.next()
                    copy_evac(tt, z, ps, [PB[bank]], [Bz])
                    dma("sp", ZZ[tok0 + tt * 128:tok0 + (tt + 1) * 128, blk * 256:(blk + 1) * 256], z, sz, reads=[Bz])
                proj_tm(win[6 * NQ + blk], 256, consZ)

            def consG(tt, ps, bank, tok0=tok0):
                g, Bg, sg_ = rg.next()
                copy_evac(tt, g, ps, [PB[bank]], [Bg])
                dma("sp", GG[tok0 + tt * 128:tok0 + (tt + 1) * 128, :], g, sg_, reads=[Bg])
            proj_tm(None, 4 * NH, consG, wv_wb=(wg, B_wg))
            p.barrier()

    def diff_attn():
        breset()
        lam0 = cfg.lambda_init
        o_lp = balloc(256 * 4)
        lp = view(o_lp, 256, F32)
        o_lm = balloc(16 * 4)
        lm = view(o_lm, 16, F32)
        B_lp, B_lm = Buf("lp"), Buf("lm")
        o_sw = balloc(8)
        sw = view(o_sw, 2, F32)
        B_sw = Buf("sw")
        o_ones = balloc(128 * 2)
        onesb = view(o_ones, 128, BF16)
        B_onesb = Buf("onesb")
        p.emit("dve", lambda e: e.tensor_copy(out=onesb, in_=ones), [B_const], [B_onesb])
        dma("sp", lp, lam_in.partition_broadcast(128), misc_sem, writes=[B_lp])
        dma("sp", sw[:, 0:1], subln.rearrange("o d -> d o"), misc_sem, writes=[B_sw])
        p.emit("dve", lambda e: e.tensor_scalar(out=sw[:, 1:2], in0=sw[:, 0:1], scalar1=float(1.0 - lam0), scalar2=None,
                                                op0=ALU.mult), [B_sw], [B_sw])
        p.emit("dve", lambda e: e.tensor_tensor(out=lp[:, 0:64], in0=lp[:, 0:64], in1=lp[:, 64:128], op=ALU.mult), [B_lp], [B_lp])
        p.emit("dve", lambda e: e.tensor_tensor(out=lp[:, 128:192], in0=lp[:, 128:192], in1=lp[:, 192:256], op=ALU.mult), [B_lp], [B_lp])
        p.emit("dve", lambda e: e.reduce_sum(out=lm[:, 0:1], in_=lp[:, 0:64], axis=AX.X), [B_lp], [B_lm])
        p.emit("dve", lambda e: e.reduce_sum(out=lm[:, 1:2], in_=lp[:, 128:192], axis=AX.X), [B_lp], [B_lm])
        p.emit("act", lambda e: e.activation(out=lm[:, 2:4], in_=lm[:, 0:2], func=AF.Exp), [B_lm], [B_lm])
        p.emit("dve", lambda e: e.tensor_tensor(out=lm[:, 4:5], in0=lm[:, 3:4], in1=lm[:, 2:3], op=ALU.subtract), [B_lm], [B_lm])
        p.emit("dve", lambda e: e.tensor_scalar(out=lm[:, 4:5], in0=lm[:, 4:5], scalar1=-float(lam0), scalar2=None, op0=ALU.add),
               [B_lm], [B_lm])
        o_q, o_k, o_v = balloc(S * 2), balloc(S * 2), balloc(NT * 128 * 2)
        qt, kt = view(o_q, S, BF16), view(o_k, S, BF16)
        qz = [view(balloc(S * 2), S, BF16) for _ in range(2)]
        B_qz = [Buf("qz0"), Buf("qz1")]
        hm = view(balloc(8), 2, F32)
        B_hm = Buf("hm")
        p.emit("dve", lambda e: e.memset(hm, 0.0), [], [B_hm])
        p.emit("dve", lambda e: e.memset(hm[0:64, 0:1], 1.0), [B_hm], [B_hm])
        p.emit("dve", lambda e: e.memset(hm[64:128, 1:2], 1.0), [B_hm], [B_hm])
        vt = view(o_v, NT * 128, BF16).rearrange("p (a b) -> p a b", a=NT)
        B_q, B_k, B_v = Buf("dq"), Buf("dk"), Buf("dv")
        sq_, sk_, sv_ = p.dsem("dq"), p.dsem("dk"), p.dsem("dv")
        o_E = [balloc(T * 2) for _ in range(2)]
        E = [view(o, T, BF16) for o in o_E]
        B_E = [Buf("E0"), Buf("E1")]
        f32t = lambda: view(balloc(T * 4), T, F32)
        r1, o1, o2, sqv, rs = f32t(), f32t(), f32t(), f32t(), f32t()
        B_r1, B_o1, B_o2, B_sqv, B_rs = Buf("r1"), Buf("o1"), Buf("o2"), Buf("sqv"), Buf("rs")
        ron = Ring("ron", 2, T, BF16)
        for c in range(NH):
            dma("sp", qt, QT[c], sq_, writes=[B_q])
            dma("sp", kt, KT[c], sk_, writes=[B_k])
            dma("sp", vt, VV[:, c * 128:(c + 1) * 128].rearrange("(a p) d -> p a d", p=128), sv_, writes=[B_v])
            for a in range(2):
                eng = "dve" if a == 0 else "pool"
                p.emit(eng, lambda e, a=a: e.tensor_scalar(out=qz[a], in0=qt, scalar1=hm[:, a:a + 1], scalar2=None, op0=ALU.mult),
                       [B_q, B_hm], [B_qz[a]])
            for qg in range(S // T):
                q0 = qg * T
                for a in range(2):
                    pa = slice(a * 64, (a + 1) * 64)
                    bo, bs = 2 + 2 * a, 3 + 2 * a
                    def score(kb, a=a, q0=q0):
                        sb = kb % 2
                        mm(psum[:, sb, 0:T], kt[:, kb * 128:(kb + 1) * 128], qz[a][:, q0:q0 + T], True, True, [B_k, B_qz[a]], PB[sb])
                        p.emit("act", lambda e, sb=sb: e.activation(out=E[sb], in_=psum[:, sb, 0:T], func=AF.Exp, scale=0.125),
                               [PB[sb]], [B_E[sb]])
                    score(0)
                    for kb in range(NT):
                        sb = kb % 2
                        if kb + 1 < NT:
                            score(kb + 1)
                        mm(psum[:, bo, 0:T], vt[:, kb, :], E[sb], kb == 0, kb == NT - 1, [B_v, B_E[sb]], PB[bo])
                        mm(psum[:, bs, 0:T], onesb, E[sb], kb == 0, kb == NT - 1, [B_onesb, B_E[sb]], PB[bs])
                p.emit("dve", lambda e: e.reciprocal(out=r1, in_=psum[:, 3, 0:T]), [PB[3]], [B_r1])
                p.emit("dve", lambda e: e.tensor_tensor(out=o1, in0=psum[:, 2, 0:T], in1=r1, op=ALU.mult), [PB[2], B_r1], [B_o1])
                p.emit("dve", lambda e: e.reciprocal(out=r1, in_=psum[:, 5, 0:T]), [PB[5]], [B_r1])
                p.emit("dve", lambda e: e.tensor_tensor(out=o2, in0=psum[:, 4, 0:T], in1=r1, op=ALU.mult), [PB[4], B_r1], [B_o2])
                p.emit("dve", lambda e: e.scalar_tensor_tensor(out=o1, in0=o2, scalar=lm[:, 4:5], in1=o1, op0=ALU.mult,
                                                               op1=ALU.add), [B_o2, B_o1, B_lm], [B_o1])
                p.emit("act", lambda e: e.activation(out=sqv, in_=o1, func=AF.Square), [B_o1], [B_sqv])
                mm(psum[:, 6, 0:T], ones, sqv, True, True, [B_const, B_sqv], PB[6])
                p.emit("dve", lambda e: e.tensor_scalar(out=rs, in0=psum[:, 6, 0:T], scalar1=1.0 / 128, scalar2=1e-5,
                                                        op0=ALU.mult, op1=ALU.add), [PB[6]], [B_rs])
                p.emit("act", lambda e: e.activation(out=rs, in_=rs, func=AF.Ln), [B_rs], [B_rs])
                p.emit("act", lambda e: e.activation(out=rs, in_=rs, func=AF.Exp, scale=-0.5), [B_rs], [B_rs])
                on, Bon, son = ron.next()
                p.emit("dve", lambda e, on=on: e.scalar_tensor_tensor(out=on, in0=o1, scalar=sw[:, 1:2], in1=rs, op0=ALU.mult,
                                                                      op1=ALU.mult), [B_o1, B_sw, B_rs], [Bon])
                dma("sp", OT[c, :, q0:q0 + T], on, son, reads=[Bon])
        p.barrier()

    op_sem = p.dsem("opl")

    def out_proj():
        for ps_ in range(NP):
            tok0 = ps_ * T
            breset()
            stg = make_ystage()
            set_xslots(4)
            for c_ in range(KC):
                dma("sp", hT[:, c_, :], OT[c_, :, tok0:tok0 + T], op_sem, writes=[B_hT])
            for blk in range(NB256):
                def consO(tt, ps, bank, blk=blk, tok0=tok0):
                    evac_y(ps, bank, tok0 + tt * 128, blk * 256, 256, blk, stg)
                proj_tm(wout[blk], 256, consO)
            p.barrier()


    def gdn():
        o_big_holder[0] = o_hT
        breset()
        G4 = 4 * NH
        f32v = lambda n: view(balloc(n * 4), n, F32)
        gts = f32v(NT * G4).rearrange("p (a b) -> p a b", a=NT)
        dsm = f32v(G4)
        gd = f32v(NT * 2 * NH).rearrange("p (a d h) -> p a d h", a=NT, d=2)
        bd = f32v(NT * 2 * NH).rearrange("p (a d h) -> p a d h", a=NT, d=2)
        NN = NT * NH
        egc_f, negc_f, etail_f, egtot_f = f32v(2 * NN), f32v(2 * NN), f32v(2 * NN), f32v(2 * NN)
        v4 = lambda f: f.rearrange("p (d a h) -> p d a h", d=2, a=NT)
        egc, negc, etail, egtot = v4(egc_f), v4(negc_f), v4(etail_f), v4(egtot_f)
        tmpg_f, tmpg2_f = f32v(NN), f32v(NN)
        tmpg = tmpg_f.rearrange("p (a h) -> p a h", a=NT)
        tmpg2 = tmpg2_f.rearrange("p (a h) -> p a h", a=NT)
        cw = f32v(3 * NH * 5).rearrange("p (c t) -> p c t", c=3 * NH)
        nwb = f32v(128)
        Bg = Buf("gates")
        dma("sp", gts, GG.rearrange("(a p) g -> p a g", p=128), misc_sem, writes=[Bg])
        dma("sp", dsm, dn_small.partition_broadcast(128), misc_sem, writes=[Bg])
        dma("sp", cw, convw, misc_sem, writes=[Bg])
        dma("sp", nwb, dn_normw.partition_broadcast(128), misc_sem, writes=[Bg])
        p.barrier()
        E1 = lambda eng, fn: p.emit(eng, fn, [Bg], [Bg])
        E1("act", lambda e: e.activation(out=dsm[:, 0:2 * NH], in_=dsm[:, 0:2 * NH], func=AF.Exp))
        E1("dve", lambda e: e.tensor_scalar(out=dsm[:, 0:2 * NH], in0=dsm[:, 0:2 * NH], scalar1=-1.0, scalar2=None, op0=ALU.mult))
        for d in range(2):
            a_sl = slice(d * 2 * NH, d * 2 * NH + NH)
            b_sl = slice(d * 2 * NH + NH, (d + 1) * 2 * NH)
            for t in range(NT):
                E1("dve", lambda e, t=t, a_sl=a_sl, d=d: e.tensor_tensor(out=tmpg[:, t, :], in0=gts[:, t, a_sl],
                                                                      in1=dsm[:, 2 * NH + d * NH:2 * NH + (d + 1) * NH], op=ALU.add))
            E1("dve", lambda e: e.tensor_scalar(out=tmpg2, in0=tmpg, scalar1=-1.0, scalar2=None, op0=ALU.mult))
            E1("dve", lambda e: e.tensor_tensor(out=tmpg2, in0=tmpg2, in1=tmpg, op=ALU.max))
            E1("act", lambda e: e.activation(out=tmpg2, in_=tmpg2, func=AF.Exp, scale=-1.0))
            E1("dve", lambda e: e.tensor_scalar(out=tmpg2, in0=tmpg2, scalar1=1.0, scalar2=None, op0=ALU.add))
            E1("act", lambda e: e.activation(out=tmpg2, in_=tmpg2, func=AF.Ln))
            E1("dve", lambda e: e.scalar_tensor_tensor(out=tmpg, in0=tmpg, scalar=0.0, in1=tmpg2, op0=ALU.max, op1=ALU.add))
            for t in range(NT):
                E1("dve", lambda e, t=t, d=d: e.tensor_tensor(out=gd[:, t, d, :], in0=tmpg[:, t, :],
                                                             in1=dsm[:, d * NH:(d + 1) * NH], op=ALU.mult))
            E1("act", lambda e, d=d, b_sl=b_sl: e.activation(out=bd[:, :, d, :], in_=gts[:, :, b_sl], func=AF.Sigmoid))
            mincl = cst[:, 1 if d == 0 else 3, :]
            dsl = slice(d * NN, (d + 1) * NN)
            E1("dve", lambda e, d=d: e.tensor_copy(out=tmpg2, in_=gd[:, :, d, :]))
            mm(psum[:, 0, 0:NN], mincl, tmpg2_f, True, True, [Bg, B_const], PB[0])
            mm(psum[:, 1, 0:NN], ones, tmpg2_f, True, True, [Bg, B_const], PB[1])
            CL = lambda o_, i_: (lambda e: e.tensor_scalar(out=o_, in0=i_, scalar1=-80.0, scalar2=None, op0=ALU.max))
            p.emit("dve", CL(egc_f[:, dsl], psum[:, 0, 0:NN]), [PB[0]], [Bg])
            p.emit("dve", CL(egtot_f[:, dsl], psum[:, 1, 0:NN]), [PB[1]], [Bg])
            p.emit("dve", lambda e: e.tensor_copy(out=tmpg_f, in_=psum[:, 0, 0:NN]), [PB[0]], [Bg])
            p.emit("dve", lambda e, dsl=dsl: e.tensor_tensor(out=etail_f[:, dsl], in0=psum[:, 1, 0:NN], in1=tmpg_f, op=ALU.subtract),
                   [PB[1], Bg], [Bg])
            E1("dve", CL(etail_f[:, dsl], etail_f[:, dsl]))
            E1("act", lambda e, dsl=dsl: e.activation(out=egc_f[:, dsl], in_=egc_f[:, dsl], func=AF.Exp))
            E1("act", lambda e, dsl=dsl: e.activation(out=egtot_f[:, dsl], in_=egtot_f[:, dsl], func=AF.Exp))
            E1("act", lambda e, dsl=dsl: e.activation(out=etail_f[:, dsl], in_=etail_f[:, dsl], func=AF.Exp))
            E1("dve", lambda e, dsl=dsl: e.tensor_scalar(out=negc_f[:, dsl], in0=egc_f[:, dsl], scalar1=-1.0, scalar2=None, op0=ALU.mult))
        p.barrier()
        HG = 2 if NH % 2 == 0 else 1
        xp = [f32v(S + 4)]
        B_xp = [Buf("xp0")]
        xp_sem = [p.dsem("xp0")]
        acc, sqb = f32v(S), f32v(S)
        B_acc, B_sqb = Buf("acc"), Buf("sqb")
        rsb = f32v(512)
        B_rsb = Buf("rsb")
        zt = f32v(NT * 128).rearrange("p (a b) -> p a b", a=NT)
        B_z = Buf("zt")
        z_sem = p.dsem("zt")
        st2 = f32v(2 * NT)
        B_st2 = Buf("st2")
        HC = []
        for hi in range(HG):
            c_ = {}
            c_["qkv"] = (f32v(S), f32v(S), f32v(S))
            c_["B_qkv"] = [Buf(f"qn{hi}"), Buf(f"kn{hi}"), Buf(f"vs{hi}")]
            c_["oacc"] = [f32v(NT * 128).rearrange("p (a b) -> p a b", a=NT) for _ in range(2)]
            c_["B_oacc"] = [Buf(f"of{hi}"), Buf(f"ob{hi}")]
            c_["Sst"] = [f32v(128), f32v(128)]
            c_["B_S"] = [Buf(f"S0{hi}"), Buf(f"S1{hi}")]
            W = []
            for d in range(2):
                w_ = {n: f32v(128) for n in ("Gm", "DT", "DTi", "DTs", "X", "XT", "QKD", "Ra", "Rb", "P", "PT", "P2", "P2T", "kt",
                                             "vt", "r", "vn", "t1")}
                w_["B"] = {n: Buf(n + str(d) + str(hi)) for n in w_}
                W.append(w_)
            c_["W"] = W
            HC.append(c_)
        rog = Ring("rog", 2, 128, BF16)
        rot = Ring("rot", 2, 128, BF16)
        for i_ in range(1):
            p.emit("pool", lambda e, i_=i_: e.memset(xp[i_][:, 0:2], 0.0), [], [B_xp[i_]])
            p.emit("pool", lambda e, i_=i_: e.memset(xp[i_][:, S + 2:S + 4], 0.0), [], [B_xp[i_]])
        pb = [0]

        def nb():
            b = pb[0] % 8
            pb[0] += 1
            return b

        def mmf(lhsT, rhs, reads):
            b = nb()
            o_ = psum[:, b, 0:128]
            mm(o_, lhsT, rhs, True, True, reads, PB[b])
            return o_, PB[b]

        def prep(h, c_):
            qn, kn, vs = c_["qkv"]
            B_qkv = c_["B_qkv"]
            Sst, B_S = c_["Sst"], c_["B_S"]
            for qi, (dst, Bd) in enumerate(zip((qn, kn, vs), B_qkv)):
                ch = qi * NH + h
                i_ = 0
                x_, Bx_ = xp[i_], B_xp[i_]
                dma("sp", x_[:, 2:S + 2], DNT[ch], xp_sem[i_], writes=[Bx_])
                eng = "dve" if qi != 1 else "pool"
                p.emit(eng, lambda e, x_=x_, ch=ch: e.tensor_scalar(out=acc, in0=x_[:, 0:S], scalar1=cw[:, ch, 0:1], scalar2=None,
                                                                   op0=ALU.mult), [Bx_, Bg], [B_acc])
                for j in range(1, 5):
                    p.emit("dve", lambda e, x_=x_, ch=ch, j=j: e.scalar_tensor_tensor(out=acc, in0=x_[:, j:j + S], scalar=cw[:, ch, j:j + 1],
                                                                                   in1=acc, op0=ALU.mult, op1=ALU.add), [Bx_, Bg, B_acc], [B_acc])
                if qi == 2:
                    p.emit("act", lambda e, dst=dst: e.activation(out=dst, in_=acc, func=AF.Silu), [B_acc], [Bd])
                    continue
                p.emit("act", lambda e: e.activation(out=acc, in_=acc, func=AF.Silu), [B_acc], [B_acc])
                p.emit("act", lambda e: e.activation(out=sqb, in_=acc, func=AF.Square), [B_acc], [B_sqb])
                for b0 in range(0, S, 512):
                    bk = nb()
                    mm(psum[:, bk, :], ones, sqb[:, b0:b0 + 512], True, True, [B_const, B_sqb], PB[bk])
                    p.emit("dve", lambda e, bk=bk: e.tensor_scalar(out=rsb, in0=psum[:, bk, :], scalar1=1e-6, scalar2=None, op0=ALU.add),
                           [PB[bk]], [B_rsb])
                    p.emit("act", lambda e: e.activation(out=rsb, in_=rsb, func=AF.Ln), [B_rsb], [B_rsb])
                    p.emit("act", lambda e: e.activation(out=rsb, in_=rsb, func=AF.Exp, scale=-0.5), [B_rsb], [B_rsb])
                    sc = float(128 ** -0.5) if qi == 0 else 1.0
                    p.emit("dve", lambda e, dst=dst, b0=b0, sc=sc: e.scalar_tensor_tensor(out=dst[:, b0:b0 + 512], in0=acc[:, b0:b0 + 512],
                                                                                       scalar=sc, in1=rsb, op0=ALU.mult, op1=ALU.mult),
                           [B_acc, B_rsb], [Bd])
            for d in range(2):
                p.emit("pool", lambda e, d=d: e.memset(Sst[d], 0.0), [], [B_S[d]])
        def step(h, c_, n_, d):
            qn, kn, vs = c_["qkv"]
            B_qkv = c_["B_qkv"]
            Sst, B_S = c_["Sst"], c_["B_S"]
            oacc, B_oacc = c_["oacc"], c_["B_oacc"]
            W = c_["W"]
            if True:
                if True:
                    t = n_ if d == 0 else NT - 1 - n_
                    w_ = W[d]
                    B_ = w_["B"]
                    cs = slice(t * 128, (t + 1) * 128)
                    MinclG = cst[:, 1 if d == 0 else 3, :]
                    Mstr = cst[:, 2 if d == 0 else 4, :]
                    maskI = cst[:, 1 if d == 0 else 3, :]
                    maskS = cst[:, 4 if d == 0 else 2, :]
                    gcol = gd[:, t, d, h:h + 1]
                    bcol = bd[:, t, d, h:h + 1]
                    Bq, Bk, Bv = B_qkv
                    p.emit("pool", lambda e, w_=w_, MinclG=MinclG, gcol=gcol: e.tensor_scalar(out=w_["Gm"], in0=MinclG, scalar1=gcol,
                                                                                           scalar2=None, op0=ALU.mult), [B_const, Bg], [B_["Gm"]])
                    ps, Bp = mmf(Mstr, w_["Gm"], [B_const, B_["Gm"]])
                    p.emit("dve", lambda e, w_=w_, ps=ps: e.tensor_scalar(out=w_["DT"], in0=ps, scalar1=-80.0, scalar2=None, op0=ALU.max),
                           [Bp], [B_["DT"]])
                    p.emit("act", lambda e, w_=w_: e.activation(out=w_["DT"], in_=w_["DT"], func=AF.Exp), [B_["DT"]], [B_["DT"]])
                    p.emit("pool", lambda e, w_=w_, maskI=maskI: e.tensor_tensor(out=w_["DTi"], in0=w_["DT"], in1=maskI, op=ALU.mult),
                           [B_["DT"], B_const], [B_["DTi"]])
                    p.emit("pool", lambda e, w_=w_, maskS=maskS: e.tensor_tensor(out=w_["DTs"], in0=w_["DT"], in1=maskS, op=ALU.mult),
                           [B_["DT"], B_const], [B_["DTs"]])
                    yield
                    ps, Bp = mmf(kn[:, cs], kn[:, cs], [Bk])
                    p.emit("dve", lambda e, w_=w_, ps=ps, bcol=bcol: e.scalar_tensor_tensor(out=w_["X"], in0=ps, scalar=bcol, in1=w_["DTs"],
                                                                                        op0=ALU.mult, op1=ALU.mult), [Bp, Bg, B_["DTs"]], [B_["X"]])
                    yield
                    ps, Bp = mmf(kn[:, cs], qn[:, cs], [Bk, Bq])
                    p.emit("dve", lambda e, w_=w_, ps=ps: e.tensor_tensor(out=w_["QKD"], in0=ps, in1=w_["DTi"], op=ALU.mult),
                           [Bp, B_["DTi"]], [B_["QKD"]])
                    yield
                    ps, Bp = mmf(w_["X"], ident, [B_["X"], B_const])
                    p.emit("act", lambda e, w_=w_, ps=ps: e.activation(out=w_["XT"], in_=ps, func=AF.Copy), [Bp], [B_["XT"]])
                    p.emit("dve", lambda e, w_=w_: e.tensor_tensor(out=w_["Ra"], in0=ident, in1=w_["X"], op=ALU.subtract),
                           [B_const, B_["X"]], [B_["Ra"]])
                    P_, PT_, R_ = ("X", "XT", "Ra")
                    for lvl in range(1, 8):
                        if (1 << lvl) >= 128:
                            break
                        last = (1 << (lvl + 1)) >= 128
                        nP, nPT = ("P2", "P2T") if P_ in ("X", "P") else ("P", "PT")
                        yield
                        ps, Bp = mmf(w_[P_], w_[PT_], [B_[P_], B_[PT_]])
                        p.emit("act", lambda e, w_=w_, ps=ps, nPT=nPT: e.activation(out=w_[nPT], in_=ps, func=AF.Copy), [Bp], [B_[nPT]])
                        if not last:
                            yield
                            ps, Bp = mmf(w_[PT_], w_[P_], [B_[P_], B_[PT_]])
                            p.emit("dve", lambda e, w_=w_, ps=ps, nP=nP: e.tensor_copy(out=w_[nP], in_=ps), [Bp], [B_[nP]])
                        nR = "Rb" if R_ == "Ra" else "Ra"
                        yield
                        ps, Bp = mmf(w_[nPT], w_[R_], [B_[nPT], B_[R_]])
                        p.emit("dve", lambda e, w_=w_, ps=ps, R_=R_, nR=nR: e.tensor_tensor(out=w_[nR], in0=ps, in1=w_[R_], op=ALU.add),
                               [Bp, B_[R_]], [B_[nR]])
                        P_, PT_, R_ = nP, nPT, nR
                    yield
                    ps, Bp = mmf(kn[:, cs], ident, [Bk, B_const])
                    p.emit("dve", lambda e, w_=w_, ps=ps, t=t, d=d, h=h: e.tensor_scalar(out=w_["kt"], in0=ps, scalar1=etail[:, d, t, h:h + 1],
                                                                                  scalar2=None, op0=ALU.mult), [Bp, Bg], [B_["kt"]])
                    yield
                    ps, Bp = mmf(vs[:, cs], ident, [Bv, B_const])
                    p.emit("act", lambda e, w_=w_, ps=ps: e.activation(out=w_["vt"], in_=ps, func=AF.Copy), [Bp], [B_["vt"]])
                    yield
                    ps, Bp = mmf(kn[:, cs], Sst[d], [Bk, B_S[d]])
                    p.emit("dve", lambda e, w_=w_, ps=ps, t=t, d=d, h=h: e.scalar_tensor_tensor(out=w_["r"], in0=ps, scalar=negc[:, d, t, h:h + 1],
                                                                                         in1=w_["vt"], op0=ALU.mult, op1=ALU.add),
                           [Bp, Bg, B_["vt"]], [B_["r"]])
                    yield
                    ps, Bp = mmf(w_[R_], w_["r"], [B_[R_], B_["r"]])
                    p.emit("dve", lambda e, w_=w_, ps=ps, bcol=bcol: e.tensor_scalar(out=w_["vn"], in0=ps, scalar1=bcol, scalar2=None,
                                                                                   op0=ALU.mult), [Bp, Bg], [B_["vn"]])
                    yield
                    ps, Bp = mmf(qn[:, cs], Sst[d], [Bq, B_S[d]])
                    p.emit("dve", lambda e, w_=w_, ps=ps, t=t, d=d, h=h: e.tensor_scalar(out=w_["t1"], in0=ps, scalar1=egc[:, d, t, h:h + 1],
                                                                                  scalar2=None, op0=ALU.mult), [Bp, Bg], [B_["t1"]])
                    yield
                    ps, Bp = mmf(w_["QKD"], w_["vn"], [B_["QKD"], B_["vn"]])
                    p.emit("dve", lambda e, w_=w_, ps=ps, t=t, d=d, h=h: e.tensor_tensor(out=oacc[d][:, t, :], in0=ps, in1=w_["t1"], op=ALU.add),
                           [Bp, B_["t1"]], [B_oacc[d]])
                    yield
                    ps, Bp = mmf(w_["kt"], w_["vn"], [B_["kt"], B_["vn"]])
                    p.emit("dve", lambda e, ps=ps, t=t, d=d, h=h: e.scalar_tensor_tensor(out=Sst[d], in0=Sst[d], scalar=egtot[:, d, t, h:h + 1],
                                                                                  in1=ps, op0=ALU.mult, op1=ALU.add), [Bp, Bg, B_S[d]], [B_S[d]])
        def gate(h, c_):
            oacc, B_oacc = c_["oacc"], c_["B_oacc"]
            dma("sp", zt, ZZ[:, h * 128:(h + 1) * 128].rearrange("(a p) d -> p a d", p=128), z_sem, writes=[B_z])
            of_, ob_ = oacc
            p.emit("dve", lambda e: e.tensor_tensor(out=of_, in0=of_, in1=ob_, op=ALU.add), [B_oacc[0], B_oacc[1]], [B_oacc[0]])
            p.emit("act", lambda e: e.activation(out=ob_, in_=of_, func=AF.Square), [B_oacc[0]], [B_oacc[1]])
            p.emit("dve", lambda e: e.reduce_sum(out=st2[:, 0:NT], in_=ob_, axis=AX.X), [B_oacc[1]], [B_st2])
            rstd_from_ss(st2[:, 0:NT], st2[:, 0:NT], 128, 1e-6, [B_st2], [B_st2])
            p.emit("act", lambda e: e.activation(out=zt, in_=zt, func=AF.Silu), [B_z], [B_z])
            for t in range(NT):
                og, Bog, _ = rog.next()
                ot, Bot, sot = rot.next()
                p.emit("dve", lambda e, t=t: e.scalar_tensor_tensor(out=ob_[:, t, :], in0=of_[:, t, :], scalar=st2[:, t:t + 1], in1=nwb,
                                                                   op0=ALU.mult, op1=ALU.mult), [B_oacc[0], B_st2, Bg], [B_oacc[1]])
                p.emit("dve", lambda e, t=t, og=og: e.tensor_tensor(out=og, in0=ob_[:, t, :], in1=zt[:, t, :], op=ALU.mult),
                       [B_oacc[1], B_z], [Bog])
                bk = nb()
                pt = psum[:, bk, :].bitcast(BF16)[:, 0:128]
                p.emit("pe", lambda e, pt=pt, og=og: e.transpose(pt, og, identb), [Bog, B_identb], [PB[bk]])
                p.emit("act", lambda e, pt=pt, ot=ot: e.activation(out=ot, in_=pt, func=AF.Copy), [PB[bk]], [Bot])
                dma("sp", OT[NH + h, :, t * 128:(t + 1) * 128], ot, sot, reads=[Bot])

        for h0 in range(0, NH, HG):
            for hi in range(HG):
                prep(h0 + hi, HC[hi])
            for n_ in range(NT):
                gens = [step(h0 + hi, HC[hi], n_, d) for hi in range(HG) for d in range(2)]
                while gens:
                    alive = []
                    for g_ in gens:
                        try:
                            next(g_)
                            alive.append(g_)
                        except StopIteration:
                            pass
                    gens = alive
            for hi in range(HG):
                gate(h0 + hi, HC[hi])
        p.barrier()
        o_big_holder[0] = o_big

    dbg_sem = p.dsem("dbg")

    def dbg(name, v, bufs):
        t = nc.dram_tensor(name, list(v.shape), v.dtype, kind="ExternalOutput").ap()
        dma("sp", t, v, dbg_sem, reads=bufs)

    k.phase = dict(dbg=dbg, gdn=gdn, mixer_inproj=mixer_inproj, diff_attn=diff_attn, out_proj=out_proj, mem_attn=mem_attn, boundary=boundary, ffn=ffn, evac_y=evac_y, make_ystage=make_ystage, wload=wload, mm=mm,
                   dma=dma, view=view, balloc=balloc, breset=breset, rstd_from_ss=rstd_from_ss)
    k.syms = dict(locals())
    return k


def finish(k, final_events_wait=True):
    p, nc, es = k.p, k.nc, k.es
    p.barrier()
    with nc.Block() as block:
        p.finalize(block)
    es.close()
    return nc


def blk256(w, KC):
    K, N = w.shape
    return np.ascontiguousarray(w.reshape(KC, 128, N // 256, 256).transpose(2, 1, 0, 3))


def lay_wgu(w, cfg):
    K, N2 = w.shape
    FC, KC = cfg.FC, cfg.KC
    g = w[:, :cfg.DFF].reshape(KC, 128, FC, 128)
    u = w[:, cfg.DFF:].reshape(KC, 128, FC, 128)
    gu = np.concatenate([g, u], axis=3)
    return np.ascontiguousarray(gu.transpose(2, 1, 0, 3))


def lay_wdn(w, cfg):
    FC, NKG, D = cfg.FC, cfg.NKG, cfg.D
    wp = np.zeros((NKG * 16 * 128, D), np.float32)
    wp[:cfg.DFF] = w
    a = wp.reshape(NKG, 16, 128, D // 512, 512)
    return np.ascontiguousarray(a.transpose(3, 0, 2, 1, 4))


def make_consts():
    c = np.zeros((128, 8, 128), np.float32)
    c[:, 0, :] = np.eye(128)
    i = np.arange(128)
    c[:, 1, :] = (i[:, None] <= i[None, :])
    c[:, 2, :] = (i[:, None] > i[None, :])
    c[:, 3, :] = (i[:, None] >= i[None, :])
    c[:, 4, :] = (i[:, None] < i[None, :])
    RT = np.zeros((128, 128), np.float32)
    for p_ in range(128):
        d = p_ % 64
        if d < 8:
            RT[p_ + 8, p_] = -1.0
        elif d < 16:
            RT[p_ - 8, p_] = 1.0
    c[:, 5, :] = RT
    c[:, 6, :] = 1.0
    return c


def host_mixer_inputs(cfg, w_in, w_out, lam, subln, pos):
    HW, NH, KC = cfg.HW, cfg.NH, cfg.KC
    cols = np.concatenate([np.arange(0, 2 * HW), np.arange(3 * HW, 6 * HW), np.arange(2 * HW, 3 * HW),
                           np.arange(6 * HW, 7 * HW)])
    d = {}
    d["win"] = blk256(w_in[:, cols], KC)
    wg = w_in[:, 7 * HW:]
    d["wgate"] = np.ascontiguousarray(wg.reshape(KC, 128, 4 * NH).transpose(1, 0, 2))
    d["wout"] = blk256(w_out, KC)
    d["lam"] = np.ascontiguousarray(lam.reshape(1, 256))
    d["subln"] = np.ascontiguousarray(subln.reshape(1, 128))
    d["pos"] = np.ascontiguousarray(pos.reshape(1, -1).astype(np.int32))
    invf = np.zeros((128, 1), np.float32)
    base = (np.float32(cfg.rope_theta) ** (-np.arange(0, 16, 2, dtype=np.float32) / np.float32(16))).astype(np.float32)
    for p_ in range(128):
        dd = p_ % 64
        if dd < 16:
            invf[p_, 0] = base[dd % 8]
    d["invf"] = invf
    return d


def host_gdn_inputs(cfg, conv_w, a_log, dt_bias, normw):
    NH = cfg.NH
    d = {}
    d["convw"] = np.ascontiguousarray(conv_w.T.reshape(3 * NH, 128, 5).transpose(1, 0, 2))
    d["dn_small"] = np.ascontiguousarray(np.concatenate([a_log.reshape(-1), dt_bias.reshape(-1)]).reshape(1, 4 * NH))
    d["dn_normw"] = np.ascontiguousarray(normw.reshape(1, 128))
    return d


_CACHE = {}


def build_full(cfg):
    k = build(cfg)
    ph = k.phase
    out = k.syms["out"]
    NFB, NB256 = cfg.D // 512, cfg.D // 256
    ph["ffn"](0, k.dram["x"], None, 0)
    ph["mixer_inproj"](out, (0.5, 1, NFB), 2)
    ph["diff_attn"]()
    ph["gdn"]()
    ph["out_proj"]()
    ph["mem_attn"](out, (1.0, 3, NB256), 4, 5)
    ph["ffn"](1, out, (1.0, 6, NB256), 7)
    ph["boundary"](out, (0.5, 8, NFB), 0, 0, cfg.S, want_h=False)
    return k, finish(k)


def kernel(x, mem, positions, ffn1_norms, ffn1_w_gu, ffn1_w_down, mix_norms, mix_w_in, dn_conv_w, dn_a_log,
           dn_dt_bias, dn_norm_w, diff_lambda, diff_subln_w, mix_w_out, mem_norms, mem_w_q, mem_w_kv, mem_w_o,
           ffn2_norms, ffn2_w_gu, ffn2_w_down):
    f = lambda a: np.asarray(a, dtype=np.float32)
    x = f(x)
    B, S, D = x.shape
    cfg = Cfg(D=D, S=S, DFF=f(ffn1_w_down).shape[1], MEM=np.asarray(mem).shape[1], ncores=B)
    k, nc = build_full(cfg)
    shared = {}
    shared["wgu1"] = lay_wgu(f(ffn1_w_gu)[0], cfg)
    shared["wdn1"] = lay_wdn(f(ffn1_w_down)[0], cfg)
    shared["wgu2"] = lay_wgu(f(ffn2_w_gu)[0], cfg)
    shared["wdn2"] = lay_wdn(f(ffn2_w_down)[0], cfg)
    shared.update(host_gdn_inputs(cfg, f(dn_conv_w)[0], f(dn_a_log)[0], f(dn_dt_bias)[0], f(dn_norm_w)[0]))
    shared["wmq"] = blk256(f(mem_w_q)[0], cfg.KC)
    shared["wmkv"] = blk256(f(mem_w_kv)[0], cfg.KC)
    shared["wmo"] = blk256(f(mem_w_o)[0], 4)
    shared["norms"] = np.ascontiguousarray(np.concatenate([f(ffn1_norms)[0], f(mix_norms)[0], f(mem_norms)[0][[0, 1, 2]],
                                                           f(ffn2_norms)[0]], axis=0))
    shared["consts"] = make_consts()
    pos = np.asarray(positions)
    in_maps = []
    for b in range(B):
        m = dict(shared)
        m.update(host_mixer_inputs(cfg, f(mix_w_in)[0], f(mix_w_out)[0], f(diff_lambda)[0], f(diff_subln_w)[0], pos[b])
                 if b == 0 else {kk: in_maps[0][kk] for kk in ("win", "wgate", "wout", "lam", "subln", "invf")})
        m["pos"] = np.ascontiguousarray(pos[b].reshape(1, -1).astype(np.int32))
        m["x"] = np.ascontiguousarray(x[b])
        m["mem"] = np.ascontiguousarray(f(mem)[b])
        in_maps.append(m)
    res = run_bass_kernel_spmd(nc, in_maps, core_ids=list(range(B)))
    return np.stack([np.asarray(r["out"]) for r in res.results], axis=0).astype(np.float32)
```

```python
import numpy as np
from contextlib import ExitStack
import concourse.bass as bass
import concourse.mybir as mybir
from concourse.bass_utils import run_bass_kernel_spmd

F32 = mybir.dt.float32
BF16 = mybir.dt.bfloat16
I32 = mybir.dt.int32
AF = mybir.ActivationFunctionType
ALU = mybir.AluOpType
AX = mybir.AxisListType


class Cfg:
    def __init__(self, D=4096, S=2048, DFF=11008, MEM=256, ncores=8):
        self.D, self.S, self.DFF, self.MEM, self.ncores = D, S, DFF, MEM, ncores
        self.KC = D // 128
        self.NT = S // 128
        self.T = min(512, S)
        self.NP = S // self.T
        self.TT = self.T // 128
        self.FC = DFF // 128
        self.NKG = (self.FC + 15) // 16
        self.HW = D // 2
        self.NH = self.HW // 128
        self.INCOLS = 3 * self.HW + 4 * self.HW + 4 * self.NH
        self.MH = 4
        self.lambda_init = 0.2
        self.rope_theta = 500000.0


class Buf:
    __slots__ = ("w", "r", "name")

    def __init__(self, name=""):
        self.w = None
        self.r = []
        self.name = name


class DSem:
    def __init__(self, sem):
        self.sem = sem
        self.count = 0


ENGS = ("pe", "act", "dve", "pool", "sp")


class Prog:
    def __init__(self, nc, es):
        self.nc = nc
        self.es = es
        self.ops = {e: [] for e in ENGS}
        self.dsems = []
        self.all_dma_events = []

    def dsem(self, name):
        d = DSem(self.es.enter_context(self.nc.semaphore("d_" + name)))
        self.dsems.append(d)
        return d

    def emit(self, eng, fn, reads=(), writes=(), dsem=None, acc=False):
        deps = []
        for b in reads:
            if b.w is not None:
                deps.append(b.w)
        for b in writes:
            if b.w is not None:
                if not (acc and b.w[0] == "c" and b.w[1] == "pe"):
                    deps.append(b.w)
            deps.extend(b.r)
        seq = len(self.ops[eng])
        if dsem is None:
            ev = ("c", eng, seq)
        else:
            dsem.count += 16
            ev = ("d", dsem, dsem.count)
            self.all_dma_events.append(ev)
        deps2 = []
        for d in deps:
            if d[0] == "c" and d[1] == eng and eng == "pe":
                continue
            deps2.append(d)
        self.ops[eng].append([deps2, fn, ev, dsem])
        for b in reads:
            b.r.append(ev)
        for b in writes:
            b.w = ev
            b.r = []
        return ev

    def barrier(self):
        lasts = []
        for e in ENGS:
            if self.ops[e]:
                for op in reversed(self.ops[e]):
                    if op[2][0] == "c" and op[1] is not None:
                        lasts.append(op[2])
                        break
        dm = list(self.all_dma_events)
        for e in ENGS:
            self.ops[e].append([lasts + dm, None, ("c", e, len(self.ops[e])), None])
        self.all_dma_events = []

    def finalize(self, block):
        nc = self.nc
        csem = {e: self.es.enter_context(nc.semaphore("c_" + e)) for e in ENGS}
        waited = {e: set() for e in ENGS}
        for e in ENGS:
            for deps, fn, ev, ds in self.ops[e]:
                for d in deps:
                    if d[0] == "c":
                        waited[d[1]].add(d[2])
        rank = {}
        for e in ENGS:
            r = 0
            for seq in sorted(waited[e]):
                assert self.ops[e][seq][1] is not None, "wait on a barrier pseudo-op"
                r += 1
                rank[(e, seq)] = r
        ops = self.ops

        def run(e, eng):
            have = {}
            for seq, (deps, fn, ev, ds) in enumerate(ops[e]):
                need = {}
                for d in deps:
                    if d[0] == "c":
                        key = ("c", d[1])
                        val = rank[(d[1], d[2])]
                        sem = csem[d[1]]
                    else:
                        key = ("d", id(d[1]))
                        val = d[2]
                        sem = d[1].sem
                    if have.get(key, 0) >= val:
                        continue
                    if key not in need or need[key][1] < val:
                        need[key] = (sem, val)
                for key, (sem, val) in need.items():
                    eng.wait_ge(sem, val)
                    have[key] = val
                if fn is None:
                    continue
                ins = fn(eng)
                if ds is not None:
                    ins.then_inc(ds.sem, 16)
                elif (e, seq) in rank:
                    ins.then_inc(csem[e], 1)

        block.tensor(lambda eng: run("pe", eng))
        block.scalar(lambda eng: run("act", eng))
        block.vector(lambda eng: run("dve", eng))
        block.gpsimd(lambda eng: run("pool", eng))
        block.sync(lambda eng: run("sp", eng))


class K:
    def __init__(self, cfg):
        self.cfg = cfg
        self.nc = bass.Bass("TRN2", target_bir_lowering=False)
        self.es = ExitStack()
        self.p = Prog(self.nc, self.es)
        self.dram = {}

    def din(self, name, shape, dt=F32):
        t = self.nc.dram_tensor(name, list(shape), dt, kind="ExternalInput").ap()
        self.dram[name] = t
        return t

    def dscr(self, name, shape, dt=F32):
        t = self.nc.dram_tensor(name, list(shape), dt,
                                kind="ExternalOutput" if getattr(self.cfg, "debug", False) else "Internal").ap()
        self.dram[name] = t
        return t


def build(cfg):
    k = K(cfg)
    nc, es, p = k.nc, k.es, k.p
    D, S, DFF, KC, NT, T, NP, TT, FC, NKG = (cfg.D, cfg.S, cfg.DFF, cfg.KC, cfg.NT, cfg.T, cfg.NP,
                                             cfg.TT, cfg.FC, cfg.NKG)
    HW, NH, MEM = cfg.HW, cfg.NH, cfg.MEM
    NB256 = D // 256
    NFB = D // 512
    EPS = 1e-6

    x_in = k.din("x", [S, D])
    mem_in = k.din("mem", [MEM, D])
    pos_in = k.din("pos", [1, S], I32)
    out = nc.dram_tensor("out", [S, D], F32, kind="ExternalOutput").ap()
    wgu = [k.din(f"wgu{i}", [FC, 128, KC, 256]) for i in (1, 2)]
    wdn = [k.din(f"wdn{i}", [NFB, NKG, 128, 16, 512]) for i in (1, 2)]
    NBIN = (3 * HW + 4 * HW) // 256
    win = k.din("win", [NBIN, 128, KC, 256])
    wgate = k.din("wgate", [128, KC, 4 * NH])
    wout = k.din("wout", [NB256, 128, KC, 256])
    wmq = k.din("wmq", [2, 128, KC, 256])
    wmkv = k.din("wmkv", [4, 128, KC, 256])
    wmo = k.din("wmo", [NB256, 128, 4, 256])
    norms = k.din("norms", [9, D])
    convw = k.din("convw", [128, 3 * NH, 5])
    dn_small = k.din("dn_small", [1, 4 * NH])
    dn_normw = k.din("dn_normw", [1, 128])
    subln = k.din("subln", [1, 128])
    lam_in = k.din("lam", [1, 256])
    consts = k.din("consts", [128, 8, 128])
    invf = k.din("invf", [128, 1])
    Y = k.dscr("Y", [S, D])
    QT = k.dscr("QT", [NH, 128, S], BF16)
    KT = k.dscr("KT", [NH, 128, S], BF16)
    VV = k.dscr("VV", [S, HW], BF16)
    DNT = k.dscr("DNT", [3 * NH, 128, S])
    ZZ = k.dscr("ZZ", [S, HW])
    GG = k.dscr("GG", [S, 4 * NH])
    OT = k.dscr("OT", [KC, 128, S], BF16)
    MQT = k.dscr("MQT", [4, 128, S], BF16)

    ARENA_F32 = 46000
    arena = es.enter_context(nc.sbuf_tensor("arena", [128, ARENA_F32], F32))
    psum = es.enter_context(nc.psum_tensor("psum", [128, 8, 512], F32))
    cursor = [0]

    def alloc(nbytes):
        off = cursor[0]
        cursor[0] += (nbytes + 31) // 32 * 32
        assert cursor[0] <= ARENA_F32 * 4, f"arena overflow {cursor[0]}"
        return off

    def view(off, n_elems, dt):
        nb = n_elems * mybir.dt.size(dt)
        assert off % 4 == 0 and nb % 4 == 0
        v = arena[:, off // 4:(off + nb) // 4]
        return v if dt == F32 else v.bitcast(dt)

    o_const = alloc(8 * 128 * 4)
    cst = view(o_const, 8 * 128, F32).rearrange("p (a b) -> p a b", a=8)
    ident, ones = cst[:, 0, :], cst[:, 6, :]
    o_identb = alloc(128 * 2)
    identb = view(o_identb, 128, BF16)
    o_ss = alloc(NT * 16 * 4)
    ssparts = view(o_ss, NT * 16, F32).rearrange("p (a b) -> p a b", a=NT)
    o_small = alloc(64 * 4)
    small = view(o_small, 64, F32)
    B_const, B_ss, B_small, B_identb = Buf("const"), Buf("ss"), Buf("small"), Buf("identb")
    PB = [Buf(f"ps{i}") for i in range(8)]
    o_hT = alloc(KC * T * 2)
    hT = view(o_hT, KC * T, BF16).rearrange("p (a b) -> p a b", a=KC)
    B_hT = Buf("hT")
    NSLOT = 2
    o_ws = [alloc(8192 * 2) for _ in range(NSLOT)]
    wslot = [view(o, 8192, BF16) for o in o_ws]
    B_ws = [Buf(f"ws{i}") for i in range(NSLOT)]
    ws_sem = [p.dsem(f"ws{i}") for i in range(NSLOT)]
    ws_ctr = [0]
    o_big = cursor[0]
    big_cursor = [o_big]

    def balloc(nbytes):
        off = big_cursor[0]
        big_cursor[0] += (nbytes + 31) // 32 * 32
        assert big_cursor[0] <= ARENA_F32 * 4, f"big arena overflow {big_cursor[0]} {ARENA_F32*4}"
        return off

    o_big_holder = [o_big]

    def breset():
        big_cursor[0] = o_big_holder[0]
        del xslots[:]

    misc_sem = p.dsem("misc")

    def dma(eng, out_ap, in_ap, dsem, reads=(), writes=()):
        return p.emit(eng, lambda e: e.dma_start(out=out_ap, in_=in_ap), reads, writes, dsem=dsem)

    o_epsv = alloc(8 * 4)
    epsv = view(o_epsv, 8, F32)
    B_eps = Buf("eps")
    EPSV = {1e-6: 0, 1e-5: 1, 1.0: 2, 0.0: 3}
    for val_, col_ in EPSV.items():
        p.emit("dve", lambda e, val_=val_, col_=col_: e.memset(epsv[:, col_:col_ + 1], float(val_)), [], [B_eps])

    def epsc(v):
        c = EPSV[v]
        return epsv[:, c:c + 1]

    dma("sp", cst, consts, misc_sem, writes=[B_const])
    p.emit("dve", lambda e: e.tensor_copy(out=identb, in_=ident), [B_const], [B_identb])

    NXS = 4
    xs_sem = [p.dsem(f"xs{i}") for i in range(NXS)]
    xslots = []

    def set_xslots(n):
        del xslots[:]
        for i in range(n):
            xslots.append((view(balloc(8192 * 2), 8192, BF16), Buf(f"xs{i}"), xs_sem[i]))

    def wload(src_ap, shape_str=None, **kw):
        nsl = NSLOT + len(xslots)
        i = ws_ctr[0] % nsl
        ws_ctr[0] += 1
        n = 1
        for s_ in src_ap.shape[1:]:
            n *= s_
        assert n <= 8192
        if i >= NSLOT:
            xv, xb, xsm = xslots[i - NSLOT]
            dst = xv[:, 0:n]
            if len(src_ap.shape) == 3:
                dst = dst.rearrange("p (a b) -> p a b", a=src_ap.shape[1])
            p.emit("pool", lambda e: e.dma_start(out=dst, in_=src_ap, max_dma_last_dim=8192), [], [xb], dsem=xsm)
            return dst, xb
        dst = wslot[i][:, 0:n]
        if len(src_ap.shape) == 3:
            dst = dst.rearrange("p (a b) -> p a b", a=src_ap.shape[1])
        p.emit("pool", lambda e: e.dma_start(out=dst, in_=src_ap, max_dma_last_dim=8192), [], [B_ws[i]],
               dsem=ws_sem[i])
        return dst, B_ws[i]

    def mm(out_ap, lhsT, rhs, start, stop, reads, wbuf):
        p.emit("pe", lambda e: e.matmul(out_ap, lhsT, rhs, start=start, stop=stop), reads, [wbuf],
               acc=not start)

    def rstd_from_ss(dst, src, n, eps, rb, wb, eng="dve"):
        p.emit("dve", lambda e: e.tensor_scalar(out=dst, in0=src, scalar1=1.0 / n, scalar2=float(eps), op0=ALU.mult,
                                                op1=ALU.add), rb, wb)
        p.emit("act", lambda e: e.activation(out=dst, in_=dst, func=AF.Ln), wb, wb)
        p.emit("act", lambda e: e.activation(out=dst, in_=dst, func=AF.Exp, scale=-0.5), wb, wb)

    def boundary(x_src, y_info, pre_row, tok0, ntok, want_h=True, write_x=True):
        breset()
        o_xt, o_yt = balloc(D * 4), balloc(D * 4)
        o_wpost, o_wpre, o_xn = balloc(D * 4), balloc(D * 4), balloc(D * 2)
        o_junk = balloc(D * 4)
        xt, yt = view(o_xt, D, F32), view(o_yt, D, F32)
        wpost, wpre, xn = view(o_wpost, D, F32), view(o_wpre, D, F32), view(o_xn, D, BF16)
        junk = view(o_junk, D, F32)
        Bx, By, Bwpo, Bwpr, Bxn, Bj = Buf("xt"), Buf("yt"), Buf("wpost"), Buf("wpre"), Buf("xn"), Buf("junk")
        Bst = Buf("stat")
        st = small
        sx, sy, sw, sxs = (boundary.sems[i] for i in range(4))
        if y_info is not None:
            coef, post_row, nparts = y_info
            dma("sp", wpost, norms[post_row:post_row + 1, :].partition_broadcast(128), sw, writes=[Bwpo])
        if want_h:
            dma("sp", wpre, norms[pre_row:pre_row + 1, :].partition_broadcast(128), sw, writes=[Bwpr])
        for tt in range(ntok // 128):
            r0 = tok0 + tt * 128
            gt = r0 // 128
            dma("sp", xt, x_src[r0:r0 + 128, :], sx, writes=[Bx])
            if y_info is not None:
                dma("sp", yt, Y[r0:r0 + 128, :], sy, writes=[By])
                p.emit("dve", lambda e, gt=gt: e.reduce_sum(out=st[:, 2:3], in_=ssparts[:, gt, 0:nparts], axis=AX.X),
                       [B_ss], [Bst])
                rstd_from_ss(st[:, 3:4], st[:, 2:3], D, EPS, [Bst], [Bst])
                if coef != 1.0:
                    p.emit("dve", lambda e: e.tensor_scalar(out=st[:, 3:4], in0=st[:, 3:4], scalar1=float(coef),
                                                            scalar2=None, op0=ALU.mult), [Bst], [Bst])
                p.emit("pool", lambda e: e.tensor_tensor(out=yt, in0=yt, in1=wpost, op=ALU.mult), [By, Bwpo], [By])
                p.emit("dve", lambda e: e.scalar_tensor_tensor(out=xt, in0=yt, scalar=st[:, 3:4], in1=xt,
                                                               op0=ALU.mult, op1=ALU.add), [By, Bx, Bst], [Bx])
            if write_x and (y_info is not None or x_src is not out):
                dma("sp", out[r0:r0 + 128, :], xt, sxs, reads=[Bx])
            if not want_h:
                continue
            p.emit("act", lambda e: e.activation(out=junk, in_=xt, func=AF.Square), [Bx], [Bj])
            p.emit("dve", lambda e: e.reduce_sum(out=st[:, 0:1], in_=junk, axis=AX.X), [Bj], [Bst])
            rstd_from_ss(st[:, 1:2], st[:, 0:1], D, EPS, [Bst], [Bst])
            p.emit("dve", lambda e: e.scalar_tensor_tensor(out=xn, in0=xt, scalar=st[:, 1:2], in1=wpre,
                                                           op0=ALU.mult, op1=ALU.mult), [Bx, Bst, Bwpr], [Bxn])
            for g in range(0, KC, 8):
                ng = min(8, KC - g)
                bank = (g // 8) % 2
                pt = psum[:, bank, :].bitcast(BF16)[:, 0:ng * 128].rearrange("p (a b) -> p a b", a=ng)
                for j in range(ng):
                    p.emit("pe", lambda e, j=j, g=g, pt=pt: e.transpose(pt[:, j, :], xn[:, (g + j) * 128:(g + j + 1) * 128],
                                                                         identb),
                           [Bxn, B_identb], [PB[bank]], acc=(j > 0))
                eng = "act" if (g // 8) % 2 == 0 else "dve"
                dst = hT[:, g:g + ng, tt * 128:(tt + 1) * 128]
                if eng == "act":
                    p.emit("act", lambda e, dst=dst, pt=pt: e.activation(out=dst, in_=pt, func=AF.Copy),
                           [PB[bank]], [B_hT])
                else:
                    p.emit("dve", lambda e, dst=dst, pt=pt: e.tensor_copy(out=dst, in_=pt), [PB[bank]], [B_hT])

    boundary.sems = [p.dsem(n) for n in ("bx", "by", "bw", "bxs")]

    ystage_sem = [p.dsem("ys0"), p.dsem("ys1")]

    def evac_y(ps_ap, bank, r0, c0, ncols, col_idx, stg):
        i = evac_y.ctr % 2
        evac_y.ctr += 1
        ys, Bys, jk, Bjk = stg[i]
        p.emit("dve", lambda e: e.tensor_copy(out=ys[:, 0:ncols], in_=ps_ap), [PB[bank]], [Bys])
        p.emit("act", lambda e: e.activation(out=jk[:, 0:ncols], in_=ys[:, 0:ncols], func=AF.Square), [Bys], [Bjk])
        p.emit("dve", lambda e: e.reduce_sum(out=ssparts[:, r0 // 128, col_idx:col_idx + 1], in_=jk[:, 0:ncols],
                                             axis=AX.X), [Bjk], [B_ss])
        dma("sp", Y[r0:r0 + 128, c0:c0 + ncols], ys[:, 0:ncols], ystage_sem[i], reads=[Bys])

    evac_y.ctr = 0

    def make_ystage():
        stg = []
        for i in range(2):
            o1, o2 = balloc(512 * 4), balloc(512 * 4)
            stg.append((view(o1, 512, F32), Buf(f"ys{i}"), view(o2, 512, F32), Buf(f"jk{i}")))
        return stg

    def ffn(idx, x_src, y_prev, pre_row):
        for ps_ in range(NP):
            tok0 = ps_ * T
            boundary(x_src, y_prev, pre_row, tok0, T)
            p.barrier()
            breset()
            o_act = balloc(FC * T * 2)
            actT = view(o_act, FC * T, BF16).rearrange("p (a b) -> p a b", a=FC)
            B_act = Buf("actT")
            o_sg = [balloc(T * 4) for _ in range(2)]
            sg = [view(o, T, F32) for o in o_sg]
            B_sg = [Buf("sg0"), Buf("sg1")]
            stg = make_ystage()
            for j in range(FC):
                wv, wb = wload(wgu[idx][j])
                bg, bu = (2 * j) % 4, (2 * j) % 4 + 1
                for kc in range(KC):
                    mm(psum[:, bg, 0:T], wv[:, kc, 0:128], hT[:, kc, :], kc == 0, kc == KC - 1, [wb, B_hT], PB[bg])
                for kc in range(KC):
                    mm(psum[:, bu, 0:T], wv[:, kc, 128:256], hT[:, kc, :], kc == 0, kc == KC - 1, [wb, B_hT], PB[bu])
                si = j % 2
                p.emit("act", lambda e, si=si, bg=bg: e.activation(out=sg[si], in_=psum[:, bg, 0:T], func=AF.Silu),
                       [PB[bg]], [B_sg[si]])
                p.emit("dve", lambda e, si=si, bu=bu, j=j: e.tensor_tensor(out=actT[:, j, :], in0=sg[si],
                                                                          in1=psum[:, bu, 0:T], op=ALU.mult),
                       [B_sg[si], PB[bu]], [B_act])
            for fb in range(NFB):
                for kg in range(NKG):
                    nk = min(16, FC - kg * 16)
                    wv, wb = wload(wdn[idx][fb, kg][:, 0:nk, :])
                    for kk in range(nk):
                        kc = kg * 16 + kk
                        for tt in range(TT):
                            bank = 4 * (fb % 2) + tt
                            mm(psum[:, bank, :], actT[:, kc, tt * 128:(tt + 1) * 128], wv[:, kk, :], kc == 0,
                               kc == FC - 1, [wb, B_act], PB[bank])
                for tt in range(TT):
                    bank = 4 * (fb % 2) + tt
                    evac_y(psum[:, bank, :], bank, tok0 + tt * 128, fb * 512, 512, fb, stg)
            p.barrier()


    def proj_fm(wblk_ap, nchunk, consume, ntok=T, hsrc=None, Bh=None, kcn=KC, banks=(2, 3)):
        hsrc = hT if hsrc is None else hsrc
        Bh = B_hT if Bh is None else Bh
        wv, wb = wload(wblk_ap)
        for j in range(nchunk):
            bank = banks[proj_fm.ctr % len(banks)]
            proj_fm.ctr += 1
            for kc in range(kcn):
                mm(psum[:, bank, 0:ntok], wv[:, kc, j * 128:(j + 1) * 128], hsrc[:, kc, 0:ntok], kc == 0, kc == kcn - 1,
                   [wb, Bh], PB[bank])
            consume(j, psum[:, bank, 0:ntok], bank)

    proj_fm.ctr = 0

    def proj_tm(wblk_ap, ncols, consume, ntok=T, hsrc=None, Bh=None, kcn=KC, banks=(4, 5, 6, 7), wv_wb=None):
        hsrc = hT if hsrc is None else hsrc
        Bh = B_hT if Bh is None else Bh
        wv, wb = wload(wblk_ap) if wv_wb is None else wv_wb
        for tt in range(ntok // 128):
            bank = banks[proj_tm.ctr % len(banks)]
            proj_tm.ctr += 1
            for kc in range(kcn):
                mm(psum[:, bank, 0:ncols], hsrc[:, kc, tt * 128:(tt + 1) * 128], wv[:, kc, 0:ncols], kc == 0,
                   kc == kcn - 1, [wb, Bh], PB[bank])
            consume(tt, psum[:, bank, 0:ncols], bank)

    proj_tm.ctr = 0

    class Ring:
        def __init__(self, name, n, nelem, dt):
            self.items = []
            for i in range(n):
                o = balloc(nelem * mybir.dt.size(dt))
                self.items.append((view(o, nelem, dt), Buf(f"{name}{i}"), Ring.sems(name, i)))
            self.i = 0

        def next(self):
            it = self.items[self.i % len(self.items)]
            self.i += 1
            return it

        _sems = {}

        @staticmethod
        def sems(name, i):
            key = (name, i)
            if key not in Ring._sems:
                Ring._sems[key] = p.dsem(f"{name}{i}")
            return Ring._sems[key]

    Ring._sems = {}

    def copy_evac(i, dst, src, reads, writes):
        if i % 2 == 0:
            p.emit("act", lambda e: e.activation(out=dst, in_=src, func=AF.Copy), reads, writes)
        else:
            p.emit("dve", lambda e: e.tensor_copy(out=dst, in_=src), reads, writes)

    def mem_attn(x_src, y_prev, pre_row, memn_row):
        MH = 4
        boundary(mem_in, None, memn_row, 0, MEM, write_x=False)
        p.barrier()
        breset()
        o_mk, o_mv = balloc(MH * MEM * 2), balloc((MEM // 128) * MH * 128 * 2)
        memK = view(o_mk, MH * MEM, BF16).rearrange("p (a b) -> p a b", a=MH)
        memV = view(o_mv, (MEM // 128) * MH * 128, BF16).rearrange("p (a b c) -> p a b c", a=MEM // 128, b=MH)
        B_mk, B_mv = Buf("memK"), Buf("memV")
        o_ones = balloc(128 * 2)
        onesb = view(o_ones, 128, BF16)
        B_onesb = Buf("onesb")
        p.emit("dve", lambda e: e.tensor_copy(out=onesb, in_=ones), [B_const], [B_onesb])
        for blk in range(2):
            def consK(j, ps, bank, blk=blk):
                h = blk * 2 + j
                copy_evac(h, memK[:, h, :], ps, [PB[bank]], [B_mk])
            proj_fm(wmkv[blk], 2, consK, ntok=MEM)
        for blk in range(2):
            def consV(tt, ps, bank, blk=blk):
                copy_evac(tt, memV[:, tt, 2 * blk:2 * blk + 2, :], ps.rearrange("p (a b) -> p a b", a=2),
                          [PB[bank]], [B_mv])
            proj_tm(wmkv[2 + blk], 256, consV, ntok=MEM)
        p.barrier()
        keep = big_cursor[0]
        scale = 128 ** -0.5
        for ps_ in range(NP):
            tok0 = ps_ * T
            big_cursor[0] = keep
            boundary_keep(x_src, y_prev, pre_row, tok0, T, keep)
            p.barrier()
            big_cursor[0] = keep
            o_q, o_oc = balloc(MH * T * 2), balloc(MH * T * 2)
            qT = view(o_q, MH * T, BF16).rearrange("p (a b) -> p a b", a=MH)
            ocT = view(o_oc, MH * T, BF16).rearrange("p (a b) -> p a b", a=MH)
            B_q, B_oc = Buf("mqT"), Buf("ocT")
            o_E = [balloc(T * 2) for _ in range(2)]
            E = [view(o, T, BF16) for o in o_E]
            B_E = [Buf("E0"), Buf("E1")]
            o_r = balloc(T * 4)
            rr = view(o_r, T, F32)
            B_r = Buf("rr")
            stg = make_ystage()
            for blk in range(2):
                def consQ(j, ps, bank, blk=blk):
                    h = blk * 2 + j
                    copy_evac(h, qT[:, h, :], ps, [PB[bank]], [B_q])
                proj_fm(wmq[blk], 2, consQ)
            for h in range(MH):
                for mt in range(MEM // 128):
                    sb = mt % 2
                    mm(psum[:, sb, 0:T], memK[:, h, mt * 128:(mt + 1) * 128], qT[:, h, :], True, True, [B_mk, B_q], PB[sb])
                    p.emit("act", lambda e, sb=sb: e.activation(out=E[sb], in_=psum[:, sb, 0:T], func=AF.Exp, scale=scale),
                           [PB[sb]], [B_E[sb]])
                    mm(psum[:, 2, 0:T], memV[:, mt, h, :], E[sb], mt == 0, mt == MEM // 128 - 1, [B_mv, B_E[sb]], PB[2])
                    mm(psum[:, 3, 0:T], onesb, E[sb], mt == 0, mt == MEM // 128 - 1, [B_onesb, B_E[sb]], PB[3])
                p.emit("dve", lambda e: e.reciprocal(out=rr, in_=psum[:, 3, 0:T]), [PB[3]], [B_r])
                p.emit("dve", lambda e, h=h: e.tensor_tensor(out=ocT[:, h, :], in0=psum[:, 2, 0:T], in1=rr, op=ALU.mult),
                       [PB[2], B_r], [B_oc])
            for blk in range(NB256):
                def consO(tt, ps, bank, blk=blk, tok0=tok0):
                    evac_y(ps, bank, tok0 + tt * 128, blk * 256, 256, blk, stg)
                proj_tm(wmo[blk], 256, consO, hsrc=ocT, Bh=B_oc, kcn=4)
            p.barrier()

    def boundary_keep(x_src, y_prev, pre_row, tok0, ntok, keep):
        saved = o_big_holder[0]
        o_big_holder[0] = keep
        try:
            boundary(x_src, y_prev, pre_row, tok0, ntok)
        finally:
            o_big_holder[0] = saved


    def mixer_inproj(x_src, y_prev, pre_row):
        import math
        NQ = HW // 256
        breset()
        o_cos, o_sin = balloc(S * 4), balloc(S * 4)
        COS, SIN = view(o_cos, S, F32), view(o_sin, S, F32)
        B_cs = Buf("cossin")
        o_wg = balloc(KC * 4 * NH * 2)
        wg = view(o_wg, KC * 4 * NH, BF16).rearrange("p (a b) -> p a b", a=KC)
        B_wg = Buf("wg")
        o_invf = balloc(4)
        invf_t = view(o_invf, 1, F32)
        keep = big_cursor[0]
        o_pi, o_pf = balloc(S * 4), balloc(S * 4)
        posi, posf = view(o_pi, S, I32), view(o_pf, S, F32)
        B_pi, B_pf = Buf("posi"), Buf("posf")
        dma("sp", posi, pos_in.partition_broadcast(128), misc_sem, writes=[B_pi])
        dma("sp", invf_t, invf, misc_sem, writes=[B_cs])
        p.emit("pool", lambda e: e.dma_start(out=wg, in_=wgate), [], [B_wg], dsem=misc_sem)
        p.emit("dve", lambda e: e.tensor_copy(out=posf, in_=posi), [B_pi], [B_pf])
        p.emit("dve", lambda e: e.tensor_scalar(out=posf, in0=posf, scalar1=invf_t[:, 0:1], scalar2=None, op0=ALU.mult),
               [B_pf, B_cs], [B_pf])
        TWO_PI = 2.0 * math.pi
        o_kf = balloc(S * 4)
        kf = view(o_kf, S, F32)
        B_kf = Buf("kf")
        TS = lambda **kw: (lambda e: e.tensor_scalar(**kw))
        for tab, shift in ((SIN, 0.0), (COS, 0.25)):
            p.emit("dve", TS(out=tab, in0=posf, scalar1=1.0 / TWO_PI, scalar2=shift, op0=ALU.mult, op1=ALU.add), [B_pf], [B_cs])
            p.emit("dve", lambda e, tab=tab: e.tensor_copy(out=posi, in_=tab), [B_cs], [B_pi])
            p.emit("dve", lambda e: e.tensor_copy(out=kf, in_=posi), [B_pi], [B_kf])
            p.emit("dve", lambda e, tab=tab: e.tensor_tensor(out=tab, in0=tab, in1=kf, op=ALU.subtract), [B_cs, B_kf], [B_cs])
            p.emit("dve", TS(out=kf, in0=tab, scalar1=0.5, scalar2=None, op0=ALU.is_gt), [B_cs], [B_kf])
            p.emit("dve", lambda e, tab=tab: e.tensor_tensor(out=tab, in0=tab, in1=kf, op=ALU.subtract), [B_cs, B_kf], [B_cs])
            p.emit("dve", TS(out=kf, in0=tab, scalar1=-0.5, scalar2=None, op0=ALU.is_lt), [B_cs], [B_kf])
            p.emit("dve", lambda e, tab=tab: e.tensor_tensor(out=tab, in0=tab, in1=kf, op=ALU.add), [B_cs, B_kf], [B_cs])
            p.emit("dve", TS(out=tab, in0=tab, scalar1=0.49999, scalar2=-0.49999, op0=ALU.min, op1=ALU.max), [B_cs], [B_cs])
            p.emit("act", lambda e, tab=tab: e.activation(out=tab, in_=tab, func=AF.Sin, scale=TWO_PI), [B_cs], [B_cs])
        p.barrier()
        RT = cst[:, 5, :]
        for ps_ in range(NP):
            tok0 = ps_ * T
            boundary_keep(x_src, y_prev, pre_row, tok0, T, keep)
            p.barrier()
            big_cursor[0] = keep
            rq = Ring("rq", 2, T, F32)
            rt1 = Ring("rt1", 2, T, F32)
            rt2 = Ring("rt2", 2, T, F32)
            rqo = Ring("rqo", 2, T, BF16)
            rdn = Ring("rdn", 2, T, F32)
            rv = Ring("rv", 2, 256, BF16)
            rz = Ring("rz", 2, 256, F32)
            rg = Ring("rg", 2, 4 * NH, F32)
            del xslots[:]
            set_xslots(3)
            for blk in range(2 * NQ):
                dst_t = QT if blk < NQ else KT

                def consQK(j, ps, bank, blk=blk, dst_t=dst_t, tok0=tok0):
                    c = (blk % NQ) * 2 + j
                    qs, Bq, _ = rq.next()
                    t1, Bt1, _ = rt1.next()
                    t2, Bt2, _ = rt2.next()
                    qo, Bqo, sqo = rqo.next()
                    p.emit("act", lambda e: e.activation(out=qs, in_=ps, func=AF.Copy), [PB[bank]], [Bq])
                    mm(psum[:, 0, 0:T], RT, qs, True, True, [B_const, Bq], PB[0])
                    p.emit("dve", lambda e: e.tensor_tensor(out=t1, in0=qs, in1=COS[:, tok0:tok0 + T], op=ALU.mult),
                           [Bq, B_cs], [Bt1])
                    p.emit("dve", lambda e: e.tensor_tensor(out=t2, in0=psum[:, 0, 0:T], in1=SIN[:, tok0:tok0 + T],
                                                            op=ALU.mult), [PB[0], B_cs], [Bt2])
                    p.emit("dve", lambda e: e.tensor_tensor(out=qo, in0=t1, in1=t2, op=ALU.add), [Bt1, Bt2], [Bqo])
                    dma("sp", dst_t[c, :, tok0:tok0 + T], qo, sqo, reads=[Bqo])
                proj_fm(win[blk], 2, consQK)
            for blk in range(3 * NQ):
                def consDN(j, ps, bank, blk=blk, tok0=tok0):
                    c = blk * 2 + j
                    d, Bd, sd = rdn.next()
                    copy_evac(c, d, ps, [PB[bank]], [Bd])
                    dma("sp", DNT[c, :, tok0:tok0 + T], d, sd, reads=[Bd])
                proj_fm(win[2 * NQ + blk], 2, consDN)
            for blk in range(NQ):
                def consV(tt, ps, bank, blk=blk, tok0=tok0):
                    v, Bv, sv = rv.next()
                    copy_evac(tt, v, ps, [PB[bank]], [Bv])
                    dma("sp", VV[tok0 + tt * 128:tok0 + (tt + 1) * 128, blk * 256:(blk + 1) * 256], v, sv, reads=[Bv])
                proj_tm(win[5 * NQ + blk], 256, consV)
            for blk in range(NQ):
                def consZ(tt, ps, bank, blk=blk, tok0=tok0):
                    z, Bz, sz = rz.next()
                    copy_evac(tt, z, ps, [PB[bank]], [Bz])
                    dma("sp", ZZ[tok0 + tt * 128:tok0 + (tt + 1) * 128, blk * 256:(blk + 1) * 256], z, sz, reads=[Bz])
                proj_tm(win[6 * NQ + blk], 256, consZ)

            def consG(tt, ps, bank, tok0=tok0):
                g, Bg, sg_ = rg.next()
                copy_evac(tt, g, ps, [PB[bank]], [Bg])
                dma("sp", GG[tok0 + tt * 128:tok0 + (tt + 1) * 128, :], g, sg_, reads=[Bg])
            proj_tm(None, 4 * NH, consG, wv_wb=(wg, B_wg))
            p.barrier()

    def diff_attn():
        breset()
        lam0 = cfg.lambda_init
        o_lp = balloc(256 * 4)
        lp = view(o_lp, 256, F32)
        o_lm = balloc(16 * 4)
        lm = view(o_lm, 16, F32)
        B_lp, B_lm = Buf("lp"), Buf("lm")
        o_sw = balloc(8)
        sw = view(o_sw, 2, F32)
        B_sw = Buf("sw")
        o_ones = balloc(128 * 2)
        onesb = view(o_ones, 128, BF16)
        B_onesb = Buf("onesb")
        p.emit("dve", lambda e: e.tensor_copy(out=onesb, in_=ones), [B_const], [B_onesb])
        dma("sp", lp, lam_in.partition_broadcast(128), misc_sem, writes=[B_lp])
        dma("sp", sw[:, 0:1], subln.rearrange("o d -> d o"), misc_sem, writes=[B_sw])
        p.emit("dve", lambda e: e.tensor_scalar(out=sw[:, 1:2], in0=sw[:, 0:1], scalar1=float(1.0 - lam0), scalar2=None,
                                                op0=ALU.mult), [B_sw], [B_sw])
        p.emit("dve", lambda e: e.tensor_tensor(out=lp[:, 0:64], in0=lp[:, 0:64], in1=lp[:, 64:128], op=ALU.mult), [B_lp], [B_lp])
        p.emit("dve", lambda e: e.tensor_tensor(out=lp[:, 128:192], in0=lp[:, 128:192], in1=lp[:, 192:256], op=ALU.mult), [B_lp], [B_lp])
        p.emit("dve", lambda e: e.reduce_sum(out=lm[:, 0:1], in_=lp[:, 0:64], axis=AX.X), [B_lp], [B_lm])
        p.emit("dve", lambda e: e.reduce_sum(out=lm[:, 1:2], in_=lp[:, 128:192], axis=AX.X), [B_lp], [B_lm])
        p.emit("act", lambda e: e.activation(out=lm[:, 2:4], in_=lm[:, 0:2], func=AF.Exp), [B_lm], [B_lm])
        p.emit("dve", lambda e: e.tensor_tensor(out=lm[:, 4:5], in0=lm[:, 3:4], in1=lm[:, 2:3], op=ALU.subtract), [B_lm], [B_lm])
        p.emit("dve", lambda e: e.tensor_scalar(out=lm[:, 4:5], in0=lm[:, 4:5], scalar1=-float(lam0), scalar2=None, op0=ALU.add),
               [B_lm], [B_lm])
        o_q, o_k, o_v = balloc(S * 2), balloc(S * 2), balloc(NT * 128 * 2)
        qt, kt = view(o_q, S, BF16), view(o_k, S, BF16)
        qz = [view(balloc(S * 2), S, BF16) for _ in range(2)]
        B_qz = [Buf("qz0"), Buf("qz1")]
        hm = view(balloc(8), 2, F32)
        B_hm = Buf("hm")
        p.emit("dve", lambda e: e.memset(hm, 0.0), [], [B_hm])
        p.emit("dve", lambda e: e.memset(hm[0:64, 0:1], 1.0), [B_hm], [B_hm])
        p.emit("dve", lambda e: e.memset(hm[64:128, 1:2], 1.0), [B_hm], [B_hm])
        vt = view(o_v, NT * 128, BF16).rearrange("p (a b) -> p a b", a=NT)
        B_q, B_k, B_v = Buf("dq"), Buf("dk"), Buf("dv")
        sq_, sk_, sv_ = p.dsem("dq"), p.dsem("dk"), p.dsem("dv")
        o_E = [balloc(T * 2) for _ in range(2)]
        E = [view(o, T, BF16) for o in o_E]
        B_E = [Buf("E0"), Buf("E1")]
        f32t = lambda: view(balloc(T * 4), T, F32)
        r1, o1, o2, sqv, rs = f32t(), f32t(), f32t(), f32t(), f32t()
        B_r1, B_o1, B_o2, B_sqv, B_rs = Buf("r1"), Buf("o1"), Buf("o2"), Buf("sqv"), Buf("rs")
        ron = Ring("ron", 2, T, BF16)
        for c in range(NH):
            dma("sp", qt, QT[c], sq_, writes=[B_q])
            dma("sp", kt, KT[c], sk_, writes=[B_k])
            dma("sp", vt, VV[:, c * 128:(c + 1) * 128].rearrange("(a p) d -> p a d", p=128), sv_, writes=[B_v])
            for a in range(2):
                eng = "dve" if a == 0 else "pool"
                p.emit(eng, lambda e, a=a: e.tensor_scalar(out=qz[a], in0=qt, scalar1=hm[:, a:a + 1], scalar2=None, op0=ALU.mult),
                       [B_q, B_hm], [B_qz[a]])
            for qg in range(S // T):
                q0 = qg * T
                for a in range(2):
                    pa = slice(a * 64, (a + 1) * 64)
                    bo, bs = 2 + 2 * a, 3 + 2 * a
                    def score(kb, a=a, q0=q0):
                        sb = kb % 2
                        mm(psum[:, sb, 0:T], kt[:, kb * 128:(kb + 1) * 128], qz[a][:, q0:q0 + T], True, True, [B_k, B_qz[a]], PB[sb])
                        p.emit("act", lambda e, sb=sb: e.activation(out=E[sb], in_=psum[:, sb, 0:T], func=AF.Exp, scale=0.125),
                               [PB[sb]], [B_E[sb]])
                    score(0)
                    for kb in range(NT):
                        sb = kb % 2
                        if kb + 1 < NT:
                            score(kb + 1)
                        mm(psum[:, bo, 0:T], vt[:, kb, :], E[sb], kb == 0, kb == NT - 1, [B_v, B_E[sb]], PB[bo])
                        mm(psum[:, bs, 0:T], onesb, E[sb], kb == 0, kb == NT - 1, [B_onesb, B_E[sb]], PB[bs])
                p.emit("dve", lambda e: e.reciprocal(out=r1, in_=psum[:, 3, 0:T]), [PB[3]], [B_r1])
                p.emit("dve", lambda e: e.tensor_tensor(out=o1, in0=psum[:, 2, 0:T], in1=r1, op=ALU.mult), [PB[2], B_r1], [B_o1])
                p.emit("dve", lambda e: e.reciprocal(out=r1, in_=psum[:, 5, 0:T]), [PB[5]], [B_r1])
                p.emit("dve", lambda e: e.tensor_tensor(out=o2, in0=psum[:, 4, 0:T], in1=r1, op=ALU.mult), [PB[4], B_r1], [B_o2])
                p.emit("dve", lambda e: e.scalar_tensor_tensor(out=o1, in0=o2, scalar=lm[:, 4:5], in1=o1, op0=ALU.mult,
                                                               op1=ALU.add), [B_o2, B_o1, B_lm], [B_o1])
                p.emit("act", lambda e: e.activation(out=sqv, in_=o1, func=AF.Square), [B_o1], [B_sqv])
                mm(psum[:, 6, 0:T], ones, sqv, True, True, [B_const, B_sqv], PB[6])
                p.emit("dve", lambda e: e.tensor_scalar(out=rs, in0=psum[:, 6, 0:T], scalar1=1.0 / 128, scalar2=1e-5,
                                                        op0=ALU.mult, op1=ALU.add), [PB[6]], [B_rs])
                p.emit("act", lambda e: e.activation(out=rs, in_=rs, func=AF.Ln), [B_rs], [B_rs])
                p.emit("act", lambda e: e.activation(out=rs, in_=rs, func=AF.Exp, scale=-0.5), [B_rs], [B_rs])
                on, Bon, son = ron.next()
                p.emit("dve", lambda e, on=on: e.scalar_tensor_tensor(out=on, in0=o1, scalar=sw[:, 1:2], in1=rs, op0=ALU.mult,
                                                                      op1=ALU.mult), [B_o1, B_sw, B_rs], [Bon])
                dma("sp", OT[c, :, q0:q0 + T], on, son, reads=[Bon])
        p.barrier()

    op_sem = p.dsem("opl")

    def out_proj():
        for ps_ in range(NP):
            tok0 = ps_ * T
            breset()
            stg = make_ystage()
            set_xslots(4)
            for c_ in range(KC):
                dma("sp", hT[:, c_, :], OT[c_, :, tok0:tok0 + T], op_sem, writes=[B_hT])
            for blk in range(NB256):
                def consO(tt, ps, bank, blk=blk, tok0=tok0):
                    evac_y(ps, bank, tok0 + tt * 128, blk * 256, 256, blk, stg)
                proj_tm(wout[blk], 256, consO)
            p.barrier()


    def gdn():
        o_big_holder[0] = o_hT
        breset()
        G4 = 4 * NH
        f32v = lambda n: view(balloc(n * 4), n, F32)
        gts = f32v(NT * G4).rearrange("p (a b) -> p a b", a=NT)
        dsm = f32v(G4)
        gd = f32v(NT * 2 * NH).rearrange("p (a d h) -> p a d h", a=NT, d=2)
        bd = f32v(NT * 2 * NH).rearrange("p (a d h) -> p a d h", a=NT, d=2)
        NN = NT * NH
        egc_f, negc_f, etail_f, egtot_f = f32v(2 * NN), f32v(2 * NN), f32v(2 * NN), f32v(2 * NN)
        v4 = lambda f: f.rearrange("p (d a h) -> p d a h", d=2, a=NT)
        egc, negc, etail, egtot = v4(egc_f), v4(negc_f), v4(etail_f), v4(egtot_f)
        tmpg_f, tmpg2_f = f32v(NN), f32v(NN)
        tmpg = tmpg_f.rearrange("p (a h) -> p a h", a=NT)
        tmpg2 = tmpg2_f.rearrange("p (a h) -> p a h", a=NT)
        cw = f32v(3 * NH * 5).rearrange("p (c t) -> p c t", c=3 * NH)
        nwb = f32v(128)
        Bg = Buf("gates")
        dma("sp", gts, GG.rearrange("(a p) g -> p a g", p=128), misc_sem, writes=[Bg])
        dma("sp", dsm, dn_small.partition_broadcast(128), misc_sem, writes=[Bg])
        dma("sp", cw, convw, misc_sem, writes=[Bg])
        dma("sp", nwb, dn_normw.partition_broadcast(128), misc_sem, writes=[Bg])
        p.barrier()
        E1 = lambda eng, fn: p.emit(eng, fn, [Bg], [Bg])
        E1("act", lambda e: e.activation(out=dsm[:, 0:2 * NH], in_=dsm[:, 0:2 * NH], func=AF.Exp))
        E1("dve", lambda e: e.tensor_scalar(out=dsm[:, 0:2 * NH], in0=dsm[:, 0:2 * NH], scalar1=-1.0, scalar2=None, op0=ALU.mult))
        for d in range(2):
            a_sl = slice(d * 2 * NH, d * 2 * NH + NH)
            b_sl = slice(d * 2 * NH + NH, (d + 1) * 2 * NH)
            for t in range(NT):
                E1("dve", lambda e, t=t, a_sl=a_sl, d=d: e.tensor_tensor(out=tmpg[:, t, :], in0=gts[:, t, a_sl],
                                                                      in1=dsm[:, 2 * NH + d * NH:2 * NH + (d + 1) * NH], op=ALU.add))
            E1("dve", lambda e: e.tensor_scalar(out=tmpg2, in0=tmpg, scalar1=-1.0, scalar2=None, op0=ALU.mult))
            E1("dve", lambda e: e.tensor_tensor(out=tmpg2, in0=tmpg2, in1=tmpg, op=ALU.max))
            E1("act", lambda e: e.activation(out=tmpg2, in_=tmpg2, func=AF.Exp, scale=-1.0))
            E1("dve", lambda e: e.tensor_scalar(out=tmpg2, in0=tmpg2, scalar1=1.0, scalar2=None, op0=ALU.add))
            E1("act", lambda e: e.activation(out=tmpg2, in_=tmpg2, func=AF.Ln))
            E1("dve", lambda e: e.scalar_tensor_tensor(out=tmpg, in0=tmpg, scalar=0.0, in1=tmpg2, op0=ALU.max, op1=ALU.add))
            for t in range(NT):
                E1("dve", lambda e, t=t, d=d: e.tensor_tensor(out=gd[:, t, d, :], in0=tmpg[:, t, :],
                                                             in1=dsm[:, d * NH:(d + 1) * NH], op=ALU.mult))
            E1("act", lambda e, d=d, b_sl=b_sl: e.activation(out=bd[:, :, d, :], in_=gts[:, :, b_sl], func=AF.Sigmoid))
            mincl = cst[:, 1 if d == 0 else 3, :]
            dsl = slice(d * NN, (d + 1) * NN)
            E1("dve", lambda e, d=d: e.tensor_copy(out=tmpg2, in_=gd[:, :, d, :]))
            mm(psum[:, 0, 0:NN], mincl, tmpg2_f, True, True, [Bg, B_const], PB[0])
            mm(psum[:, 1, 0:NN], ones, tmpg2_f, True, True, [Bg, B_const], PB[1])
            CL = lambda o_, i_: (lambda e: e.tensor_scalar(out=o_, in0=i_, scalar1=-80.0, scalar2=None, op0=ALU.max))
            p.emit("dve", CL(egc_f[:, dsl], psum[:, 0, 0:NN]), [PB[0]], [Bg])
            p.emit("dve", CL(egtot_f[:, dsl], psum[:, 1, 0:NN]), [PB[1]], [Bg])
            p.emit("dve", lambda e: e.tensor_copy(out=tmpg_f, in_=psum[:, 0, 0:NN]), [PB[0]], [Bg])
            p.emit("dve", lambda e, dsl=dsl: e.tensor_tensor(out=etail_f[:, dsl], in0=psum[:, 1, 0:NN], in1=tmpg_f, op=ALU.subtract),
                   [PB[1], Bg], [Bg])
            E1("dve", CL(etail_f[:, dsl], etail_f[:, dsl]))
            E1("act", lambda e, dsl=dsl: e.activation(out=egc_f[:, dsl], in_=egc_f[:, dsl], func=AF.Exp))
            E1("act", lambda e, dsl=dsl: e.activation(out=egtot_f[:, dsl], in_=egtot_f[:, dsl], func=AF.Exp))
            E1("act", lambda e, dsl=dsl: e.activation(out=etail_f[:, dsl], in_=etail_f[:, dsl], func=AF.Exp))
            E1("dve", lambda e, dsl=dsl: e.tensor_scalar(out=negc_f[:, dsl], in0=egc_f[:, dsl], scalar1=-1.0, scalar2=None, op0=ALU.mult))
        p.barrier()
        HG = 2 if NH % 2 == 0 else 1
        xp = [f32v(S + 4)]
        B_xp = [Buf("xp0")]
        xp_sem = [p.dsem("xp0")]
        acc, sqb = f32v(S), f32v(S)
        B_acc, B_sqb = Buf("acc"), Buf("sqb")
        rsb = f32v(512)
        B_rsb = Buf("rsb")
        zt = f32v(NT * 128).rearrange("p (a b) -> p a b", a=NT)
        B_z = Buf("zt")
        z_sem = p.dsem("zt")
        st2 = f32v(2 * NT)
        B_st2 = Buf("st2")
        HC = []
        for hi in range(HG):
            c_ = {}
            c_["qkv"] = (f32v(S), f32v(S), f32v(S))
            c_["B_qkv"] = [Buf(f"qn{hi}"), Buf(f"kn{hi}"), Buf(f"vs{hi}")]
            c_["oacc"] = [f32v(NT * 128).rearrange("p (a b) -> p a b", a=NT) for _ in range(2)]
            c_["B_oacc"] = [Buf(f"of{hi}"), Buf(f"ob{hi}")]
            c_["Sst"] = [f32v(128), f32v(128)]
            c_["B_S"] = [Buf(f"S0{hi}"), Buf(f"S1{hi}")]
            W = []
            for d in range(2):
                w_ = {n: f32v(128) for n in ("Gm", "DT", "DTi", "DTs", "X", "XT", "QKD", "Ra", "Rb", "P", "PT", "P2", "P2T", "kt",
                                             "vt", "r", "vn", "t1")}
                w_["B"] = {n: Buf(n + str(d) + str(hi)) for n in w_}
                W.append(w_)
            c_["W"] = W
            HC.append(c_)
        rog = Ring("rog", 2, 128, BF16)
        rot = Ring("rot", 2, 128, BF16)
        for i_ in range(1):
            p.emit("pool", lambda e, i_=i_: e.memset(xp[i_][:, 0:2], 0.0), [], [B_xp[i_]])
            p.emit("pool", lambda e, i_=i_: e.memset(xp[i_][:, S + 2:S + 4], 0.0), [], [B_xp[i_]])
        pb = [0]

        def nb():
            b = pb[0] % 8
            pb[0] += 1
            return b

        def mmf(lhsT, rhs, reads):
            b = nb()
            o_ = psum[:, b, 0:128]
            mm(o_, lhsT, rhs, True, True, reads, PB[b])
            return o_, PB[b]

        def prep(h, c_):
            qn, kn, vs = c_["qkv"]
            B_qkv = c_["B_qkv"]
            Sst, B_S = c_["Sst"], c_["B_S"]
            for qi, (dst, Bd) in enumerate(zip((qn, kn, vs), B_qkv)):
                ch = qi * NH + h
                i_ = 0
                x_, Bx_ = xp[i_], B_xp[i_]
                dma("sp", x_[:, 2:S + 2], DNT[ch], xp_sem[i_], writes=[Bx_])
                eng = "dve" if qi != 1 else "pool"
                p.emit(eng, lambda e, x_=x_, ch=ch: e.tensor_scalar(out=acc, in0=x_[:, 0:S], scalar1=cw[:, ch, 0:1], scalar2=None,
                                                                   op0=ALU.mult), [Bx_, Bg], [B_acc])
                for j in range(1, 5):
                    p.emit("dve", lambda e, x_=x_, ch=ch, j=j: e.scalar_tensor_tensor(out=acc, in0=x_[:, j:j + S], scalar=cw[:, ch, j:j + 1],
                                                                                   in1=acc, op0=ALU.mult, op1=ALU.add), [Bx_, Bg, B_acc], [B_acc])
                if qi == 2:
                    p.emit("act", lambda e, dst=dst: e.activation(out=dst, in_=acc, func=AF.Silu), [B_acc], [Bd])
                    continue
                p.emit("act", lambda e: e.activation(out=acc, in_=acc, func=AF.Silu), [B_acc], [B_acc])
                p.emit("act", lambda e: e.activation(out=sqb, in_=acc, func=AF.Square), [B_acc], [B_sqb])
                for b0 in range(0, S, 512):
                    bk = nb()
                    mm(psum[:, bk, :], ones, sqb[:, b0:b0 + 512], True, True, [B_const, B_sqb], PB[bk])
                    p.emit("dve", lambda e, bk=bk: e.tensor_scalar(out=rsb, in0=psum[:, bk, :], scalar1=1e-6, scalar2=None, op0=ALU.add),
                           [PB[bk]], [B_rsb])
                    p.emit("act", lambda e: e.activation(out=rsb, in_=rsb, func=AF.Ln), [B_rsb], [B_rsb])
                    p.emit("act", lambda e: e.activation(out=rsb, in_=rsb, func=AF.Exp, scale=-0.5), [B_rsb], [B_rsb])
                    sc = float(128 ** -0.5) if qi == 0 else 1.0
                    p.emit("dve", lambda e, dst=dst, b0=b0, sc=sc: e.scalar_tensor_tensor(out=dst[:, b0:b0 + 512], in0=acc[:, b0:b0 + 512],
                                                                                       scalar=sc, in1=rsb, op0=ALU.mult, op1=ALU.mult),
                           [B_acc, B_rsb], [Bd])
            for d in range(2):
                p.emit("pool", lambda e, d=d: e.memset(Sst[d], 0.0), [], [B_S[d]])
        def step(h, c_, n_, d):
            qn, kn, vs = c_["qkv"]
            B_qkv = c_["B_qkv"]
            Sst, B_S = c_["Sst"], c_["B_S"]
            oacc, B_oacc = c_["oacc"], c_["B_oacc"]
            W = c_["W"]
            if True:
                if True:
                    t = n_ if d == 0 else NT - 1 - n_
                    w_ = W[d]
                    B_ = w_["B"]
                    cs = slice(t * 128, (t + 1) * 128)
                    MinclG = cst[:, 1 if d == 0 else 3, :]
                    Mstr = cst[:, 2 if d == 0 else 4, :]
                    maskI = cst[:, 1 if d == 0 else 3, :]
                    maskS = cst[:, 4 if d == 0 else 2, :]
                    gcol = gd[:, t, d, h:h + 1]
                    bcol = bd[:, t, d, h:h + 1]
                    Bq, Bk, Bv = B_qkv
                    p.emit("pool", lambda e, w_=w_, MinclG=MinclG, gcol=gcol: e.tensor_scalar(out=w_["Gm"], in0=MinclG, scalar1=gcol,
                                                                                           scalar2=None, op0=ALU.mult), [B_const, Bg], [B_["Gm"]])
                    ps, Bp = mmf(Mstr, w_["Gm"], [B_const, B_["Gm"]])
                    p.emit("dve", lambda e, w_=w_, ps=ps: e.tensor_scalar(out=w_["DT"], in0=ps, scalar1=-80.0, scalar2=None, op0=ALU.max),
                           [Bp], [B_["DT"]])
                    p.emit("act", lambda e, w_=w_: e.activation(out=w_["DT"], in_=w_["DT"], func=AF.Exp), [B_["DT"]], [B_["DT"]])
                    p.emit("pool", lambda e, w_=w_, maskI=maskI: e.tensor_tensor(out=w_["DTi"], in0=w_["DT"], in1=maskI, op=ALU.mult),
                           [B_["DT"], B_const], [B_["DTi"]])
                    p.emit("pool", lambda e, w_=w_, maskS=maskS: e.tensor_tensor(out=w_["DTs"], in0=w_["DT"], in1=maskS, op=ALU.mult),
                           [B_["DT"], B_const], [B_["DTs"]])
                    yield
                    ps, Bp = mmf(kn[:, cs], kn[:, cs], [Bk])
                    p.emit("dve", lambda e, w_=w_, ps=ps, bcol=bcol: e.scalar_tensor_tensor(out=w_["X"], in0=ps, scalar=bcol, in1=w_["DTs"],
                                                                                        op0=ALU.mult, op1=ALU.mult), [Bp, Bg, B_["DTs"]], [B_["X"]])
                    yield
                    ps, Bp = mmf(kn[:, cs], qn[:, cs], [Bk, Bq])
                    p.emit("dve", lambda e, w_=w_, ps=ps: e.tensor_tensor(out=w_["QKD"], in0=ps, in1=w_["DTi"], op=ALU.mult),
                           [Bp, B_["DTi"]], [B_["QKD"]])
                    yield
                    ps, Bp = mmf(w_["X"], ident, [B_["X"], B_const])
                    p.emit("act", lambda e, w_=w_, ps=ps: e.activation(out=w_["XT"], in_=ps, func=AF.Copy), [Bp], [B_["XT"]])
                    p.emit("dve", lambda e, w_=w_: e.tensor_tensor(out=w_["Ra"], in0=ident, in1=w_["X"], op=ALU.subtract),
                           [B_const, B_["X"]], [B_["Ra"]])
                    P_, PT_, R_ = ("X", "XT", "Ra")
                    for lvl in range(1, 8):
                        if (1 << lvl) >= 128:
                            break
                        last = (1 << (lvl + 1)) >= 128
                        nP, nPT = ("P2", "P2T") if P_ in ("X", "P") else ("P", "PT")
                        yield
                        ps, Bp = mmf(w_[P_], w_[PT_], [B_[P_], B_[PT_]])
                        p.emit("act", lambda e, w_=w_, ps=ps, nPT=nPT: e.activation(out=w_[nPT], in_=ps, func=AF.Copy), [Bp], [B_[nPT]])
                        if not last:
                            yield
                            ps, Bp = mmf(w_[PT_], w_[P_], [B_[P_], B_[PT_]])
                            p.emit("dve", lambda e, w_=w_, ps=ps, nP=nP: e.tensor_copy(out=w_[nP], in_=ps), [Bp], [B_[nP]])
                        nR = "Rb" if R_ == "Ra" else "Ra"
                        yield
                        ps, Bp = mmf(w_[nPT], w_[R_], [B_[nPT], B_[R_]])
                        p.emit("dve", lambda e, w_=w_, ps=ps, R_=R_, nR=nR: e.tensor_tensor(out=w_[nR], in0=ps, in1=w_[R_], op=ALU.add),
                               [Bp, B_[R_]], [B_[nR]])
                        P_, PT_, R_ = nP, nPT, nR
                    yield
                    ps, Bp = mmf(kn[:, cs], ident, [Bk, B_const])
                    p.emit("dve", lambda e, w_=w_, ps=ps, t=t, d=d, h=h: e.tensor_scalar(out=w_["kt"], in0=ps, scalar1=etail[:, d, t, h:h + 1],
                                                                                  scalar2=None, op0=ALU.mult), [Bp, Bg], [B_["kt"]])
                    yield
                    ps, Bp = mmf(vs[:, cs], ident, [Bv, B_const])
                    p.emit("act", lambda e, w_=w_, ps=ps: e.activation(out=w_["vt"], in_=ps, func=AF.Copy), [Bp], [B_["vt"]])
                    yield
                    ps, Bp = mmf(kn[:, cs], Sst[d], [Bk, B_S[d]])
                    p.emit("dve", lambda e, w_=w_, ps=ps, t=t, d=d, h=h: e.scalar_tensor_tensor(out=w_["r"], in0=ps, scalar=negc[:, d, t, h:h + 1],
                                                                                         in1=w_["vt"], op0=ALU.mult, op1=ALU.add),
                           [Bp, Bg, B_["vt"]], [B_["r"]])
                    yield
                    ps, Bp = mmf(w_[R_], w_["r"], [B_[R_], B_["r"]])
                    p.emit("dve", lambda e, w_=w_, ps=ps, bcol=bcol: e.tensor_scalar(out=w_["vn"], in0=ps, scalar1=bcol, scalar2=None,
                                                                                   op0=ALU.mult), [Bp, Bg], [B_["vn"]])
                    yield
                    ps, Bp = mmf(qn[:, cs], Sst[d], [Bq, B_S[d]])
                    p.emit("dve", lambda e, w_=w_, ps=ps, t=t, d=d, h=h: e.tensor_scalar(out=w_["t1"], in0=ps, scalar1=egc[:, d, t, h:h + 1],
                                                                                  scalar2=None, op0=ALU.mult), [Bp, Bg], [B_["t1"]])
                    yield
                    ps, Bp = mmf(w_["QKD"], w_["vn"], [B_["QKD"], B_["vn"]])
                    p.emit("dve", lambda e, w_=w_, ps=ps, t=t, d=d, h=h: e.tensor_tensor(out=oacc[d][:, t, :], in0=ps, in1=w_["t1"], op=ALU.add),
                           [Bp, B_["t1"]], [B_oacc[d]])
                    yield
                    ps, Bp = mmf(w_["kt"], w_["vn"], [B_["kt"], B_["vn"]])
                    p.emit("dve", lambda e, ps=ps, t=t, d=d, h=h: e.scalar_tensor_tensor(out=Sst[d], in0=Sst[d], scalar=egtot[:, d, t, h:h + 1],
                                                                                  in1=ps, op0=ALU.mult, op1=ALU.add), [Bp, Bg, B_S[d]], [B_S[d]])
        def gate(h, c_):
            oacc, B_oacc = c_["oacc"], c_["B_oacc"]
            dma("sp", zt, ZZ[:, h * 128:(h + 1) * 128].rearrange("(a p) d -> p a d", p=128), z_sem, writes=[B_z])
            of_, ob_ = oacc
            p.emit("dve", lambda e: e.tensor_tensor(out=of_, in0=of_, in1=ob_, op=ALU.add), [B_oacc[0], B_oacc[1]], [B_oacc[0]])
            p.emit("act", lambda e: e.activation(out=ob_, in_=of_, func=AF.Square), [B_oacc[0]], [B_oacc[1]])
            p.emit("dve", lambda e: e.reduce_sum(out=st2[:, 0:NT], in_=ob_, axis=AX.X), [B_oacc[1]], [B_st2])
            rstd_from_ss(st2[:, 0:NT], st2[:, 0:NT], 128, 1e-6, [B_st2], [B_st2])
            p.emit("act", lambda e: e.activation(out=zt, in_=zt, func=AF.Silu), [B_z], [B_z])
            for t in range(NT):
                og, Bog, _ = rog.next()
                ot, Bot, sot = rot.next()
                p.emit("dve", lambda e, t=t: e.scalar_tensor_tensor(out=ob_[:, t, :], in0=of_[:, t, :], scalar=st2[:, t:t + 1], in1=nwb,
                                                                   op0=ALU.mult, op1=ALU.mult), [B_oacc[0], B_st2, Bg], [B_oacc[1]])
                p.emit("dve", lambda e, t=t, og=og: e.tensor_tensor(out=og, in0=ob_[:, t, :], in1=zt[:, t, :], op=ALU.mult),
                       [B_oacc[1], B_z], [Bog])
                bk = nb()
                pt = psum[:, bk, :].bitcast(BF16)[:, 0:128]
                p.emit("pe", lambda e, pt=pt, og=og: e.transpose(pt, og, identb), [Bog, B_identb], [PB[bk]])
                p.emit("act", lambda e, pt=pt, ot=ot: e.activation(out=ot, in_=pt, func=AF.Copy), [PB[bk]], [Bot])
                dma("sp", OT[NH + h, :, t * 128:(t + 1) * 128], ot, sot, reads=[Bot])

        for h0 in range(0, NH, HG):
            for hi in range(HG):
                prep(h0 + hi, HC[hi])
            for n_ in range(NT):
                gens = [step(h0 + hi, HC[hi], n_, d) for hi in range(HG) for d in range(2)]
                while gens:
                    alive = []
                    for g_ in gens:
                        try:
                            next(g_)
                            alive.append(g_)
                        except StopIteration:
                            pass
                    gens = alive
            for hi in range(HG):
                gate(h0 + hi, HC[hi])
        p.barrier()
        o_big_holder[0] = o_big

    dbg_sem = p.dsem("dbg")

    def dbg(name, v, bufs):
        t = nc.dram_tensor(name, list(v.shape), v.dtype, kind="ExternalOutput").ap()
        dma("sp", t, v, dbg_sem, reads=bufs)

    k.phase = dict(dbg=dbg, gdn=gdn, mixer_inproj=mixer_inproj, diff_attn=diff_attn, out_proj=out_proj, mem_attn=mem_attn, boundary=boundary, ffn=ffn, evac_y=evac_y, make_ystage=make_ystage, wload=wload, mm=mm,
                   dma=dma, view=view, balloc=balloc, breset=breset, rstd_from_ss=rstd_from_ss)
    k.syms = dict(locals())
    return k


def finish(k, final_events_wait=True):
    p, nc, es = k.p, k.nc, k.es
    p.barrier()
    with nc.Block() as block:
        p.finalize(block)
    es.close()
    return nc


def blk256(w, KC):
    K, N = w.shape
    return np.ascontiguousarray(w.reshape(KC, 128, N // 256, 256).transpose(2, 1, 0, 3))


def lay_wgu(w, cfg):
    K, N2 = w.shape
    FC, KC = cfg.FC, cfg.KC
    g = w[:, :cfg.DFF].reshape(KC, 128, FC, 128)
    u = w[:, cfg.DFF:].reshape(KC, 128, FC, 128)
    gu = np.concatenate([g, u], axis=3)
    return np.ascontiguousarray(gu.transpose(2, 1, 0, 3))


def lay_wdn(w, cfg):
    FC, NKG, D = cfg.FC, cfg.NKG, cfg.D
    wp = np.zeros((NKG * 16 * 128, D), np.float32)
    wp[:cfg.DFF] = w
    a = wp.reshape(NKG, 16, 128, D // 512, 512)
    return np.ascontiguousarray(a.transpose(3, 0, 2, 1, 4))


def make_consts():
    c = np.zeros((128, 8, 128), np.float32)
    c[:, 0, :] = np.eye(128)
    i = np.arange(128)
    c[:, 1, :] = (i[:, None] <= i[None, :])
    c[:, 2, :] = (i[:, None] > i[None, :])
    c[:, 3, :] = (i[:, None] >= i[None, :])
    c[:, 4, :] = (i[:, None] < i[None, :])
    RT = np.zeros((128, 128), np.float32)
    for p_ in range(128):
        d = p_ % 64
        if d < 8:
            RT[p_ + 8, p_] = -1.0
        elif d < 16:
            RT[p_ - 8, p_] = 1.0
    c[:, 5, :] = RT
    c[:, 6, :] = 1.0
    return c


def host_mixer_inputs(cfg, w_in, w_out, lam, subln, pos):
    HW, NH, KC = cfg.HW, cfg.NH, cfg.KC
    cols = np.concatenate([np.arange(0, 2 * HW), np.arange(3 * HW, 6 * HW), np.arange(2 * HW, 3 * HW),
                           np.arange(6 * HW, 7 * HW)])
    d = {}
    d["win"] = blk256(w_in[:, cols], KC)
    wg = w_in[:, 7 * HW:]
    d["wgate"] = np.ascontiguousarray(wg.reshape(KC, 128, 4 * NH).transpose(1, 0, 2))
    d["wout"] = blk256(w_out, KC)
    d["lam"] = np.ascontiguousarray(lam.reshape(1, 256))
    d["subln"] = np.ascontiguousarray(subln.reshape(1, 128))
    d["pos"] = np.ascontiguousarray(pos.reshape(1, -1).astype(np.int32))
    invf = np.zeros((128, 1), np.float32)
    base = (np.float32(cfg.rope_theta) ** (-np.arange(0, 16, 2, dtype=np.float32) / np.float32(16))).astype(np.float32)
    for p_ in range(128):
        dd = p_ % 64
        if dd < 16:
            invf[p_, 0] = base[dd % 8]
    d["invf"] = invf
    return d


def host_gdn_inputs(cfg, conv_w, a_log, dt_bias, normw):
    NH = cfg.NH
    d = {}
    d["convw"] = np.ascontiguousarray(conv_w.T.reshape(3 * NH, 128, 5).transpose(1, 0, 2))
    d["dn_small"] = np.ascontiguousarray(np.concatenate([a_log.reshape(-1), dt_bias.reshape(-1)]).reshape(1, 4 * NH))
    d["dn_normw"] = np.ascontiguousarray(normw.reshape(1, 128))
    return d


_CACHE = {}


def build_full(cfg):
    k = build(cfg)
    ph = k.phase
    out = k.syms["out"]
    NFB, NB256 = cfg.D // 512, cfg.D // 256
    ph["ffn"](0, k.dram["x"], None, 0)
    ph["mixer_inproj"](out, (0.5, 1, NFB), 2)
    ph["diff_attn"]()
    ph["gdn"]()
    ph["out_proj"]()
    ph["mem_attn"](out, (1.0, 3, NB256), 4, 5)
    ph["ffn"](1, out, (1.0, 6, NB256), 7)
    ph["boundary"](out, (0.5, 8, NFB), 0, 0, cfg.S, want_h=False)
    return k, finish(k)


def kernel(x, mem, positions, ffn1_norms, ffn1_w_gu, ffn1_w_down, mix_norms, mix_w_in, dn_conv_w, dn_a_log,
           dn_dt_bias, dn_norm_w, diff_lambda, diff_subln_w, mix_w_out, mem_norms, mem_w_q, mem_w_kv, mem_w_o,
           ffn2_norms, ffn2_w_gu, ffn2_w_down):
    f = lambda a: np.asarray(a, dtype=np.float32)
    x = f(x)
    B, S, D = x.shape
    cfg = Cfg(D=D, S=S, DFF=f(ffn1_w_down).shape[1], MEM=np.asarray(mem).shape[1], ncores=B)
    k, nc = build_full(cfg)
    shared = {}
    shared["wgu1"] = lay_wgu(f(ffn1_w_gu)[0], cfg)
    shared["wdn1"] = lay_wdn(f(ffn1_w_down)[0], cfg)
    shared["wgu2"] = lay_wgu(f(ffn2_w_gu)[0], cfg)
    shared["wdn2"] = lay_wdn(f(ffn2_w_down)[0], cfg)
    shared.update(host_gdn_inputs(cfg, f(dn_conv_w)[0], f(dn_a_log)[0], f(dn_dt_bias)[0], f(dn_norm_w)[0]))
    shared["wmq"] = blk256(f(mem_w_q)[0], cfg.KC)
    shared["wmkv"] = blk256(f(mem_w_kv)[0], cfg.KC)
    shared["wmo"] = blk256(f(mem_w_o)[0], 4)
    shared["norms"] = np.ascontiguousarray(np.concatenate([f(ffn1_norms)[0], f(mix_norms)[0], f(mem_norms)[0][[0, 1, 2]],
                                                           f(ffn2_norms)[0]], axis=0))
    shared["consts"] = make_consts()
    pos = np.asarray(positions)
    in_maps = []
    for b in range(B):
        m = dict(shared)
        m.update(host_mixer_inputs(cfg, f(mix_w_in)[0], f(mix_w_out)[0], f(diff_lambda)[0], f(diff_subln_w)[0], pos[b])
                 if b == 0 else {kk: in_maps[0][kk] for kk in ("win", "wgate", "wout", "lam", "subln", "invf")})
        m["pos"] = np.ascontiguousarray(pos[b].reshape(1, -1).astype(np.int32))
        m["x"] = np.ascontiguousarray(x[b])
        m["mem"] = np.ascontiguousarray(f(mem)[b])
        in_maps.append(m)
    res = run_bass_kernel_spmd(nc, in_maps, core_ids=list(range(B)))
    return np.stack([np.asarray(r["out"]) for r in res.results], axis=0).astype(np.float32)
```

```python
import numpy as np
from contextlib import ExitStack
import concourse.bass as bass
import concourse.mybir as mybir
from concourse.bass_utils import run_bass_kernel_spmd

F32 = mybir.dt.float32
BF16 = mybir.dt.bfloat16
I32 = mybir.dt.int32
AF = mybir.ActivationFunctionType
ALU = mybir.AluOpType
AX = mybir.AxisListType


class Cfg:
    def __init__(self, D=4096, S=2048, DFF=11008, MEM=256, ncores=8):
        self.D, self.S, self.DFF, self.MEM, self.ncores = D, S, DFF, MEM, ncores
        self.KC = D // 128
        self.NT = S // 128
        self.T = min(512, S)
        self.NP = S // self.T
        self.TT = self.T // 128
        self.FC = DFF // 128
        self.NKG = (self.FC + 15) // 16
        self.HW = D // 2
        self.NH = self.HW // 128
        self.INCOLS = 3 * self.HW + 4 * self.HW + 4 * self.NH
        self.MH = 4
        self.lambda_init = 0.2
        self.rope_theta = 500000.0


class Buf:
    __slots__ = ("w", "r", "name")

    def __init__(self, name=""):
        self.w = None
        self.r = []
        self.name = name


class DSem:
    def __init__(self, sem):
        self.sem = sem
        self.count = 0


ENGS = ("pe", "act", "dve", "pool", "sp")


class Prog:
    def __init__(self, nc, es):
        self.nc = nc
        self.es = es
        self.ops = {e: [] for e in ENGS}
        self.dsems = []
        self.all_dma_events = []

    def dsem(self, name):
        d = DSem(self.es.enter_context(self.nc.semaphore("d_" + name)))
        self.dsems.append(d)
        return d

    def emit(self, eng, fn, reads=(), writes=(), dsem=None, acc=False):
        deps = []
        for b in reads:
            if b.w is not None:
                deps.append(b.w)
        for b in writes:
            if b.w is not None:
                if not (acc and b.w[0] == "c" and b.w[1] == "pe"):
                    deps.append(b.w)
            deps.extend(b.r)
        seq = len(self.ops[eng])
        if dsem is None:
            ev = ("c", eng, seq)
        else:
            dsem.count += 16
            ev = ("d", dsem, dsem.count)
            self.all_dma_events.append(ev)
        deps2 = []
        for d in deps:
            if d[0] == "c" and d[1] == eng and eng == "pe":
                continue
            deps2.append(d)
        self.ops[eng].append([deps2, fn, ev, dsem])
        for b in reads:
            b.r.append(ev)
        for b in writes:
            b.w = ev
            b.r = []
        return ev

    def barrier(self):
        lasts = []
        for e in ENGS:
            if self.ops[e]:
                for op in reversed(self.ops[e]):
                    if op[2][0] == "c" and op[1] is not None:
                        lasts.append(op[2])
                        break
        dm = list(self.all_dma_events)
        for e in ENGS:
            self.ops[e].append([lasts + dm, None, ("c", e, len(self.ops[e])), None])
        self.all_dma_events = []

    def finalize(self, block):
        nc = self.nc
        csem = {e: self.es.enter_context(nc.semaphore("c_" + e)) for e in ENGS}
        waited = {e: set() for e in ENGS}
        for e in ENGS:
            for deps, fn, ev, ds in self.ops[e]:
                for d in deps:
                    if d[0] == "c":
                        waited[d[1]].add(d[2])
        rank = {}
        for e in ENGS:
            r = 0
            for seq in sorted(waited[e]):
                assert self.ops[e][seq][1] is not None, "wait on a barrier pseudo-op"
                r += 1
                rank[(e, seq)] = r
        ops = self.ops

        def run(e, eng):
            have = {}
            for seq, (deps, fn, ev, ds) in enumerate(ops[e]):
                need = {}
                for d in deps:
                    if d[0] == "c":
                        key = ("c", d[1])
                        val = rank[(d[1], d[2])]
                        sem = csem[d[1]]
                    else:
                        key = ("d", id(d[1]))
                        val = d[2]
                        sem = d[1].sem
                    if have.get(key, 0) >= val:
                        continue
                    if key not in need or need[key][1] < val:
                        need[key] = (sem, val)
                for key, (sem, val) in need.items():
                    eng.wait_ge(sem, val)
                    have[key] = val
                if fn is None:
                    continue
                ins = fn(eng)
                if ds is not None:
                    ins.then_inc(ds.sem, 16)
                elif (e, seq) in rank:
                    ins.then_inc(csem[e], 1)

        block.tensor(lambda eng: run("pe", eng))
        block.scalar(lambda eng: run("act", eng))
        block.vector(lambda eng: run("dve", eng))
        block.gpsimd(lambda eng: run("pool", eng))
        block.sync(lambda eng: run("sp", eng))


class K:
    def __init__(self, cfg):
        self.cfg = cfg
        self.nc = bass.Bass("TRN2", target_bir_lowering=False)
        self.es = ExitStack()
        self.p = Prog(self.nc, self.es)
        self.dram = {}

    def din(self, name, shape, dt=F32):
        t = self.nc.dram_tensor(name, list(shape), dt, kind="ExternalInput").ap()
        self.dram[name] = t
        return t

    def dscr(self, name, shape, dt=F32):
        t = self.nc.dram_tensor(name, list(shape), dt,
                                kind="ExternalOutput" if getattr(self.cfg, "debug", False) else "Internal").ap()
        self.dram[name] = t
        return t


def build(cfg):
    k = K(cfg)
    nc, es, p = k.nc, k.es, k.p
    D, S, DFF, KC, NT, T, NP, TT, FC, NKG = (cfg.D, cfg.S, cfg.DFF, cfg.KC, cfg.NT, cfg.T, cfg.NP,
                                             cfg.TT, cfg.FC, cfg.NKG)
    HW, NH, MEM = cfg.HW, cfg.NH, cfg.MEM
    NB256 = D // 256
    NFB = D // 512
    EPS = 1e-6

    x_in = k.din("x", [S, D])
    mem_in = k.din("mem", [MEM, D])
    pos_in = k.din("pos", [1, S], I32)
    out = nc.dram_tensor("out", [S, D], F32, kind="ExternalOutput").ap()
    wgu = [k.din(f"wgu{i}", [FC, 128, KC, 256]) for i in (1, 2)]
    wdn = [k.din(f"wdn{i}", [NFB, NKG, 128, 16, 512]) for i in (1, 2)]
    NBIN = (3 * HW + 4 * HW) // 256
    win = k.din("win", [NBIN, 128, KC, 256])
    wgate = k.din("wgate", [128, KC, 4 * NH])
    wout = k.din("wout", [NB256, 128, KC, 256])
    wmq = k.din("wmq", [2, 128, KC, 256])
    wmkv = k.din("wmkv", [4, 128, KC, 256])
    wmo = k.din("wmo", [NB256, 128, 4, 256])
    norms = k.din("norms", [9, D])
    convw = k.din("convw", [128, 3 * NH, 5])
    dn_small = k.din("dn_small", [1, 4 * NH])
    dn_normw = k.din("dn_normw", [1, 128])
    subln = k.din("subln", [1, 128])
    lam_in = k.din("lam", [1, 256])
    consts = k.din("consts", [128, 8, 128])
    invf = k.din("invf", [128, 1])
    Y = k.dscr("Y", [S, D])
    QT = k.dscr("QT", [NH, 128, S], BF16)
    KT = k.dscr("KT", [NH, 128, S], BF16)
    VV = k.dscr("VV", [S, HW], BF16)
    DNT = k.dscr("DNT", [3 * NH, 128, S])
    ZZ = k.dscr("ZZ", [S, HW])
    GG = k.dscr("GG", [S, 4 * NH])
    OT = k.dscr("OT", [KC, 128, S], BF16)
    MQT = k.dscr("MQT", [4, 128, S], BF16)

    ARENA_F32 = 46000
    arena = es.enter_context(nc.sbuf_tensor("arena", [128, ARENA_F32], F32))
    psum = es.enter_context(nc.psum_tensor("psum", [128, 8, 512], F32))
    cursor = [0]

    def alloc(nbytes):
        off = cursor[0]
        cursor[0] += (nbytes + 31) // 32 * 32
        assert cursor[0] <= ARENA_F32 * 4, f"arena overflow {cursor[0]}"
        return off

    def view(off, n_elems, dt):
        nb = n_elems * mybir.dt.size(dt)
        assert off % 4 == 0 and nb % 4 == 0
        v = arena[:, off // 4:(off + nb) // 4]
        return v if dt == F32 else v.bitcast(dt)

    o_const = alloc(8 * 128 * 4)
    cst = view(o_const, 8 * 128, F32).rearrange("p (a b) -> p a b", a=8)
    ident, ones = cst[:, 0, :], cst[:, 6, :]
    o_identb = alloc(128 * 2)
    identb = view(o_identb, 128, BF16)
    o_ss = alloc(NT * 16 * 4)
    ssparts = view(o_ss, NT * 16, F32).rearrange("p (a b) -> p a b", a=NT)
    o_small = alloc(64 * 4)
    small = view(o_small, 64, F32)
    B_const, B_ss, B_small, B_identb = Buf("const"), Buf("ss"), Buf("small"), Buf("identb")
    PB = [Buf(f"ps{i}") for i in range(8)]
    o_hT = alloc(KC * T * 2)
    hT = view(o_hT, KC * T, BF16).rearrange("p (a b) -> p a b", a=KC)
    B_hT = Buf("hT")
    NSLOT = 2
    o_ws = [alloc(8192 * 2) for _ in range(NSLOT)]
    wslot = [view(o, 8192, BF16) for o in o_ws]
    B_ws = [Buf(f"ws{i}") for i in range(NSLOT)]
    ws_sem = [p.dsem(f"ws{i}") for i in range(NSLOT)]
    ws_ctr = [0]
    o_big = cursor[0]
    big_cursor = [o_big]

    def balloc(nbytes):
        off = big_cursor[0]
        big_cursor[0] += (nbytes + 31) // 32 * 32
        assert big_cursor[0] <= ARENA_F32 * 4, f"big arena overflow {big_cursor[0]} {ARENA_F32*4}"
        return off

    o_big_holder = [o_big]

    def breset():
        big_cursor[0] = o_big_holder[0]
        del xslots[:]

    misc_sem = p.dsem("misc")

    def dma(eng, out_ap, in_ap, dsem, reads=(), writes=()):
        return p.emit(eng, lambda e: e.dma_start(out=out_ap, in_=in_ap), reads, writes, dsem=dsem)

    o_epsv = alloc(8 * 4)
    epsv = view(o_epsv, 8, F32)
    B_eps = Buf("eps")
    EPSV = {1e-6: 0, 1e-5: 1, 1.0: 2, 0.0: 3}
    for val_, col_ in EPSV.items():
        p.emit("dve", lambda e, val_=val_, col_=col_: e.memset(epsv[:, col_:col_ + 1], float(val_)), [], [B_eps])

    def epsc(v):
        c = EPSV[v]
        return epsv[:, c:c + 1]

    dma("sp", cst, consts, misc_sem, writes=[B_const])
    p.emit("dve", lambda e: e.tensor_copy(out=identb, in_=ident), [B_const], [B_identb])

    NXS = 4
    xs_sem = [p.dsem(f"xs{i}") for i in range(NXS)]
    xslots = []

    def set_xslots(n):
        del xslots[:]
        for i in range(n):
            xslots.append((view(balloc(8192 * 2), 8192, BF16), Buf(f"xs{i}"), xs_sem[i]))

    def wload(src_ap, shape_str=None, **kw):
        nsl = NSLOT + len(xslots)
        i = ws_ctr[0] % nsl
        ws_ctr[0] += 1
        n = 1
        for s_ in src_ap.shape[1:]:
            n *= s_
        assert n <= 8192
        if i >= NSLOT:
            xv, xb, xsm = xslots[i - NSLOT]
            dst = xv[:, 0:n]
            if len(src_ap.shape) == 3:
                dst = dst.rearrange("p (a b) -> p a b", a=src_ap.shape[1])
            p.emit("pool", lambda e: e.dma_start(out=dst, in_=src_ap, max_dma_last_dim=8192), [], [xb], dsem=xsm)
            return dst, xb
        dst = wslot[i][:, 0:n]
        if len(src_ap.shape) == 3:
            dst = dst.rearrange("p (a b) -> p a b", a=src_ap.shape[1])
        p.emit("pool", lambda e: e.dma_start(out=dst, in_=src_ap, max_dma_last_dim=8192), [], [B_ws[i]],
               dsem=ws_sem[i])
        return dst, B_ws[i]

    def mm(out_ap, lhsT, rhs, start, stop, reads, wbuf):
        p.emit("pe", lambda e: e.matmul(out_ap, lhsT, rhs, start=start, stop=stop), reads, [wbuf],
               acc=not start)

    def rstd_from_ss(dst, src, n, eps, rb, wb, eng="dve"):
        p.emit("dve", lambda e: e.tensor_scalar(out=dst, in0=src, scalar1=1.0 / n, scalar2=float(eps), op0=ALU.mult,
                                                op1=ALU.add), rb, wb)
        p.emit("act", lambda e: e.activation(out=dst, in_=dst, func=AF.Ln), wb, wb)
        p.emit("act", lambda e: e.activation(out=dst, in_=dst, func=AF.Exp, scale=-0.5), wb, wb)

    def boundary(x_src, y_info, pre_row, tok0, ntok, want_h=True, write_x=True):
        breset()
        o_xt, o_yt = balloc(D * 4), balloc(D * 4)
        o_wpost, o_wpre, o_xn = balloc(D * 4), balloc(D * 4), balloc(D * 2)
        o_junk = balloc(D * 4)
        xt, yt = view(o_xt, D, F32), view(o_yt, D, F32)
        wpost, wpre, xn = view(o_wpost, D, F32), view(o_wpre, D, F32), view(o_xn, D, BF16)
        junk = view(o_junk, D, F32)
        Bx, By, Bwpo, Bwpr, Bxn, Bj = Buf("xt"), Buf("yt"), Buf("wpost"), Buf("wpre"), Buf("xn"), Buf("junk")
        Bst = Buf("stat")
        st = small
        sx, sy, sw, sxs = (boundary.sems[i] for i in range(4))
        if y_info is not None:
            coef, post_row, nparts = y_info
            dma("sp", wpost, norms[post_row:post_row + 1, :].partition_broadcast(128), sw, writes=[Bwpo])
        if want_h:
            dma("sp", wpre, norms[pre_row:pre_row + 1, :].partition_broadcast(128), sw, writes=[Bwpr])
        for tt in range(ntok // 128):
            r0 = tok0 + tt * 128
            gt = r0 // 128
            dma("sp", xt, x_src[r0:r0 + 128, :], sx, writes=[Bx])
            if y_info is not None:
                dma("sp", yt, Y[r0:r0 + 128, :], sy, writes=[By])
                p.emit("dve", lambda e, gt=gt: e.reduce_sum(out=st[:, 2:3], in_=ssparts[:, gt, 0:nparts], axis=AX.X),
                       [B_ss], [Bst])
                rstd_from_ss(st[:, 3:4], st[:, 2:3], D, EPS, [Bst], [Bst])
                if coef != 1.0:
                    p.emit("dve", lambda e: e.tensor_scalar(out=st[:, 3:4], in0=st[:, 3:4], scalar1=float(coef),
                                                            scalar2=None, op0=ALU.mult), [Bst], [Bst])
                p.emit("pool", lambda e: e.tensor_tensor(out=yt, in0=yt, in1=wpost, op=ALU.mult), [By, Bwpo], [By])
                p.emit("dve", lambda e: e.scalar_tensor_tensor(out=xt, in0=yt, scalar=st[:, 3:4], in1=xt,
                                                               op0=ALU.mult, op1=ALU.add), [By, Bx, Bst], [Bx])
            if write_x and (y_info is not None or x_src is not out):
                dma("sp", out[r0:r0 + 128, :], xt, sxs, reads=[Bx])
            if not want_h:
                continue
            p.emit("act", lambda e: e.activation(out=junk, in_=xt, func=AF.Square), [Bx], [Bj])
            p.emit("dve", lambda e: e.reduce_sum(out=st[:, 0:1], in_=junk, axis=AX.X), [Bj], [Bst])
            rstd_from_ss(st[:, 1:2], st[:, 0:1], D, EPS, [Bst], [Bst])
            p.emit("dve", lambda e: e.scalar_tensor_tensor(out=xn, in0=xt, scalar=st[:, 1:2], in1=wpre,
                                                           op0=ALU.mult, op1=ALU.mult), [Bx, Bst, Bwpr], [Bxn])
            for g in range(0, KC, 8):
                ng = min(8, KC - g)
                bank = (g // 8) % 2
                pt = psum[:, bank, :].bitcast(BF16)[:, 0:ng * 128].rearrange("p (a b) -> p a b", a=ng)
                for j in range(ng):
                    p.emit("pe", lambda e, j=j, g=g, pt=pt: e.transpose(pt[:, j, :], xn[:, (g + j) * 128:(g + j + 1) * 128],
                                                                         identb),
                           [Bxn, B_identb], [PB[bank]], acc=(j > 0))
                eng = "act" if (g // 8) % 2 == 0 else "dve"
                dst = hT[:, g:g + ng, tt * 128:(tt + 1) * 128]
                if eng == "act":
                    p.emit("act", lambda e, dst=dst, pt=pt: e.activation(out=dst, in_=pt, func=AF.Copy),
                           [PB[bank]], [B_hT])
                else:
                    p.emit("dve", lambda e, dst=dst, pt=pt: e.tensor_copy(out=dst, in_=pt), [PB[bank]], [B_hT])

    boundary.sems = [p.dsem(n) for n in ("bx", "by", "bw", "bxs")]

    ystage_sem = [p.dsem("ys0"), p.dsem("ys1")]

    def evac_y(ps_ap, bank, r0, c0, ncols, col_idx, stg):
        i = evac_y.ctr % 2
        evac_y.ctr += 1
        ys, Bys, jk, Bjk = stg[i]
        p.emit("dve", lambda e: e.tensor_copy(out=ys[:, 0:ncols], in_=ps_ap), [PB[bank]], [Bys])
        p.emit("act", lambda e: e.activation(out=jk[:, 0:ncols], in_=ys[:, 0:ncols], func=AF.Square), [Bys], [Bjk])
        p.emit("dve", lambda e: e.reduce_sum(out=ssparts[:, r0 // 128, col_idx:col_idx + 1], in_=jk[:, 0:ncols],
                                             axis=AX.X), [Bjk], [B_ss])
        dma("sp", Y[r0:r0 + 128, c0:c0 + ncols], ys[:, 0:ncols], ystage_sem[i], reads=[Bys])

    evac_y.ctr = 0

    def make_ystage():
        stg = []
        for i in range(2):
            o1, o2 = balloc(512 * 4), balloc(512 * 4)
            stg.append((view(o1, 512, F32), Buf(f"ys{i}"), view(o2, 512, F32), Buf(f"jk{i}")))
        return stg

    def ffn(idx, x_src, y_prev, pre_row):
        for ps_ in range(NP):
            tok0 = ps_ * T
            boundary(x_src, y_prev, pre_row, tok0, T)
            p.barrier()
            breset()
            o_act = balloc(FC * T * 2)
            actT = view(o_act, FC * T, BF16).rearrange("p (a b) -> p a b", a=FC)
            B_act = Buf("actT")
            o_sg = [balloc(T * 4) for _ in range(2)]
            sg = [view(o, T, F32) for o in o_sg]
            B_sg = [Buf("sg0"), Buf("sg1")]
            stg = make_ystage()
            for j in range(FC):
                wv, wb = wload(wgu[idx][j])
                bg, bu = (2 * j) % 4, (2 * j) % 4 + 1
                for kc in range(KC):
                    mm(psum[:, bg, 0:T], wv[:, kc, 0:128], hT[:, kc, :], kc == 0, kc == KC - 1, [wb, B_hT], PB[bg])
                for kc in range(KC):
                    mm(psum[:, bu, 0:T], wv[:, kc, 128:256], hT[:, kc, :], kc == 0, kc == KC - 1, [wb, B_hT], PB[bu])
                si = j % 2
                p.emit("act", lambda e, si=si, bg=bg: e.activation(out=sg[si], in_=psum[:, bg, 0:T], func=AF.Silu),
                       [PB[bg]], [B_sg[si]])
                p.emit("dve", lambda e, si=si, bu=bu, j=j: e.tensor_tensor(out=actT[:, j, :], in0=sg[si],
                                                                          in1=psum[:, bu, 0:T], op=ALU.mult),
                       [B_sg[si], PB[bu]], [B_act])
            for fb in range(NFB):
                for kg in range(NKG):
                    nk = min(16, FC - kg * 16)
                    wv, wb = wload(wdn[idx][fb, kg][:, 0:nk, :])
                    for kk in range(nk):
                        kc = kg * 16 + kk
                        for tt in range(TT):
                            bank = 4 * (fb % 2) + tt
                            mm(psum[:, bank, :], actT[:, kc, tt * 128:(tt + 1) * 128], wv[:, kk, :], kc == 0,
                               kc == FC - 1, [wb, B_act], PB[bank])
                for tt in range(TT):
                    bank = 4 * (fb % 2) + tt
                    evac_y(psum[:, bank, :], bank, tok0 + tt * 128, fb * 512, 512, fb, stg)
            p.barrier()


    def proj_fm(wblk_ap, nchunk, consume, ntok=T, hsrc=None, Bh=None, kcn=KC, banks=(2, 3)):
        hsrc = hT if hsrc is None else hsrc
        Bh = B_hT if Bh is None else Bh
        wv, wb = wload(wblk_ap)
        for j in range(nchunk):
            bank = banks[proj_fm.ctr % len(banks)]
            proj_fm.ctr += 1
            for kc in range(kcn):
                mm(psum[:, bank, 0:ntok], wv[:, kc, j * 128:(j + 1) * 128], hsrc[:, kc, 0:ntok], kc == 0, kc == kcn - 1,
                   [wb, Bh], PB[bank])
            consume(j, psum[:, bank, 0:ntok], bank)

    proj_fm.ctr = 0

    def proj_tm(wblk_ap, ncols, consume, ntok=T, hsrc=None, Bh=None, kcn=KC, banks=(4, 5, 6, 7), wv_wb=None):
        hsrc = hT if hsrc is None else hsrc
        Bh = B_hT if Bh is None else Bh
        wv, wb = wload(wblk_ap) if wv_wb is None else wv_wb
        for tt in range(ntok // 128):
            bank = banks[proj_tm.ctr % len(banks)]
            proj_tm.ctr += 1
            for kc in range(kcn):
                mm(psum[:, bank, 0:ncols], hsrc[:, kc, tt * 128:(tt + 1) * 128], wv[:, kc, 0:ncols], kc == 0,
                   kc == kcn - 1, [wb, Bh], PB[bank])
            consume(tt, psum[:, bank, 0:ncols], bank)

    proj_tm.ctr = 0

    class Ring:
        def __init__(self, name, n, nelem, dt):
            self.items = []
            for i in range(n):
                o = balloc(nelem * mybir.dt.size(dt))
                self.items.append((view(o, nelem, dt), Buf(f"{name}{i}"), Ring.sems(name, i)))
            self.i = 0

        def next(self):
            it = self.items[self.i % len(self.items)]
            self.i += 1
            return it

        _sems = {}

        @staticmethod
        def sems(name, i):
            key = (name, i)
            if key not in Ring._sems:
                Ring._sems[key] = p.dsem(f"{name}{i}")
            return Ring._sems[key]

    Ring._sems = {}

    def copy_evac(i, dst, src, reads, writes):
        if i % 2 == 0:
            p.emit("act", lambda e: e.activation(out=dst, in_=src, func=AF.Copy), reads, writes)
        else:
            p.emit("dve", lambda e: e.tensor_copy(out=dst, in_=src), reads, writes)

    def mem_attn(x_src, y_prev, pre_row, memn_row):
        MH = 4
        boundary(mem_in, None, memn_row, 0, MEM, write_x=False)
        p.barrier()
        breset()
        o_mk, o_mv = balloc(MH * MEM * 2), balloc((MEM // 128) * MH * 128 * 2)
        memK = view(o_mk, MH * MEM, BF16).rearrange("p (a b) -> p a b", a=MH)
        memV = view(o_mv, (MEM // 128) * MH * 128, BF16).rearrange("p (a b c) -> p a b c", a=MEM // 128, b=MH)
        B_mk, B_mv = Buf("memK"), Buf("memV")
        o_ones = balloc(128 * 2)
        onesb = view(o_ones, 128, BF16)
        B_onesb = Buf("onesb")
        p.emit("dve", lambda e: e.tensor_copy(out=onesb, in_=ones), [B_const], [B_onesb])
        for blk in range(2):
            def consK(j, ps, bank, blk=blk):
                h = blk * 2 + j
                copy_evac(h, memK[:, h, :], ps, [PB[bank]], [B_mk])
            proj_fm(wmkv[blk], 2, consK, ntok=MEM)
        for blk in range(2):
            def consV(tt, ps, bank, blk=blk):
                copy_evac(tt, memV[:, tt, 2 * blk:2 * blk + 2, :], ps.rearrange("p (a b) -> p a b", a=2),
                          [PB[bank]], [B_mv])
            proj_tm(wmkv[2 + blk], 256, consV, ntok=MEM)
        p.barrier()
        keep = big_cursor[0]
        scale = 128 ** -0.5
        for ps_ in range(NP):
            tok0 = ps_ * T
            big_cursor[0] = keep
            boundary_keep(x_src, y_prev, pre_row, tok0, T, keep)
            p.barrier()
            big_cursor[0] = keep
            o_q, o_oc = balloc(MH * T * 2), balloc(MH * T * 2)
            qT = view(o_q, MH * T, BF16).rearrange("p (a b) -> p a b", a=MH)
            ocT = view(o_oc, MH * T, BF16).rearrange("p (a b) -> p a b", a=MH)
            B_q, B_oc = Buf("mqT"), Buf("ocT")
            o_E = [balloc(T * 2) for _ in range(2)]
            E = [view(o, T, BF16) for o in o_E]
            B_E = [Buf("E0"), Buf("E1")]
            o_r = balloc(T * 4)
            rr = view(o_r, T, F32)
            B_r = Buf("rr")
            stg = make_ystage()
            for blk in range(2):
                def consQ(j, ps, bank, blk=blk):
                    h = blk * 2 + j
                    copy_evac(h, qT[:, h, :], ps, [PB[bank]], [B_q])
                proj_fm(wmq[blk], 2, consQ)
            for h in range(MH):
                for mt in range(MEM // 128):
                    sb = mt % 2
                    mm(psum[:, sb, 0:T], memK[:, h, mt * 128:(mt + 1) * 128], qT[:, h, :], True, True, [B_mk, B_q], PB[sb])
                    p.emit("act", lambda e, sb=sb: e.activation(out=E[sb], in_=psum[:, sb, 0:T], func=AF.Exp, scale=scale),
                           [PB[sb]], [B_E[sb]])
                    mm(psum[:, 2, 0:T], memV[:, mt, h, :], E[sb], mt == 0, mt == MEM // 128 - 1, [B_mv, B_E[sb]], PB[2])
                    mm(psum[:, 3, 0:T], onesb, E[sb], mt == 0, mt == MEM // 128 - 1, [B_onesb, B_E[sb]], PB[3])
                p.emit("dve", lambda e: e.reciprocal(out=rr, in_=psum[:, 3, 0:T]), [PB[3]], [B_r])
                p.emit("dve", lambda e, h=h: e.tensor_tensor(out=ocT[:, h, :], in0=psum[:, 2, 0:T], in1=rr, op=ALU.mult),
                       [PB[2], B_r], [B_oc])
            for blk in range(NB256):
                def consO(tt, ps, bank, blk=blk, tok0=tok0):
                    evac_y(ps, bank, tok0 + tt * 128, blk * 256, 256, blk, stg)
                proj_tm(wmo[blk], 256, consO, hsrc=ocT, Bh=B_oc, kcn=4)
            p.barrier()

    def boundary_keep(x_src, y_prev, pre_row, tok0, ntok, keep):
        saved = o_big_holder[0]
        o_big_holder[0] = keep
        try:
            boundary(x_src, y_prev, pre_row, tok0, ntok)
        finally:
            o_big_holder[0] = saved


    def mixer_inproj(x_src, y_prev, pre_row):
        import math
        NQ = HW // 256
        breset()
        o_cos, o_sin = balloc(S * 4), balloc(S * 4)
        COS, SIN = view(o_cos, S, F32), view(o_sin, S, F32)
        B_cs = Buf("cossin")
        o_wg = balloc(KC * 4 * NH * 2)
        wg = view(o_wg, KC * 4 * NH, BF16).rearrange("p (a b) -> p a b", a=KC)
        B_wg = Buf("wg")
        o_invf = balloc(4)
        invf_t = view(o_invf, 1, F32)
        keep = big_cursor[0]
        o_pi, o_pf = balloc(S * 4), balloc(S * 4)
        posi, posf = view(o_pi, S, I32), view(o_pf, S, F32)
        B_pi, B_pf = Buf("posi"), Buf("posf")
        dma("sp", posi, pos_in.partition_broadcast(128), misc_sem, writes=[B_pi])
        dma("sp", invf_t, invf, misc_sem, writes=[B_cs])
        p.emit("pool", lambda e: e.dma_start(out=wg, in_=wgate), [], [B_wg], dsem=misc_sem)
        p.emit("dve", lambda e: e.tensor_copy(out=posf, in_=posi), [B_pi], [B_pf])
        p.emit("dve", lambda e: e.tensor_scalar(out=posf, in0=posf, scalar1=invf_t[:, 0:1], scalar2=None, op0=ALU.mult),
               [B_pf, B_cs], [B_pf])
        TWO_PI = 2.0 * math.pi
        o_kf = balloc(S * 4)
        kf = view(o_kf, S, F32)
        B_kf = Buf("kf")
        TS = lambda **kw: (lambda e: e.tensor_scalar(**kw))
        for tab, shift in ((SIN, 0.0), (COS, 0.25)):
            p.emit("dve", TS(out=tab, in0=posf, scalar1=1.0 / TWO_PI, scalar2=shift, op0=ALU.mult, op1=ALU.add), [B_pf], [B_cs])
            p.emit("dve", lambda e, tab=tab: e.tensor_copy(out=posi, in_=tab), [B_cs], [B_pi])
            p.emit("dve", lambda e: e.tensor_copy(out=kf, in_=posi), [B_pi], [B_kf])
            p.emit("dve", lambda e, tab=tab: e.tensor_tensor(out=tab, in0=tab, in1=kf, op=ALU.subtract), [B_cs, B_kf], [B_cs])
            p.emit("dve", TS(out=kf, in0=tab, scalar1=0.5, scalar2=None, op0=ALU.is_gt), [B_cs], [B_kf])
            p.emit("dve", lambda e, tab=tab: e.tensor_tensor(out=tab, in0=tab, in1=kf, op=ALU.subtract), [B_cs, B_kf], [B_cs])
            p.emit("dve", TS(out=kf, in0=tab, scalar1=-0.5, scalar2=None, op0=ALU.is_lt), [B_cs], [B_kf])
            p.emit("dve", lambda e, tab=tab: e.tensor_tensor(out=tab, in0=tab, in1=kf, op=ALU.add), [B_cs, B_kf], [B_cs])
            p.emit("dve", TS(out=tab, in0=tab, scalar1=0.49999, scalar2=-0.49999, op0=ALU.min, op1=ALU.max), [B_cs], [B_cs])
            p.emit("act", lambda e, tab=tab: e.activation(out=tab, in_=tab, func=AF.Sin, scale=TWO_PI), [B_cs], [B_cs])
        p.barrier()
        RT = cst[:, 5, :]
        for ps_ in range(NP):
            tok0 = ps_ * T
            boundary_keep(x_src, y_prev, pre_row, tok0, T, keep)
            p.barrier()
            big_cursor[0] = keep
            rq = Ring("rq", 2, T, F32)
            rt1 = Ring("rt1", 2, T, F32)
            rt2 = Ring("rt2", 2, T, F32)
            rqo = Ring("rqo", 2, T, BF16)
            rdn = Ring("rdn", 2, T, F32)
            rv = Ring("rv", 2, 256, BF16)
            rz = Ring("rz", 2, 256, F32)
            rg = Ring("rg", 2, 4 * NH, F32)
            del xslots[:]
            set_xslots(3)
            for blk in range(2 * NQ):
                dst_t = QT if blk < NQ else KT

                def consQK(j, ps, bank, blk=blk, dst_t=dst_t, tok0=tok0):
                    c = (blk % NQ) * 2 + j
                    qs, Bq, _ = rq.next()
                    t1, Bt1, _ = rt1.next()
                    t2, Bt2, _ = rt2.next()
                    qo, Bqo, sqo = rqo.next()
                    p.emit("act", lambda e: e.activation(out=qs, in_=ps, func=AF.Copy), [PB[bank]], [Bq])
                    mm(psum[:, 0, 0:T], RT, qs, True, True, [B_const, Bq], PB[0])
                    p.emit("dve", lambda e: e.tensor_tensor(out=t1, in0=qs, in1=COS[:, tok0:tok0 + T], op=ALU.mult),
                           [Bq, B_cs], [Bt1])
                    p.emit("dve", lambda e: e.tensor_tensor(out=t2, in0=psum[:, 0, 0:T], in1=SIN[:, tok0:tok0 + T],
                                                            op=ALU.mult), [PB[0], B_cs], [Bt2])
                    p.emit("dve", lambda e: e.tensor_tensor(out=qo, in0=t1, in1=t2, op=ALU.add), [Bt1, Bt2], [Bqo])
                    dma("sp", dst_t[c, :, tok0:tok0 + T], qo, sqo, reads=[Bqo])
                proj_fm(win[blk], 2, consQK)
            for blk in range(3 * NQ):
                def consDN(j, ps, bank, blk=blk, tok0=tok0):
                    c = blk * 2 + j
                    d, Bd, sd = rdn.next()
                    copy_evac(c, d, ps, [PB[bank]], [Bd])
                    dma("sp", DNT[c, :, tok0:tok0 + T], d, sd, reads=[Bd])
                proj_fm(win[2 * NQ + blk], 2, consDN)
            for blk in range(NQ):
                def consV(tt, ps, bank, blk=blk, tok0=tok0):
                    v, Bv, sv = rv.next()
                    copy_evac(tt, v, ps, [PB[bank]], [Bv])
                    dma("sp", VV[tok0 + tt * 128:tok0 + (tt + 1) * 128, blk * 256:(blk + 1) * 256], v, sv, reads=[Bv])
                proj_tm(win[5 * NQ + blk], 256, consV)
            for blk in range(NQ):
                def consZ(tt, ps, bank, blk=blk, tok0=tok0):
                    z, Bz, sz = rz.next()
                    copy_evac(tt, z, ps, [PB[bank]], [Bz])
                    dma("sp", ZZ[tok0 + tt * 128:tok0 + (tt + 1) * 128, blk * 256:(blk + 1) * 256], z, sz, reads=[Bz])
                proj_tm(win[6 * NQ + blk], 256, consZ)

            def consG(tt, ps, bank, tok0=tok0):
                g, Bg, sg_ = rg.next()
                copy_evac(tt, g, ps, [PB[bank]], [Bg])
                dma("sp", GG[tok0 + tt * 128:tok0 + (tt + 1) * 128, :], g, sg_, reads=[Bg])
            proj_tm(None, 4 * NH, consG, wv_wb=(wg, B_wg))
            p.barrier()

    def diff_attn():
        breset()
        lam0 = cfg.lambda_init
        o_lp = balloc(256 * 4)
        lp = view(o_lp, 256, F32)
        o_lm = balloc(16 * 4)
        lm = view(o_lm, 16, F32)
        B_lp, B_lm = Buf("lp"), Buf("lm")
        o_sw = balloc(8)
        sw = view(o_sw, 2, F32)
        B_sw = Buf("sw")
        o_ones = balloc(128 * 2)
        onesb = view(o_ones, 128, BF16)
        B_onesb = Buf("onesb")
        p.emit("dve", lambda e: e.tensor_copy(out=onesb, in_=ones), [B_const], [B_onesb])
        dma("sp", lp, lam_in.partition_broadcast(128), misc_sem, writes=[B_lp])
        dma("sp", sw[:, 0:1], subln.rearrange("o d -> d o"), misc_sem, writes=[B_sw])
        p.emit("dve", lambda e: e.tensor_scalar(out=sw[:, 1:2], in0=sw[:, 0:1], scalar1=float(1.0 - lam0), scalar2=None,
                                                op0=ALU.mult), [B_sw], [B_sw])
        p.emit("dve", lambda e: e.tensor_tensor(out=lp[:, 0:64], in0=lp[:, 0:64], in1=lp[:, 64:128], op=ALU.mult), [B_lp], [B_lp])
        p.emit("dve", lambda e: e.tensor_tensor(out=lp[:, 128:192], in0=lp[:, 128:192], in1=lp[:, 192:256], op=ALU.mult), [B_lp], [B_lp])
        p.emit("dve", lambda e: e.reduce_sum(out=lm[:, 0:1], in_=lp[:, 0:64], axis=AX.X), [B_lp], [B_lm])
        p.emit("dve", lambda e: e.reduce_sum(out=lm[:, 1:2], in_=lp[:, 128:192], axis=AX.X), [B_lp], [B_lm])
        p.emit("act", lambda e: e.activation(out=lm[:, 2:4], in_=lm[:, 0:2], func=AF.Exp), [B_lm], [B_lm])
        p.emit("dve", lambda e: e.tensor_tensor(out=lm[:, 4:5], in0=lm[:, 3:4], in1=lm[:, 2:3], op=ALU.subtract), [B_lm], [B_lm])
        p.emit("dve", lambda e: e.tensor_scalar(out=lm[:, 4:5], in0=lm[:, 4:5], scalar1=-float(lam0), scalar2=None, op0=ALU.add),
               [B_lm], [B_lm])
        o_q, o_k, o_v = balloc(S * 2), balloc(S * 2), balloc(NT * 128 * 2)
        qt, kt = view(o_q, S, BF16), view(o_k, S, BF16)
        qz = [view(balloc(S * 2), S, BF16) for _ in range(2)]
        B_qz = [Buf("qz0"), Buf("qz1")]
        hm = view(balloc(8), 2, F32)
        B_hm = Buf("hm")
        p.emit("dve", lambda e: e.memset(hm, 0.0), [], [B_hm])
        p.emit("dve", lambda e: e.memset(hm[0:64, 0:1], 1.0), [B_hm], [B_hm])
        p.emit("dve", lambda e: e.memset(hm[64:128, 1:2], 1.0), [B_hm], [B_hm])
        vt = view(o_v, NT * 128, BF16).rearrange("p (a b) -> p a b", a=NT)
        B_q, B_k, B_v = Buf("dq"), Buf("dk"), Buf("dv")
        sq_, sk_, sv_ = p.dsem("dq"), p.dsem("dk"), p.dsem("dv")
        o_E = [balloc(T * 2) for _ in range(2)]
        E = [view(o, T, BF16) for o in o_E]
        B_E = [Buf("E0"), Buf("E1")]
        f32t = lambda: view(balloc(T * 4), T, F32)
        r1, o1, o2, sqv, rs = f32t(), f32t(), f32t(), f32t(), f32t()
        B_r1, B_o1, B_o2, B_sqv, B_rs = Buf("r1"), Buf("o1"), Buf("o2"), Buf("sqv"), Buf("rs")
        ron = Ring("ron", 2, T, BF16)
        for c in range(NH):
            dma("sp", qt, QT[c], sq_, writes=[B_q])
            dma("sp", kt, KT[c], sk_, writes=[B_k])
            dma("sp", vt, VV[:, c * 128:(c + 1) * 128].rearrange("(a p) d -> p a d", p=128), sv_, writes=[B_v])
            for a in range(2):
                eng = "dve" if a == 0 else "pool"
                p.emit(eng, lambda e, a=a: e.tensor_scalar(out=qz[a], in0=qt, scalar1=hm[:, a:a + 1], scalar2=None, op0=ALU.mult),
                       [B_q, B_hm], [B_qz[a]])
            for qg in range(S // T):
                q0 = qg * T
                for a in range(2):
                    pa = slice(a * 64, (a + 1) * 64)
                    bo, bs = 2 + 2 * a, 3 + 2 * a
                    def score(kb, a=a, q0=q0):
                        sb = kb % 2
                        mm(psum[:, sb, 0:T], kt[:, kb * 128:(kb + 1) * 128], qz[a][:, q0:q0 + T], True, True, [B_k, B_qz[a]], PB[sb])
                        p.emit("act", lambda e, sb=sb: e.activation(out=E[sb], in_=psum[:, sb, 0:T], func=AF.Exp, scale=0.125),
                               [PB[sb]], [B_E[sb]])
                    score(0)
                    for kb in range(NT):
                        sb = kb % 2
                        if kb + 1 < NT:
                            score(kb + 1)
                        mm(psum[:, bo, 0:T], vt[:, kb, :], E[sb], kb == 0, kb == NT - 1, [B_v, B_E[sb]], PB[bo])
                        mm(psum[:, bs, 0:T], onesb, E[sb], kb == 0, kb == NT - 1, [B_onesb, B_E[sb]], PB[bs])
                p.emit("dve", lambda e: e.reciprocal(out=r1, in_=psum[:, 3, 0:T]), [PB[3]], [B_r1])
                p.emit("dve", lambda e: e.tensor_tensor(out=o1, in0=psum[:, 2, 0:T], in1=r1, op=ALU.mult), [PB[2], B_r1], [B_o1])
                p.emit("dve", lambda e: e.reciprocal(out=r1, in_=psum[:, 5, 0:T]), [PB[5]], [B_r1])
                p.emit("dve", lambda e: e.tensor_tensor(out=o2, in0=psum[:, 4, 0:T], in1=r1, op=ALU.mult), [PB[4], B_r1], [B_o2])
                p.emit("dve", lambda e: e.scalar_tensor_tensor(out=o1, in0=o2, scalar=lm[:, 4:5], in1=o1, op0=ALU.mult,
                                                               op1=ALU.add), [B_o2, B_o1, B_lm], [B_o1])
                p.emit("act", lambda e: e.activation(out=sqv, in_=o1, func=AF.Square), [B_o1], [B_sqv])
                mm(psum[:, 6, 0:T], ones, sqv, True, True, [B_const, B_sqv], PB[6])
                p.emit("dve", lambda e: e.tensor_scalar(out=rs, in0=psum[:, 6, 0:T], scalar1=1.0 / 128, scalar2=1e-5,
                                                        op0=ALU.mult, op1=ALU.add), [PB[6]], [B_rs])
                p.emit("act", lambda e: e.activation(out=rs, in_=rs, func=AF.Ln), [B_rs], [B_rs])
                p.emit("act", lambda e: e.activation(out=rs, in_=rs, func=AF.Exp, scale=-0.5), [B_rs], [B_rs])
                on, Bon, son = ron.next()
                p.emit("dve", lambda e, on=on: e.scalar_tensor_tensor(out=on, in0=o1, scalar=sw[:, 1:2], in1=rs, op0=ALU.mult,
                                                                      op1=ALU.mult), [B_o1, B_sw, B_rs], [Bon])
                dma("sp", OT[c, :, q0:q0 + T], on, son, reads=[Bon])
        p.barrier()

    op_sem = p.dsem("opl")

    def out_proj():
        for ps_ in range(NP):
            tok0 = ps_ * T
            breset()
            stg = make_ystage()
            set_xslots(4)
            for c_ in range(KC):
                dma("sp", hT[:, c_, :], OT[c_, :, tok0:tok0 + T], op_sem, writes=[B_hT])
            for blk in range(NB256):
                def consO(tt, ps, bank, blk=blk, tok0=tok0):
                    evac_y(ps, bank, tok0 + tt * 128, blk * 256, 256, blk, stg)
                proj_tm(wout[blk], 256, consO)
            p.barrier()


    def gdn():
        o_big_holder[0] = o_hT
        breset()
        G4 = 4 * NH
        f32v = lambda n: view(balloc(n * 4), n, F32)
        gts = f32v(NT * G4).rearrange("p (a b) -> p a b", a=NT)
        dsm = f32v(G4)
        gd = f32v(NT * 2 * NH).rearrange("p (a d h) -> p a d h", a=NT, d=2)
        bd = f32v(NT * 2 * NH).rearrange("p (a d h) -> p a d h", a=NT, d=2)
        NN = NT * NH
        egc_f, negc_f, etail_f, egtot_f = f32v(2 * NN), f32v(2 * NN), f32v(2 * NN), f32v(2 * NN)
        v4 = lambda f: f.rearrange("p (d a h) -> p d a h", d=2, a=NT)
        egc, negc, etail, egtot = v4(egc_f), v4(negc_f), v4(etail_f), v4(egtot_f)
        tmpg_f, tmpg2_f = f32v(NN), f32v(NN)
        tmpg = tmpg_f.rearrange("p (a h) -> p a h", a=NT)
        tmpg2 = tmpg2_f.rearrange("p (a h) -> p a h", a=NT)
        cw = f32v(3 * NH * 5).rearrange("p (c t) -> p c t", c=3 * NH)
        nwb = f32v(128)
        Bg = Buf("gates")
        dma("sp", gts, GG.rearrange("(a p) g -> p a g", p=128), misc_sem, writes=[Bg])
        dma("sp", dsm, dn_small.partition_broadcast(128), misc_sem, writes=[Bg])
        dma("sp", cw, convw, misc_sem, writes=[Bg])
        dma("sp", nwb, dn_normw.partition_broadcast(128), misc_sem, writes=[Bg])
        p.barrier()
        E1 = lambda eng, fn: p.emit(eng, fn, [Bg], [Bg])
        E1("act", lambda e: e.activation(out=dsm[:, 0:2 * NH], in_=dsm[:, 0:2 * NH], func=AF.Exp))
        E1("dve", lambda e: e.tensor_scalar(out=dsm[:, 0:2 * NH], in0=dsm[:, 0:2 * NH], scalar1=-1.0, scalar2=None, op0=ALU.mult))
        for d in range(2):
            a_sl = slice(d * 2 * NH, d * 2 * NH + NH)
            b_sl = slice(d * 2 * NH + NH, (d + 1) * 2 * NH)
            for t in range(NT):
                E1("dve", lambda e, t=t, a_sl=a_sl, d=d: e.tensor_tensor(out=tmpg[:, t, :], in0=gts[:, t, a_sl],
                                                                      in1=dsm[:, 2 * NH + d * NH:2 * NH + (d + 1) * NH], op=ALU.add))
            E1("dve", lambda e: e.tensor_scalar(out=tmpg2, in0=tmpg, scalar1=-1.0, scalar2=None, op0=ALU.mult))
            E1("dve", lambda e: e.tensor_tensor(out=tmpg2, in0=tmpg2, in1=tmpg, op=ALU.max))
            E1("act", lambda e: e.activation(out=tmpg2, in_=tmpg2, func=AF.Exp, scale=-1.0))
            E1("dve", lambda e: e.tensor_scalar(out=tmpg2, in0=tmpg2, scalar1=1.0, scalar2=None, op0=ALU.add))
            E1("act", lambda e: e.activation(out=tmpg2, in_=tmpg2, func=AF.Ln))
            E1("dve", lambda e: e.scalar_tensor_tensor(out=tmpg, in0=tmpg, scalar=0.0, in1=tmpg2, op0=ALU.max, op1=ALU.add))
            for t in range(NT):
                E1("dve", lambda e, t=t, d=d: e.tensor_tensor(out=gd[:, t, d, :], in0=tmpg[:, t, :],
                                                             in1=dsm[:, d * NH:(d + 1) * NH], op=ALU.mult))
            E1("act", lambda e, d=d, b_sl=b_sl: e.activation(out=bd[:, :, d, :], in_=gts[:, :, b_sl], func=AF.Sigmoid))
            mincl = cst[:, 1 if d == 0 else 3, :]
            dsl = slice(d * NN, (d + 1) * NN)
            E1("dve", lambda e, d=d: e.tensor_copy(out=tmpg2, in_=gd[:, :, d, :]))
            mm(psum[:, 0, 0:NN], mincl, tmpg2_f, True, True, [Bg, B_const], PB[0])
            mm(psum[:, 1, 0:NN], ones, tmpg2_f, True, True, [Bg, B_const], PB[1])
            CL = lambda o_, i_: (lambda e: e.tensor_scalar(out=o_, in0=i_, scalar1=-80.0, scalar2=None, op0=ALU.max))
            p.emit("dve", CL(egc_f[:, dsl], psum[:, 0, 0:NN]), [PB[0]], [Bg])
            p.emit("dve", CL(egtot_f[:, dsl], psum[:, 1, 0:NN]), [PB[1]], [Bg])
            p.emit("dve", lambda e: e.tensor_copy(out=tmpg_f, in_=psum[:, 0, 0:NN]), [PB[0]], [Bg])
            p.emit("dve", lambda e, dsl=dsl: e.tensor_tensor(out=etail_f[:, dsl], in0=psum[:, 1, 0:NN], in1=tmpg_f, op=ALU.subtract),
                   [PB[1], Bg], [Bg])
            E1("dve", CL(etail_f[:, dsl], etail_f[:, dsl]))
            E1("act", lambda e, dsl=dsl: e.activation(out=egc_f[:, dsl], in_=egc_f[:, dsl], func=AF.Exp))
            E1("act", lambda e, dsl=dsl: e.activation(out=egtot_f[:, dsl], in_=egtot_f[:, dsl], func=AF.Exp))
            E1("act", lambda e, dsl=dsl: e.activation(out=etail_f[:, dsl], in_=etail_f[:, dsl], func=AF.Exp))
            E1("dve", lambda e, dsl=dsl: e.tensor_scalar(out=negc_f[:, dsl], in0=egc_f[:, dsl], scalar1=-1.0, scalar2=None, op0=ALU.mult))
        p.barrier()
        HG = 2 if NH % 2 == 0 else 1
        xp = [f32v(S + 4)]
        B_xp = [Buf("xp0")]
        xp_sem = [p.dsem("xp0")]
        acc, sqb = f32v(S), f32v(S)
        B_acc, B_sqb = Buf("acc"), Buf("sqb")
        rsb = f32v(512)
        B_rsb = Buf("rsb")
        zt = f32v(NT * 128).rearrange("p (a b) -> p a b", a=NT)
        B_z = Buf("zt")
        z_sem = p.dsem("zt")
        st2 = f32v(2 * NT)
        B_st2 = Buf("st2")
        HC = []
        for hi in range(HG):
            c_ = {}
            c_["qkv"] = (f32v(S), f32v(S), f32v(S))
            c_["B_qkv"] = [Buf(f"qn{hi}"), Buf(f"kn{hi}"), Buf(f"vs{hi}")]
            c_["oacc"] = [f32v(NT * 128).rearrange("p (a b) -> p a b", a=NT) for _ in range(2)]
            c_["B_oacc"] = [Buf(f"of{hi}"), Buf(f"ob{hi}")]
            c_["Sst"] = [f32v(128), f32v(128)]
            c_["B_S"] = [Buf(f"S0{hi}"), Buf(f"S1{hi}")]
            W = []
            for d in range(2):
                w_ = {n: f32v(128) for n in ("Gm", "DT", "DTi", "DTs", "X", "XT", "QKD", "Ra", "Rb", "P", "PT", "P2", "P2T", "kt",
                                             "vt", "r", "vn", "t1")}
                w_["B"] = {n: Buf(n + str(d) + str(hi)) for n in w_}
                W.append(w_)
            c_["W"] = W
            HC.append(c_)
        rog = Ring("rog", 2, 128, BF16)
        rot = Ring("rot", 2, 128, BF16)
        for i_ in range(1):
            p.emit("pool", lambda e, i_=i_: e.memset(xp[i_][:, 0:2], 0.0), [], [B_xp[i_]])
            p.emit("pool", lambda e, i_=i_: e.memset(xp[i_][:, S + 2:S + 4], 0.0), [], [B_xp[i_]])
        pb = [0]

        def nb():
            b = 6 + pb[0] % 2
            pb[0] += 1
            return b

        def mmf(lhsT, rhs, reads):
            b = nb()
            o_ = psum[:, b, 0:128]
            mm(o_, lhsT, rhs, True, True, reads, PB[b])
            return o_, PB[b]

        def prep(h, c_):
            qn, kn, vs = c_["qkv"]
            B_qkv = c_["B_qkv"]
            Sst, B_S = c_["Sst"], c_["B_S"]
            for qi, (dst, Bd) in enumerate(zip((qn, kn, vs), B_qkv)):
                ch = qi * NH + h
                i_ = 0
                x_, Bx_ = xp[i_], B_xp[i_]
                dma("sp", x_[:, 2:S + 2], DNT[ch], xp_sem[i_], writes=[Bx_])
                eng = "dve" if qi != 1 else "pool"
                p.emit(eng, lambda e, x_=x_, ch=ch: e.tensor_scalar(out=acc, in0=x_[:, 0:S], scalar1=cw[:, ch, 0:1], scalar2=None,
                                                                   op0=ALU.mult), [Bx_, Bg], [B_acc])
                for j in range(1, 5):
                    p.emit("dve", lambda e, x_=x_, ch=ch, j=j: e.scalar_tensor_tensor(out=acc, in0=x_[:, j:j + S], scalar=cw[:, ch, j:j + 1],
                                                                                   in1=acc, op0=ALU.mult, op1=ALU.add), [Bx_, Bg, B_acc], [B_acc])
                if qi == 2:
                    p.emit("act", lambda e, dst=dst: e.activation(out=dst, in_=acc, func=AF.Silu), [B_acc], [Bd])
                    continue
                p.emit("act", lambda e: e.activation(out=acc, in_=acc, func=AF.Silu), [B_acc], [B_acc])
                p.emit("act", lambda e: e.activation(out=sqb, in_=acc, func=AF.Square), [B_acc], [B_sqb])
                for b0 in range(0, S, 512):
                    bk = nb()
                    mm(psum[:, bk, :], ones, sqb[:, b0:b0 + 512], True, True, [B_const, B_sqb], PB[bk])
                    p.emit("dve", lambda e, bk=bk: e.tensor_scalar(out=rsb, in0=psum[:, bk, :], scalar1=1e-6, scalar2=None, op0=ALU.add),
                           [PB[bk]], [B_rsb])
                    p.emit("act", lambda e: e.activation(out=rsb, in_=rsb, func=AF.Ln), [B_rsb], [B_rsb])
                    p.emit("act", lambda e: e.activation(out=rsb, in_=rsb, func=AF.Exp, scale=-0.5), [B_rsb], [B_rsb])
                    sc = float(128 ** -0.5) if qi == 0 else 1.0
                    p.emit("dve", lambda e, dst=dst, b0=b0, sc=sc: e.scalar_tensor_tensor(out=dst[:, b0:b0 + 512], in0=acc[:, b0:b0 + 512],
                                                                                       scalar=sc, in1=rsb, op0=ALU.mult, op1=ALU.mult),
                           [B_acc, B_rsb], [Bd])
            for d in range(2):
                p.emit("pool", lambda e, d=d: e.memset(Sst[d], 0.0), [], [B_S[d]])
        PS4 = [Buf(f"ps4_{i}") for i in range(32)]
        pb4 = [0]
        PS4_PER = 1
        NPS4 = 6 * PS4_PER

        def nb4():
            i = pb4[0] % NPS4
            pb4[0] += 1
            return psum[:, i // PS4_PER, (i % PS4_PER) * 128:(i % PS4_PER + 1) * 128], PS4[i]

        def mm1(lhsT, rhs, reads):
            o_, Bo = nb4()
            p.emit("pe", lambda e: e.matmul(o_, lhsT, rhs, start=True, stop=True), reads, [Bo])
            return o_, Bo

        def step(h, c_, n_, d):
            qn, kn, vs = c_["qkv"]
            Bq, Bk, Bv = c_["B_qkv"]
            Sst, B_S = c_["Sst"], c_["B_S"]
            oacc, B_oacc = c_["oacc"], c_["B_oacc"]
            w_ = c_["W"][d]
            B_ = w_["B"]
            t = n_ if d == 0 else NT - 1 - n_
            cs = slice(t * 128, (t + 1) * 128)
            MinclG = cst[:, 1 if d == 0 else 3, :]
            Mstr = cst[:, 2 if d == 0 else 4, :]
            maskI = cst[:, 1 if d == 0 else 3, :]
            maskS = cst[:, 4 if d == 0 else 2, :]
            gcol = gd[:, t, d, h:h + 1]
            bcol = bd[:, t, d, h:h + 1]
            E = p.emit
            TT_ = lambda o_, a_, b_, op: (lambda e: e.tensor_tensor(out=o_, in0=a_, in1=b_, op=op))
            TS_ = lambda o_, a_, s1, op: (lambda e: e.tensor_scalar(out=o_, in0=a_, scalar1=s1, scalar2=None, op0=op))
            STT_ = lambda o_, a_, sc, b_, op0, op1: (lambda e: e.scalar_tensor_tensor(out=o_, in0=a_, scalar=sc, in1=b_, op0=op0, op1=op1))
            ACT_ = lambda o_, a_, f: (lambda e: e.activation(out=o_, in_=a_, func=f))
            E("pool", TS_(w_["Gm"], MinclG, gcol, ALU.mult), [B_const, Bg], [B_["Gm"]])
            yield
            ps_g, Bg_ = mm1(Mstr, w_["Gm"], [B_const, B_["Gm"]])
            ps_kt, Bkt = mm1(kn[:, cs], ident, [Bk, B_const])
            ps_vt, Bvt = mm1(vs[:, cs], ident, [Bv, B_const])
            E("dve", TS_(w_["DT"], ps_g, -80.0, ALU.max), [Bg_], [B_["DT"]])
            E("act", ACT_(w_["DT"], w_["DT"], AF.Exp), [B_["DT"]], [B_["DT"]])
            E("dve", TS_(w_["kt"], ps_kt, etail[:, d, t, h:h + 1], ALU.mult), [Bkt, Bg], [B_["kt"]])
            E("act", ACT_(w_["vt"], ps_vt, AF.Copy), [Bvt], [B_["vt"]])
            yield
            ps_kk, Bkk = mm1(kn[:, cs], kn[:, cs], [Bk])
            ps_qk, Bqk = mm1(kn[:, cs], qn[:, cs], [Bk, Bq])
            E("pool", TT_(w_["DTs"], w_["DT"], maskS, ALU.mult), [B_["DT"], B_const], [B_["DTs"]])
            E("pool", TT_(w_["DTi"], w_["DT"], maskI, ALU.mult), [B_["DT"], B_const], [B_["DTi"]])
            E("dve", STT_(w_["X"], ps_kk, bcol, w_["DTs"], ALU.mult, ALU.mult), [Bkk, Bg, B_["DTs"]], [B_["X"]])
            E("dve", TT_(w_["QKD"], ps_qk, w_["DTi"], ALU.mult), [Bqk, B_["DTi"]], [B_["QKD"]])
            yield
            ps_xt, Bxt = mm1(w_["X"], ident, [B_["X"], B_const])
            E("act", ACT_(w_["XT"], ps_xt, AF.Copy), [Bxt], [B_["XT"]])
            E("dve", TT_(w_["Ra"], ident, w_["X"], ALU.subtract), [B_const, B_["X"]], [B_["Ra"]])
            yield
            P_, PT_, R_ = ("X", "XT", "Ra")
            pend = None
            nlv = 6
            for lvl in range(1, nlv + 1):
                last = lvl == nlv
                nP, nPT = ("P2", "P2T") if P_ in ("X", "P") else ("P", "PT")
                ps_a, Ba = mm1(w_[P_], w_[PT_], [B_[P_], B_[PT_]])
                if not last:
                    ps_b, Bb = mm1(w_[PT_], w_[P_], [B_[P_], B_[PT_]])
                if pend is not None:
                    ps_r, Br = mm1(w_[pend], w_[R_], [B_[pend], B_[R_]])
                E("act", ACT_(w_[nPT], ps_a, AF.Copy), [Ba], [B_[nPT]])
                if not last:
                    E("dve", (lambda o_, i_: (lambda e: e.tensor_copy(out=o_, in_=i_)))(w_[nP], ps_b), [Bb], [B_[nP]])
                if pend is not None:
                    nR = "Rb" if R_ == "Ra" else "Ra"
                    E("dve", TT_(w_[nR], ps_r, w_[R_], ALU.add), [Br, B_[R_]], [B_[nR]])
                    R_ = nR
                pend = nPT
                P_, PT_ = nP, nPT
                yield
            ps_r, Br = mm1(w_[pend], w_[R_], [B_[pend], B_[R_]])
            nR = "Rb" if R_ == "Ra" else "Ra"
            E("dve", TT_(w_[nR], ps_r, w_[R_], ALU.add), [Br, B_[R_]], [B_[nR]])
            R_ = nR
            yield
            ps_p, Bp_ = mm1(kn[:, cs], Sst[d], [Bk, B_S[d]])
            ps_o1, Bo1 = mm1(qn[:, cs], Sst[d], [Bq, B_S[d]])
            E("dve", STT_(w_["r"], ps_p, negc[:, d, t, h:h + 1], w_["vt"], ALU.mult, ALU.add), [Bp_, Bg, B_["vt"]], [B_["r"]])
            E("dve", TS_(w_["t1"], ps_o1, egc[:, d, t, h:h + 1], ALU.mult), [Bo1, Bg], [B_["t1"]])
            yield
            ps_vn, Bvn = mm1(w_[R_], w_["r"], [B_[R_], B_["r"]])
            E("dve", TS_(w_["vn"], ps_vn, bcol, ALU.mult), [Bvn, Bg], [B_["vn"]])
            yield
            ps_o2, Bo2 = mm1(w_["QKD"], w_["vn"], [B_["QKD"], B_["vn"]])
            ps_s, Bs_ = mm1(w_["kt"], w_["vn"], [B_["kt"], B_["vn"]])
            E("dve", TT_(oacc[d][:, t, :], ps_o2, w_["t1"], ALU.add), [Bo2, B_["t1"]], [B_oacc[d]])
            E("dve", STT_(Sst[d], Sst[d], egtot[:, d, t, h:h + 1], ps_s, ALU.mult, ALU.add), [Bs_, Bg, B_S[d]], [B_S[d]])
            yield

        def gate(h, c_):
            oacc, B_oacc = c_["oacc"], c_["B_oacc"]
            dma("sp", zt, ZZ[:, h * 128:(h + 1) * 128].rearrange("(a p) d -> p a d", p=128), z_sem, writes=[B_z])
            of_, ob_ = oacc
            p.emit("dve", lambda e: e.tensor_tensor(out=of_, in0=of_, in1=ob_, op=ALU.add), [B_oacc[0], B_oacc[1]], [B_oacc[0]])
            p.emit("act", lambda e: e.activation(out=ob_, in_=of_, func=AF.Square), [B_oacc[0]], [B_oacc[1]])
            p.emit("dve", lambda e: e.reduce_sum(out=st2[:, 0:NT], in_=ob_, axis=AX.X), [B_oacc[1]], [B_st2])
            rstd_from_ss(st2[:, 0:NT], st2[:, 0:NT], 128, 1e-6, [B_st2], [B_st2])
            p.emit("act", lambda e: e.activation(out=zt, in_=zt, func=AF.Silu), [B_z], [B_z])
            for t in range(NT):
                og, Bog, _ = rog.next()
                ot, Bot, sot = rot.next()
                p.emit("dve", lambda e, t=t: e.scalar_tensor_tensor(out=ob_[:, t, :], in0=of_[:, t, :], scalar=st2[:, t:t + 1], in1=nwb,
                                                                   op0=ALU.mult, op1=ALU.mult), [B_oacc[0], B_st2, Bg], [B_oacc[1]])
                p.emit("dve", lambda e, t=t, og=og: e.tensor_tensor(out=og, in0=ob_[:, t, :], in1=zt[:, t, :], op=ALU.mult),
                       [B_oacc[1], B_z], [Bog])
                bk = nb()
                pt = psum[:, bk, :].bitcast(BF16)[:, 0:128]
                p.emit("pe", lambda e, pt=pt, og=og: e.transpose(pt, og, identb), [Bog, B_identb], [PB[bk]])
                p.emit("act", lambda e, pt=pt, ot=ot: e.activation(out=ot, in_=pt, func=AF.Copy), [PB[bk]], [Bot])
                dma("sp", OT[NH + h, :, t * 128:(t + 1) * 128], ot, sot, reads=[Bot])

        for h0 in range(0, NH, HG):
            for hi in range(HG):
                prep(h0 + hi, HC[hi])
            for n_ in range(NT):
                gens = [step(h0 + hi, HC[hi], n_, d) for hi in range(HG) for d in range(2)]
                while gens:
                    alive = []
                    for g_ in gens:
                        try:
                            next(g_)
                            alive.append(g_)
                        except StopIteration:
                            pass
                    gens = alive
            for hi in range(HG):
                gate(h0 + hi, HC[hi])
        p.barrier()
        o_big_holder[0] = o_big

    dbg_sem = p.dsem("dbg")

    def dbg(name, v, bufs):
        t = nc.dram_tensor(name, list(v.shape), v.dtype, kind="ExternalOutput").ap()
        dma("sp", t, v, dbg_sem, reads=bufs)

    k.phase = dict(dbg=dbg, gdn=gdn, mixer_inproj=mixer_inproj, diff_attn=diff_attn, out_proj=out_proj, mem_attn=mem_attn, boundary=boundary, ffn=ffn, evac_y=evac_y, make_ystage=make_ystage, wload=wload, mm=mm,
                   dma=dma, view=view, balloc=balloc, breset=breset, rstd_from_ss=rstd_from_ss)
    k.syms = dict(locals())
    return k


def finish(k, final_events_wait=True):
    p, nc, es = k.p, k.nc, k.es
    p.barrier()
    with nc.Block() as block:
        p.finalize(block)
    es.close()
    return nc


def blk256(w, KC):
    K, N = w.shape
    return np.ascontiguousarray(w.reshape(KC, 128, N // 256, 256).transpose(2, 1, 0, 3))


def lay_wgu(w, cfg):
    K, N2 = w.shape
    FC, KC = cfg.FC, cfg.KC
    g = w[:, :cfg.DFF].reshape(KC, 128, FC, 128)
    u = w[:, cfg.DFF:].reshape(KC, 128, FC, 128)
    gu = np.concatenate([g, u], axis=3)
    return np.ascontiguousarray(gu.transpose(2, 1, 0, 3))


def lay_wdn(w, cfg):
    FC, NKG, D = cfg.FC, cfg.NKG, cfg.D
    wp = np.zeros((NKG * 16 * 128, D), np.float32)
    wp[:cfg.DFF] = w
    a = wp.reshape(NKG, 16, 128, D // 512, 512)
    return np.ascontiguousarray(a.transpose(3, 0, 2, 1, 4))


def make_consts():
    c = np.zeros((128, 8, 128), np.float32)
    c[:, 0, :] = np.eye(128)
    i = np.arange(128)
    c[:, 1, :] = (i[:, None] <= i[None, :])
    c[:, 2, :] = (i[:, None] > i[None, :])
    c[:, 3, :] = (i[:, None] >= i[None, :])
    c[:, 4, :] = (i[:, None] < i[None, :])
    RT = np.zeros((128, 128), np.float32)
    for p_ in range(128):
        d = p_ % 64
        if d < 8:
            RT[p_ + 8, p_] = -1.0
        elif d < 16:
            RT[p_ - 8, p_] = 1.0
    c[:, 5, :] = RT
    c[:, 6, :] = 1.0
    return c


def host_mixer_inputs(cfg, w_in, w_out, lam, subln, pos):
    HW, NH, KC = cfg.HW, cfg.NH, cfg.KC
    cols = np.concatenate([np.arange(0, 2 * HW), np.arange(3 * HW, 6 * HW), np.arange(2 * HW, 3 * HW),
                           np.arange(6 * HW, 7 * HW)])
    d = {}
    d["win"] = blk256(w_in[:, cols], KC)
    wg = w_in[:, 7 * HW:]
    d["wgate"] = np.ascontiguousarray(wg.reshape(KC, 128, 4 * NH).transpose(1, 0, 2))
    d["wout"] = blk256(w_out, KC)
    d["lam"] = np.ascontiguousarray(lam.reshape(1, 256))
    d["subln"] = np.ascontiguousarray(subln.reshape(1, 128))
    d["pos"] = np.ascontiguousarray(pos.reshape(1, -1).astype(np.int32))
    invf = np.zeros((128, 1), np.float32)
    base = (np.float32(cfg.rope_theta) ** (-np.arange(0, 16, 2, dtype=np.float32) / np.float32(16))).astype(np.float32)
    for p_ in range(128):
        dd = p_ % 64
        if dd < 16:
            invf[p_, 0] = base[dd % 8]
    d["invf"] = invf
    return d


def host_gdn_inputs(cfg, conv_w, a_log, dt_bias, normw):
    NH = cfg.NH
    d = {}
    d["convw"] = np.ascontiguousarray(conv_w.T.reshape(3 * NH, 128, 5).transpose(1, 0, 2))
    d["dn_small"] = np.ascontiguousarray(np.concatenate([a_log.reshape(-1), dt_bias.reshape(-1)]).reshape(1, 4 * NH))
    d["dn_normw"] = np.ascontiguousarray(normw.reshape(1, 128))
    return d


_CACHE = {}


def build_full(cfg):
    k = build(cfg)
    ph = k.phase
    out = k.syms["out"]
    NFB, NB256 = cfg.D // 512, cfg.D // 256
    ph["ffn"](0, k.dram["x"], None, 0)
    ph["mixer_inproj"](out, (0.5, 1, NFB), 2)
    ph["diff_attn"]()
    ph["gdn"]()
    ph["out_proj"]()
    ph["mem_attn"](out, (1.0, 3, NB256), 4, 5)
    ph["ffn"](1, out, (1.0, 6, NB256), 7)
    ph["boundary"](out, (0.5, 8, NFB), 0, 0, cfg.S, want_h=False)
    return k, finish(k)


def kernel(x, mem, positions, ffn1_norms, ffn1_w_gu, ffn1_w_down, mix_norms, mix_w_in, dn_conv_w, dn_a_log,
           dn_dt_bias, dn_norm_w, diff_lambda, diff_subln_w, mix_w_out, mem_norms, mem_w_q, mem_w_kv, mem_w_o,
           ffn2_norms, ffn2_w_gu, ffn2_w_down):
    f = lambda a: np.asarray(a, dtype=np.float32)
    x = f(x)
    B, S, D = x.shape
    cfg = Cfg(D=D, S=S, DFF=f(ffn1_w_down).shape[1], MEM=np.asarray(mem).shape[1], ncores=B)
    k, nc = build_full(cfg)
    shared = {}
    shared["wgu1"] = lay_wgu(f(ffn1_w_gu)[0], cfg)
    shared["wdn1"] = lay_wdn(f(ffn1_w_down)[0], cfg)
    shared["wgu2"] = lay_wgu(f(ffn2_w_gu)[0], cfg)
    shared["wdn2"] = lay_wdn(f(ffn2_w_down)[0], cfg)
    shared.update(host_gdn_inputs(cfg, f(dn_conv_w)[0], f(dn_a_log)[0], f(dn_dt_bias)[0], f(dn_norm_w)[0]))
    shared["wmq"] = blk256(f(mem_w_q)[0], cfg.KC)
    shared["wmkv"] = blk256(f(mem_w_kv)[0], cfg.KC)
    shared["wmo"] = blk256(f(mem_w_o)[0], 4)
    shared["norms"] = np.ascontiguousarray(np.concatenate([f(ffn1_norms)[0], f(mix_norms)[0], f(mem_norms)[0][[0, 1, 2]],
                                                           f(ffn2_norms)[0]], axis=0))
    shared["consts"] = make_consts()
    pos = np.asarray(positions)
    in_maps = []
    for b in range(B):
        m = dict(shared)
        m.update(host_mixer_inputs(cfg, f(mix_w_in)[0], f(mix_w_out)[0], f(diff_lambda)[0], f(diff_subln_w)[0], pos[b])
                 if b == 0 else {kk: in_maps[0][kk] for kk in ("win", "wgate", "wout", "lam", "subln", "invf")})
        m["pos"] = np.ascontiguousarray(pos[b].reshape(1, -1).astype(np.int32))
        m["x"] = np.ascontiguousarray(x[b])
        m["mem"] = np.ascontiguousarray(f(mem)[b])
        in_maps.append(m)
    res = run_bass_kernel_spmd(nc, in_maps, core_ids=list(range(B)))
    return np.stack([np.asarray(r["out"]) for r in res.results], axis=0).astype(np.float32)
```
